# Optimizing a Trainium2 kernel written in Bass

```python
import math
import jax, jax.numpy as jnp
from jax import lax
import numpy as np

D_MODEL = 1024
BATCH = 16
SEQ = 256
DEPTH = 2
DEC_BATCH = 4
DEC_SEQ = 2048
PAST_LEN = 512

GRID_W = 64
Q_BLOCK = 128
ROPE_THETA = 10000.0
EPS = 1e-6
FORGET_BIAS = 4.0

HD_A = 64
H_A = D_MODEL // (2 * HD_A)
KV_A = 2
DK_B = 64
DV_B = 64
H_B = D_MODEL // (4 * DV_B)
CHUNK_B = 64
D_C = 32
DV_C = 2 * D_C
H_C = D_MODEL // (4 * DV_C)

D_MIX = H_A * HD_A + H_B * DV_B + H_C * DV_C
D_FF = 2816
N_A = (H_A + 2 * KV_A) * HD_A
N_B = 2 * H_B * DK_B + 2 * H_B * DV_B + 4 * H_B
N_C = 4 * H_C * D_C + H_C * DV_C
N_IN = N_A + N_B + N_C

kernel_name = "hybrid_diffusion_prefix_step"


def rmsnorm(x, g):
    xf = x.astype(jnp.float32)
    y = xf * lax.rsqrt(jnp.mean(xf * xf, axis=-1, keepdims=True) + EPS)
    return (y * g.astype(jnp.float32)).astype(x.dtype)


def swiglu(x, w_in, w_out):
    g, u = jnp.split(x @ w_in, 2, axis=-1)
    return (jax.nn.silu(g) * u) @ w_out


def axial_rope(length, dim):
    rows = length // GRID_W
    t = jnp.arange(rows * GRID_W)
    row = (t // GRID_W).astype(jnp.float32)
    col = (t % GRID_W).astype(jnp.float32)
    axis_dim = dim // 2
    freqs = ROPE_THETA ** (-jnp.arange(0, axis_dim, 2, dtype=jnp.float32) / axis_dim)
    ang = jnp.concatenate([row[:, None] * freqs, col[:, None] * freqs], axis=-1)
    return jnp.cos(ang), jnp.sin(ang)


def apply_rope(x, cos, sin):
    half = x.shape[-1] // 2
    bshape = (cos.shape[0],) + (1,) * (x.ndim - 3) + (half,)
    cos = cos.reshape(bshape)
    sin = sin.reshape(bshape)
    xf = x.astype(jnp.float32)
    x1, x2 = xf[..., :half], xf[..., half:]
    return jnp.concatenate([x1 * cos - x2 * sin, x1 * sin + x2 * cos], axis=-1).astype(x.dtype)


def sweep_query_blocks(fn, q):
    B, Lq = q.shape[:2]
    nb = Lq // Q_BLOCK
    qb = jnp.moveaxis(q.reshape(B, nb, Q_BLOCK, *q.shape[2:]), 1, 0)
    out = jnp.moveaxis(lax.map(fn, qb), 0, 1)
    return out.reshape(B, Lq, *out.shape[3:])


def gqa_attention(q, k, v):
    scale = HD_A ** -0.5
    def block(qb):
        s = jnp.einsum("bqhgd,bkhd->bhgqk", qb, k).astype(jnp.float32) * scale
        p = jax.nn.softmax(s, axis=-1).astype(v.dtype)
        return jnp.einsum("bhgqk,bkhd->bqhgd", p, v)
    return sweep_query_blocks(block, q)


def diff_attention(q, k, v, lam):
    scale = D_C ** -0.5
    def block(qb):
        s = jnp.einsum("bqhnd,bkhnd->bhnqk", qb, k).astype(jnp.float32) * scale
        p = jax.nn.softmax(s, axis=-1)
        a = (p[:, :, 0] - lam * p[:, :, 1]).astype(v.dtype)
        return jnp.einsum("bhqk,bkhd->bqhd", a, v)
    return sweep_query_blocks(block, q)


def mlstm_scan(q, k, v, ig, logf, C0, n0, m0):
    f32 = jnp.float32
    B, L, H, _ = q.shape
    nc = L // CHUNK_B
    def to_chunks(x):
        x = x.astype(f32).reshape(B, nc, CHUNK_B, H, *x.shape[3:])
        return jnp.moveaxis(jnp.moveaxis(x, 1, 0), 3, 2)
    xs = (to_chunks(q), to_chunks(k * (DK_B ** -0.5)), to_chunks(v), to_chunks(ig), to_chunks(logf))
    lower = jnp.tril(jnp.ones((CHUNK_B, CHUNK_B), dtype=bool))

    def step(carry, xc):
        C, n, m = carry
        qj, kj, vj, ij, fj = xc
        b = jnp.cumsum(fj, axis=-1)
        dlog = jnp.where(lower, b[..., :, None] - b[..., None, :] + ij[..., None, :], -jnp.inf)
        inter = b + m[..., None]
        m_out = jnp.maximum(inter, jnp.max(dlog, axis=-1))
        s = jnp.einsum("bhtd,bhsd->bhts", qj, kj) * jnp.exp(dlog - m_out[..., None])
        w_inter = jnp.exp(inter - m_out)
        num = jnp.einsum("bhts,bhsv->bhtv", s, vj) + w_inter[..., None] * jnp.einsum("bhtd,bhdv->bhtv", qj, C)
        den = jnp.sum(s, axis=-1) + w_inter * jnp.einsum("bhtd,bhd->bht", qj, n)
        h = num / jnp.maximum(jnp.abs(den), jnp.exp(-m_out))[..., None]
        b_last = b[..., -1]
        g = b_last[..., None] - b + ij
        m_new = jnp.maximum(b_last + m, jnp.max(g, axis=-1))
        wk = jnp.exp(g - m_new[..., None])
        decay = jnp.exp(b_last + m - m_new)
        C_new = decay[..., None, None] * C + jnp.einsum("bhs,bhsd,bhsv->bhdv", wk, kj, vj)
        n_new = decay[..., None] * n + jnp.einsum("bhs,bhsd->bhd", wk, kj)
        return (C_new, n_new, m_new), h

    (C, n, m), hs = lax.scan(step, (C0.astype(f32), n0.astype(f32), m0.astype(f32)), xs)
    h = jnp.moveaxis(hs, 0, 2).reshape(B, H, L, hs.shape[-1])
    return jnp.moveaxis(h, 1, 2), (C, n, m)


def mlstm_bidir(q, k, v, gates, state0):
    C0, n0, m0 = state0
    ig = gates[:, :, 0::2]
    logf = jax.nn.log_sigmoid(gates[:, :, 1::2])
    h_f, (Cf, nf, mf) = mlstm_scan(q, k, v, ig[:, :, 0], logf[:, :, 0], C0[:, 0], n0[:, 0], m0[:, 0])
    rev = lambda x: jnp.flip(x, axis=1)
    h_r, (Cr, nr, mr) = mlstm_scan(rev(q), rev(k), rev(v), rev(ig[:, :, 1]), rev(logf[:, :, 1]),
                                   C0[:, 1], n0[:, 1], m0[:, 1])
    h = (h_f + rev(h_r)).astype(q.dtype)
    return h, (jnp.stack([Cf, Cr], axis=1), jnp.stack([nf, nr], axis=1), jnp.stack([mf, mr], axis=1))


def mix(u, p, lam_init, ctx):
    f32 = jnp.float32
    B, L, _ = u.shape
    z = u @ p["w_in"]
    za, zb, zc = jnp.split(z, [N_A, N_A + N_B], axis=-1)
    qa, ka, va = jnp.split(za, [H_A * HD_A, (H_A + KV_A) * HD_A], axis=-1)
    qa = rmsnorm(qa.reshape(B, L, H_A, HD_A), p["g_qa"])
    ka = rmsnorm(ka.reshape(B, L, KV_A, HD_A), p["g_ka"])
    va = va.reshape(B, L, KV_A, HD_A)
    qb, kb, vb, ob, gb = jnp.split(
        zb, [H_B * DK_B, 2 * H_B * DK_B, 2 * H_B * DK_B + H_B * DV_B, 2 * H_B * DK_B + 2 * H_B * DV_B], axis=-1)
    qb = qb.reshape(B, L, H_B, DK_B)
    kb = kb.reshape(B, L, H_B, DK_B)
    vb = vb.reshape(B, L, H_B, DV_B)
    gates = (gb.astype(f32) + p["b_gates"].astype(f32)).reshape(B, L, 4, H_B)
    qc, kc, vc = jnp.split(zc, [2 * H_C * D_C, 4 * H_C * D_C], axis=-1)
    qc = qc.reshape(B, L, H_C, 2, D_C)
    kc = kc.reshape(B, L, H_C, 2, D_C)
    vc = vc.reshape(B, L, H_C, DV_C)

    if ctx is None:
        ka_all, va_all, kc_all, vc_all = ka, va, kc, vc
        state0 = (jnp.zeros((B, 2, H_B, DK_B, DV_B), f32), jnp.zeros((B, 2, H_B, DK_B), f32),
                  jnp.zeros((B, 2, H_B), f32))
    else:
        ctx_ka, ctx_va, ctx_kc, ctx_vc, state0 = ctx
        cos_a, sin_a = axial_rope(L, HD_A)
        cos_c, sin_c = axial_rope(L, D_C)
        qa = apply_rope(qa, cos_a, sin_a)
        qc = apply_rope(qc, cos_c, sin_c)
        ka_all = jnp.concatenate([ctx_ka.astype(ka.dtype), apply_rope(ka, cos_a, sin_a)], axis=1)
        va_all = jnp.concatenate([ctx_va.astype(va.dtype), va], axis=1)
        kc_all = jnp.concatenate([ctx_kc.astype(kc.dtype), apply_rope(kc, cos_c, sin_c)], axis=1)
        vc_all = jnp.concatenate([ctx_vc.astype(vc.dtype), vc], axis=1)

    y_a = gqa_attention(qa.reshape(B, L, KV_A, H_A // KV_A, HD_A), ka_all, va_all).reshape(B, L, H_A * HD_A)
    lam = (jnp.exp(jnp.sum(p["lam_q1"].astype(f32) * p["lam_k1"].astype(f32)))
           - jnp.exp(jnp.sum(p["lam_q2"].astype(f32) * p["lam_k2"].astype(f32))) + lam_init)
    y_c = diff_attention(qc, kc_all, vc_all, lam)
    y_c = (rmsnorm(y_c, p["g_c"]) * (1.0 - lam_init)).reshape(B, L, H_C * DV_C)
    h_b, state = mlstm_bidir(qb, kb, vb, gates, state0)
    y_b = rmsnorm(h_b, p["g_b"]).reshape(B, L, H_B * DV_B) * jax.nn.sigmoid(ob)
    y = jnp.concatenate([y_a, y_b.astype(u.dtype), y_c], axis=-1) @ p["w_out"]
    if ctx is None:
        return y, (ka, va, kc, vc) + state
    return y, None


def layer(h, mod, p, lam_init, ctx):
    sh1, sc1, gt1, sh2, sc2, gt2, sh3, sc3, gt3 = jnp.split(mod[:, None, :], 9, axis=-1)
    u = rmsnorm(h, p["g_norm"][0]) * (1.0 + sc1) + sh1
    h = h + 0.5 * gt1 * swiglu(u, p["w_ff_in"][0], p["w_ff_out"][0])
    u = rmsnorm(h, p["g_norm"][1]) * (1.0 + sc2) + sh2
    y, cache = mix(u, p, lam_init, ctx)
    h = h + gt2 * y
    u = rmsnorm(h, p["g_norm"][2]) * (1.0 + sc3) + sh3
    h = h + 0.5 * gt3 * swiglu(u, p["w_ff_in"][1], p["w_ff_out"][1])
    return h, cache


def setup_inputs(seed: int = 0) -> dict:
    key = jax.random.key(seed)
    ks = jax.random.split(key, 32)
    f32 = jnp.float32
    nrm = lambda k, shape, s=1.0: s * jax.random.normal(k, shape, f32)
    gate_offset = jnp.repeat(jnp.array([0.0, FORGET_BIAS, 0.0, FORGET_BIAS], f32), H_B)
    return {
        "x_prompt": nrm(ks[0], (BATCH, SEQ, D_MODEL)),
        "x_sample": nrm(ks[1], (DEC_BATCH, DEC_SEQ, D_MODEL)),
        "cache_a_k": nrm(ks[2], (DEC_BATCH, DEPTH, PAST_LEN, KV_A, HD_A)),
        "cache_a_v": nrm(ks[3], (DEC_BATCH, DEPTH, PAST_LEN, KV_A, HD_A)),
        "cache_c_k": nrm(ks[4], (DEC_BATCH, DEPTH, PAST_LEN, H_C, 2, D_C)),
        "cache_c_v": nrm(ks[5], (DEC_BATCH, DEPTH, PAST_LEN, H_C, DV_C)),
        "state_b_C": nrm(ks[6], (DEC_BATCH, DEPTH, 2, H_B, DK_B, DV_B), 0.1),
        "state_b_n": nrm(ks[7], (DEC_BATCH, DEPTH, 2, H_B, DK_B), 0.1),
        "state_b_m": nrm(ks[8], (DEC_BATCH, DEPTH, 2, H_B), 0.5),
        "c": nrm(ks[9], (DEC_BATCH, D_MODEL)),
        "c_ctx": nrm(ks[10], (D_MODEL,)),
        "w_ada": nrm(ks[11], (DEPTH, D_MODEL, 9 * D_MODEL), 0.5 * D_MODEL ** -0.5),
        "b_ada": nrm(ks[12], (DEPTH, 9 * D_MODEL), 0.01),
        "g_norm": 1.0 + nrm(ks[13], (DEPTH, 3, D_MODEL), 0.02),
        "w_ff_in": nrm(ks[14], (DEPTH, 2, D_MODEL, 2 * D_FF), D_MODEL ** -0.5),
        "w_ff_out": nrm(ks[15], (DEPTH, 2, D_FF, D_MODEL), D_FF ** -0.5),
        "w_in": nrm(ks[16], (DEPTH, D_MODEL, N_IN), D_MODEL ** -0.5),
        "w_out": nrm(ks[17], (DEPTH, D_MIX, D_MODEL), D_MIX ** -0.5),
        "g_qa": 1.0 + nrm(ks[18], (DEPTH, HD_A), 0.02),
        "g_ka": 1.0 + nrm(ks[19], (DEPTH, HD_A), 0.02),
        "b_gates": gate_offset + nrm(ks[20], (DEPTH, 4 * H_B), 0.1),
        "g_b": 1.0 + nrm(ks[21], (DEPTH, DV_B), 0.02),
        "lam_q1": nrm(ks[22], (DEPTH, D_C), 0.1),
        "lam_k1": nrm(ks[23], (DEPTH, D_C), 0.1),
        "lam_q2": nrm(ks[24], (DEPTH, D_C), 0.1),
        "lam_k2": nrm(ks[25], (DEPTH, D_C), 0.1),
        "g_c": 1.0 + nrm(ks[26], (DEPTH, DV_C), 0.02),
        "g_final": 1.0 + nrm(ks[27], (D_MODEL,), 0.02),
    }


def reference(x_prompt, x_sample, cache_a_k, cache_a_v, cache_c_k, cache_c_v, state_b_C, state_b_n,
              state_b_m, c, c_ctx, w_ada, b_ada, g_norm, w_ff_in, w_ff_out, w_in, w_out, g_qa, g_ka,
              b_gates, g_b, lam_q1, lam_k1, lam_q2, lam_k2, g_c, g_final):
    hp = x_prompt
    hs = x_sample
    caches = []
    for l in range(DEPTH):
        p = {"g_norm": g_norm[l], "w_ff_in": w_ff_in[l], "w_ff_out": w_ff_out[l], "w_in": w_in[l],
             "w_out": w_out[l], "g_qa": g_qa[l], "g_ka": g_ka[l], "b_gates": b_gates[l], "g_b": g_b[l],
             "lam_q1": lam_q1[l], "lam_k1": lam_k1[l], "lam_q2": lam_q2[l], "lam_k2": lam_k2[l],
             "g_c": g_c[l]}
        lam_init = 0.8 - 0.6 * math.exp(-0.3 * l)
        mod_ctx = jax.nn.silu(c_ctx)[None, :] @ w_ada[l] + b_ada[l]
        mod_lat = jax.nn.silu(c) @ w_ada[l] + b_ada[l]
        hp, cache_l = layer(hp, mod_ctx, p, lam_init, None)
        caches.append(cache_l)
        ctx_l = (cache_a_k[:, l], cache_a_v[:, l], cache_c_k[:, l], cache_c_v[:, l],
                 (state_b_C[:, l], state_b_n[:, l], state_b_m[:, l]))
        hs, _ = layer(hs, mod_lat, p, lam_init, ctx_l)
    y_prompt = rmsnorm(hp, g_final)
    y_sample = rmsnorm(hs, g_final)
    new_a_k = jnp.stack([cl[0] for cl in caches], axis=1)
    new_a_v = jnp.stack([cl[1] for cl in caches], axis=1)
    new_c_k = jnp.stack([cl[2] for cl in caches], axis=1)
    new_c_v = jnp.stack([cl[3] for cl in caches], axis=1)
    new_b_C = jnp.stack([cl[4] for cl in caches], axis=1)
    new_b_n = jnp.stack([cl[5] for cl in caches], axis=1)
    new_b_m = jnp.stack([cl[6] for cl in caches], axis=1)
    return (y_prompt, y_sample, new_a_k, new_a_v, new_c_k, new_c_v, new_b_C, new_b_n, new_b_m)
```

```python
import math
from contextlib import ExitStack
import numpy as np
import ml_dtypes
import concourse.bass as bass
import concourse.mybir as mybir
from concourse.bass_utils import run_bass_kernel_spmd

F32 = mybir.dt.float32
BF16 = mybir.dt.bfloat16
AF = mybir.ActivationFunctionType
ALU = mybir.AluOpType

D = 1024
DFF = 2816
NT = 2560
LS = 2048
LP = 256
PAST = 512
EPS = 1e-6
NCORES = 8
FLAGS = {"A": True, "B": True, "C": True, "FFN": True, "LAYERS": 2}
SAMPLE_OF_CORE = [0, 1, None, None, 2, 3, None, None]

QA0, KA0, VA0 = 0, 512, 640
QB0, KB0, VB0, OB0, GB0 = 768, 1024, 1280, 1536, 1792
QC0, KC0, VC0 = 1808, 2320, 2576
NW = 2832


class _Op:
    __slots__ = ("eng", "fn", "deps", "sig", "val", "dma", "dsem", "dval", "waits", "ph")

    def __init__(self, eng, fn, deps, dma):
        self.eng = eng
        self.fn = fn
        self.deps = deps
        self.sig = False
        self.val = 0
        self.dma = dma
        self.dsem = None
        self.dval = 0
        self.waits = None


class Prog:
    ENGS = ("pe", "act", "dve", "pool", "sp")
    NDMA = 24

    def __init__(self):
        self.ops = []
        self.track = {}
        self.dma_rr = 0
        self.dma_rr2 = [0, 0]
        self.phase = "init"
        self.dma_last = [None] * self.NDMA
        self.dma_cnt = [0] * self.NDMA
        self.bar = None

    def emit(self, eng, fn, reads=(), writes=(), dma=False):
        deps = set()
        for k in reads:
            t = self.track.get(k)
            if t is not None and t[0] is not None:
                deps.add(t[0])
        for k in writes:
            t = self.track.get(k)
            if t is not None:
                if t[0] is not None:
                    deps.add(t[0])
                deps.update(t[1].values())
                deps.update(t[2])
        if self.bar is not None:
            deps.add(self.bar)
        oid = len(self.ops)
        op = _Op(eng, fn, deps, dma)
        op.ph = self.phase
        if dma:
            half = self.NDMA // 2
            base = 0 if eng == "sp" else half
            r = self.dma_rr2[eng != "sp"]
            self.dma_rr2[eng != "sp"] = (r + 1) % half
            i = base + r
            if self.dma_last[i] is not None:
                deps.add(self.dma_last[i])
            self.dma_last[i] = oid
            self.dma_cnt[i] += 16
            op.dsem = i
            op.dval = self.dma_cnt[i]
        if eng == "pe" and not dma:
            op.deps = set(d for d in deps if not (self.ops[d].eng == "pe" and not self.ops[d].dma))
        self.ops.append(op)
        for k in reads:
            t = self.track.setdefault(k, [None, {}, []])
            if dma:
                t[2].append(oid)
            else:
                t[1][eng] = oid
        for k in writes:
            self.track[k] = [oid, {}, []]
        return oid

    def finalize(self):
        for op in self.ops:
            for d in op.deps:
                self.ops[d].sig = True
        cnt = {e: 0 for e in self.ENGS}
        for op in self.ops:
            if op.dma:
                continue
            if op.sig:
                cnt[op.eng] += 1
                op.val = cnt[op.eng]
        waited = {e: {} for e in self.ENGS}
        for op in self.ops:
            w = {}
            for d in op.deps:
                dop = self.ops[d]
                if dop.dma:
                    key = ("dma", dop.dsem)
                    v = dop.dval
                else:
                    key = ("eng", dop.eng)
                    v = dop.val
                if waited[op.eng].get(key, 0) >= v:
                    continue
                if w.get(key, 0) < v:
                    w[key] = v
            for key, v in w.items():
                waited[op.eng][key] = v
            op.waits = list(w.items())

    def replay(self, nc, esems, dsems):
        self.finalize()
        fw = [(i, self.dma_cnt[i]) for i in range(self.NDMA) if self.dma_cnt[i] > 0]
        handles = {"pe": "tensor", "act": "scalar", "dve": "vector", "pool": "gpsimd", "sp": "sync"}
        with nc.Block() as block:
            for en in self.ENGS:
                myops = [op for op in self.ops if op.eng == en]

                def body(e, myops=myops, en=en):
                    for op in myops:
                        for key, v in op.waits:
                            sem = dsems[key[1]] if key[0] == "dma" else esems[key[1]]
                            e.wait_ge(sem, v)
                        inst = op.fn(e)
                        if op.dma:
                            inst.then_inc(dsems[op.dsem], 16)
                        elif op.sig:
                            inst.then_inc(esems[en], 1)
                    if en == "sp":
                        for i, v in fw:
                            e.wait_ge(dsems[i], v)

                getattr(block, handles[en])(body)


def _rope_tables():
    def tab(dim):
        t = np.arange(LS)
        row = (t // 64).astype(np.float32)
        col = (t % 64).astype(np.float32)
        axis_dim = dim // 2
        freqs = (10000.0 ** (-np.arange(0, axis_dim, 2, dtype=np.float32) / axis_dim)).astype(np.float32)
        ang = np.concatenate([row[:, None] * freqs, col[:, None] * freqs], axis=-1)
        return np.cos(ang).astype(np.float32), np.sin(ang).astype(np.float32)
    cA, sA = tab(64)
    cC, sC = tab(32)
    cosA = np.zeros((128, LS), np.float32)
    sinA = np.zeros((128, LS), np.float32)
    cosC = np.ones((128, LS), np.float32)
    sinC = np.zeros((128, LS), np.float32)
    for p in range(128):
        d = p % 64
        i = d % 32
        cosA[p] = cA[:, i]
        sinA[p] = -sA[:, i] if d < 32 else sA[:, i]
        d2 = p % 32
        j = d2 % 16
        cosC[p] = cC[:, j]
        sinC[p] = -sC[:, j] if d2 < 16 else sC[:, j]
    return cosA, sinA, cosC, sinC


def _consts():
    c = {}
    c["ident"] = np.eye(128, dtype=np.float32)
    c["ones"] = np.ones((128, 128), np.float32)
    bd = np.zeros((128, 128), np.float32)
    bd[:64, :64] = 1.0
    bd[64:, 64:] = 1.0
    c["bd"] = bd
    pa = np.zeros((128, 128), np.float32)
    pc = np.zeros((128, 128), np.float32)
    for m in range(128):
        d = m % 64
        base = m - d
        pa[base + (d ^ 32), m] = 1.0
        pc[m ^ 16, m] = 1.0
    c["pswA"] = pa
    c["pswC"] = pc
    s = np.arange(128)[:, None]
    x = np.arange(896)[None, :]
    c["mkf"] = ((x - 384) >= s).astype(np.float32)
    c["mkb"] = ((x - 384) <= s).astype(np.float32)
    sel = np.zeros((2, 4, 128), np.float32)
    for m in range(2):
        sel[m, 2 * m + 1, :64] = 1.0
        sel[m, 2 * m, 64:] = 1.0
    c["selden"] = sel.transpose(1, 0, 2).reshape(4, 256).copy()
    sel2 = np.zeros((2, 4, 128), np.float32)
    for m in range(2):
        sel2[m, 2 * m, :64] = 1.0
        sel2[m, 2 * m + 1, 64:] = 1.0
    c["selnum"] = sel2.transpose(1, 0, 2).reshape(4, 256).copy()
    cosA, sinA, cosC, sinC = _rope_tables()
    c["cosA"], c["sinA"], c["cosC"], c["sinC"] = cosA, sinA, cosC, sinC
    return c


def _perm_w_in(w_in_l):
    W = np.zeros((D, NW), np.float32)
    for m in range(4):
        W[:, QA0 + m * 128: QA0 + m * 128 + 64] = w_in_l[:, m * 64:(m + 1) * 64]
        W[:, QA0 + m * 128 + 64: QA0 + (m + 1) * 128] = w_in_l[:, (4 + m) * 64:(5 + m) * 64]
    W[:, KA0:KA0 + 128] = w_in_l[:, 512:640]
    W[:, VA0:VA0 + 128] = w_in_l[:, 640:768]
    for m in range(2):
        for i in range(4):
            W[:, QB0 + m * 512 + i * 128: QB0 + m * 512 + (i + 1) * 128] = w_in_l[:, 768 + i * 256 + m * 128: 768 + i * 256 + (m + 1) * 128]
    W[:, GB0:GB0 + 16] = w_in_l[:, 1792:1808]
    for h in range(4):
        for n in range(2):
            src = (h * 2 + n) * 32
            hp, hl = h // 2, h % 2
            dst = QC0 + hp * 256 + n * 128 + (hl * 2 + n) * 32
            W[:, dst: dst + 32] = w_in_l[:, 1808 + src: 1808 + src + 32]
    W[:, KC0:KC0 + 256] = w_in_l[:, 2064:2320]
    W[:, VC0:VC0 + 256] = w_in_l[:, 2320:2576]
    return W


def _perm_w_out(w_out_l):
    idx = []
    for m in range(4):
        idx += list(range(m * 64, (m + 1) * 64)) + list(range((4 + m) * 64, (5 + m) * 64))
    idx += list(range(512, 1024))
    return np.ascontiguousarray(w_out_l[idx, :])


class Builder:
    def __init__(self):
        self.nc = bass.Bass("TRN2", target_bir_lowering=False)
        self.P = Prog()
        self.psrr = 0
        self.ptkeys = None

    def dram_in(self, name, shape, dt=F32):
        return self.nc.dram_tensor(name, list(shape), dt, kind="ExternalInput").ap()

    def dram_out(self, name, shape):
        return self.nc.dram_tensor(name, list(shape), F32, kind="ExternalOutput").ap()

    def E(self, eng, fn, reads=(), writes=()):
        return self.P.emit(eng, fn, reads, writes)

    def DMA(self, q, out, in_, reads=(), writes=()):
        return self.P.emit(q, lambda e, o=out, i=in_: e.dma_start(out=o, in_=i), reads, writes, dma=True)

    def MM(self, out, lhsT, rhs, start, stop, reads, writes):
        return self.P.emit("pe", lambda e, o=out, l=lhsT, r=rhs, s=start, t=stop: e.matmul(o, lhsT=l, rhs=r, start=s, stop=t), reads, writes)

    def TR(self, out, in_, ident, reads, writes):
        return self.P.emit("pe", lambda e, o=out, i=in_, d=ident: e.transpose(out=o, in_=i, identity=d), reads, writes)

    def ACT(self, out, in_, func, reads, writes, bias=None, scale=None):
        kw = {}
        if bias is not None:
            kw["bias"] = bias
        if scale is not None:
            kw["scale"] = scale
        return self.P.emit("act", lambda e, o=out, i=in_, f=func, kw=kw: e.activation(out=o, in_=i, func=f, **kw), reads, writes)

    def TT(self, out, in0, in1, op, reads, writes, eng="dve"):
        return self.P.emit(eng, lambda e, o=out, a=in0, b=in1, p=op: e.tensor_tensor(out=o, in0=a, in1=b, op=p), reads, writes)

    def TS(self, out, in0, s1, s2, op0, op1, reads, writes):
        if op1 is None:
            return self.P.emit("dve", lambda e, o=out, a=in0, x=s1, p=op0: e.tensor_scalar(out=o, in0=a, scalar1=x, scalar2=None, op0=p), reads, writes)
        return self.P.emit("dve", lambda e, o=out, a=in0, x=s1, y=s2, p=op0, q=op1: e.tensor_scalar(out=o, in0=a, scalar1=x, scalar2=y, op0=p, op1=q), reads, writes)

    def STT(self, out, in0, scalar, in1, op0, op1, reads, writes):
        return self.P.emit("dve", lambda e, o=out, a=in0, s=scalar, b=in1, p=op0, q=op1: e.scalar_tensor_tensor(out=o, in0=a, scalar=s, in1=b, op0=p, op1=q), reads, writes)

    def CP(self, out, in_, reads, writes, eng="dve"):
        return self.P.emit(eng, lambda e, o=out, i=in_: e.tensor_copy(out=o, in_=i), reads, writes)

    def RCP(self, out, in_, reads, writes):
        return self.P.emit("dve", lambda e, o=out, i=in_: e.reciprocal(out=o, in_=i), reads, writes)

    def RCPF(self, out, in_, reads, writes):
        return self.P.emit("dve", lambda e, o=out, i=in_: e.reciprocal_approx_fast(out=o, in_=i), reads, writes)

    def MEMSET(self, ap, val, writes, eng="dve"):
        return self.P.emit(eng, lambda e, a=ap, v=val: e.memset(a, v), (), writes)

    def SCAN(self, out, d0, d1, init, op0, op1, reads, writes):
        return self.P.emit("dve", lambda e, o=out, a=d0, b=d1, i=init, p=op0, q=op1: e.tensor_tensor_scan(out=o, data0=a, data1=b, initial=i, op0=p, op1=q), reads, writes)

    def ps(self, pool):
        lst = self.pspools[pool]
        i = self.psidx.get(pool, 0)
        self.psidx[pool] = (i + 1) % len(lst)
        b = lst[i]
        return self.psum[b], ("ps", b)

    def build(self):
        nc = self.nc
        di = self.dram_in
        self.xin = di("xin", [NT, D])
        self.cvec = di("cvec", [128, 8, 2])
        self.w_ada = di("w_ada", [2, D, 9 * D])
        self.b_ada = di("b_ada", [2, 128, 72])
        self.g_norm = di("g_norm", [128, 2 * 3 * 8])
        self.g_final = di("g_final", [128, 8])
        self.w_ff_in = di("w_ff_in", [2, 2, D, 2 * DFF])
        self.w_ff_out = di("w_ff_out", [2, 2, DFF, D])
        self.w_in = di("w_in", [2, D, NW])
        self.w_out = di("w_out", [2, D, D])
        self.gcols = di("gcols", [128, 2 * 4])
        self.bg = di("bg", [4, 2 * 4])
        self.lamv = di("lamv", [128, 2 * 4 * 32])
        self.cak = di("cak", [2, PAST, 128])
        self.cav = di("cav", [2, PAST, 128])
        self.cck = di("cck", [2, PAST, 256])
        self.ccv = di("ccv", [2, PAST, 256])
        self.sbC = di("sbC", [2, 2, 2, 128, 64])
        self.sbn = di("sbn", [2, 2, 2, 128, 1])
        self.sbm = di("sbm", [4, 2 * 2])
        cn = {}
        for k, shp in (("ident", [128, 128]), ("ones", [128, 128]), ("bd", [128, 128]), ("pswA", [128, 128]),
                       ("pswC", [128, 128]), ("mkf", [128, 896]), ("mkb", [128, 896]), ("selden", [4, 256]), ("selnum", [4, 256]),
                       ("cosA", [128, LS]), ("sinA", [128, LS]), ("cosC", [128, LS]), ("sinC", [128, LS])):
            cn[k] = di("c_" + k, shp)
        self.cn = cn
        do = self.dram_out
        self.o_y = do("o_y", [NT, D])
        self.o_ak = do("o_ak", [2, 2, LP, 128])
        self.o_av = do("o_av", [2, 2, LP, 128])
        self.o_ck = do("o_ck", [2, 2, LP, 256])
        self.o_cv = do("o_cv", [2, 2, LP, 256])
        self.o_bC = do("o_bC", [2, 2, 2, 4, 64, 64])
        self.o_bn = do("o_bn", [2, 2, 2, 4, 64])
        self.o_bm = do("o_bm", [8, 4])

        with ExitStack() as st:
            sb = lambda n, s, d: st.enter_context(nc.sbuf_tensor(n, s, d))
            self.h = sb("h", [128, 8, NT], F32)
            AW = 29696
            self.arena = sb("arena", [128, AW], F32)
            self.ident = sb("ident", [128, 128], F32)
            self.ones = sb("ones", [128, 128], F32)
            self.bd = sb("bd", [128, 128], F32)
            self.pswA = sb("pswA", [128, 128], BF16)
            self.pswC = sb("pswC", [128, 128], BF16)
            self.mkf = sb("mkf", [128, 896], BF16)
            self.mkb = sb("mkb", [128, 896], BF16)
            self.selden = sb("selden", [4, 256], F32)
            self.selnum_t = sb("selnum", [4, 256], F32)
            self.modc = sb("modc", [128, 2, 72, 2], F32)
            self.nsc = sb("nsc", [128, 2, 3, 8, 2], F32)
            self.gt = sb("gt", [128, 2, 3, 8, 2], F32)
            self.gn = sb("gn", [128, 48], F32)
            self.gfin = sb("gfin", [128, 8], F32)
            self.bada = sb("bada", [128, 2, 72], F32)
            self.gc = sb("gc", [128, 8], F32)
            self.bgs = sb("bgs", [4, 8], F32)
            self.nbg = sb("nbg", [4, 8], F32)
            self.lamc = sb("lamc", [128, 8], F32)
            self.sbms = sb("sbms", [4, 4], F32)
            self.cv = sb("cv", [128, 8, 2], F32)
            self.cvb = sb("cvb", [128, 8, 2], BF16)
            self.small = sb("small", [128, 64], F32)
            self.psum = [st.enter_context(nc.psum_tensor("ps%d" % i, [128, 512], F32)) for i in range(8)]
            self.pspools = {"a": [0, 1, 2], "b": [3, 4], "c": [5, 6, 7]}
            self.psidx = {}
            esems = {e: st.enter_context(nc.semaphore("s_" + e)) for e in Prog.ENGS}
            dsems = [st.enter_context(nc.semaphore("d%d" % i)) for i in range(Prog.NDMA)]
            self.body()
            self.P.replay(nc, esems, dsems)
        return nc

    def carve_reset(self):
        self.aoff = 0

    def carve(self, words, dt, shape=None):
        a = self.arena[:, self.aoff:self.aoff + words]
        self.aoff += words
        assert self.aoff <= 29696, self.aoff
        if dt is BF16:
            a = a.bitcast(BF16)
        return a

    def barrier(self):
        keys = list(self.P.track.keys())
        oid = self.P.emit("dve", lambda e, a=self.small[:, 63:64]: e.memset(a, 0.0), reads=(), writes=keys + ["__bar"])
        self.P.bar = oid

    def body(self):
        B = self
        h = self.h
        ld = [("ident", self.ident), ("ones", self.ones), ("bd", self.bd), ("selden", self.selden), ("selnum", self.selnum_t)]
        for k, t in ld:
            B.DMA("sp", t[:], self.cn[k], writes=[k])
        for k, t in (("pswA", self.pswA), ("pswC", self.pswC), ("mkf", self.mkf), ("mkb", self.mkb)):
            B.DMA("pool", t[:], self.cn[k], writes=[k + "_b"])
        B.DMA("sp", self.gn[:], self.g_norm, writes=["gn"])
        B.DMA("sp", self.gfin[:], self.g_final, writes=["gfin"])
        B.DMA("sp", self.bada[:], self.b_ada.rearrange("l p j -> p l j"), writes=["bada"])
        B.DMA("sp", self.gc[:], self.gcols, writes=["gc"])
        B.DMA("sp", self.bgs[:], self.bg, writes=["bgs"])
        self.carve_reset()
        self.aoff = 3072
        self.lamt = self.carve(256, F32)
        B.DMA("sp", self.lamt, self.lamv, writes=["lamt"])
        B.DMA("sp", self.sbms[:], self.sbm, writes=["sbms"])
        B.DMA("sp", self.cv[:], self.cvec, writes=["cv"])
        B.TS(self.nbg[:], self.bgs[:], -1.0, None, ALU.mult, None, ["bgs"], ["nbg"])
        sm = self.small
        for l in range(2):
            lam_init = 0.8 - 0.6 * math.exp(-0.3 * l)
            for j in range(2):
                a = self.lamt[:, (l * 4 + 2 * j) * 32:(l * 4 + 2 * j + 1) * 32]
                b = self.lamt[:, (l * 4 + 2 * j + 1) * 32:(l * 4 + 2 * j + 2) * 32]
                B.TT(sm[:, 0:32], a, b, ALU.mult, ["lamt"], ["sm"])
                B.E("dve", lambda e, o=sm[:, 32 + j:33 + j], i=sm[:, 0:32]: e.reduce_sum(out=o, in_=i, axis=mybir.AxisListType.X), ["sm"], ["sm"])
                B.ACT(sm[:, 34 + j:35 + j], sm[:, 32 + j:33 + j], AF.Exp, ["sm"], ["sm"])
            B.TT(sm[:, 36:37], sm[:, 34:35], sm[:, 35:36], ALU.subtract, ["sm"], ["sm"])
            B.TS(self.lamc[:, 4 * l:4 * l + 1], sm[:, 36:37], lam_init, None, ALU.add, None, ["sm"], ["lamc"])
            B.TS(self.lamc[:, 4 * l + 1:4 * l + 2], self.lamc[:, 4 * l:4 * l + 1], -1.0, None, ALU.mult, None, ["lamc"], ["lamc"])
            B.TS(self.lamc[:, 4 * l + 2:4 * l + 3], self.gc[:, 4 * l + 3:4 * l + 4], 1.0 - lam_init, None, ALU.mult, None, ["gc", "lamc"], ["lamc"])

        self.carve_reset()
        xt = [self.carve(1024, F32) for _ in range(3)]
        for t in range(NT // 128):
            xb = xt[t % 3]
            B.DMA("sp", xb, self.xin[t * 128:(t + 1) * 128, :], writes=[("xt", t % 3)])
            for half in range(2):
                pb, pk = B.ps("a")
                for c in range(4):
                    cc = half * 4 + c
                    B.TR(pb[:, c * 128:(c + 1) * 128], xb[:, cc * 128:(cc + 1) * 128], self.ident[:], [("xt", t % 3), "ident"], [pk])
                o = h[:, half * 4:half * 4 + 4, t * 128:(t + 1) * 128]
                i = pb[:].rearrange("p (c t) -> p c t", t=128)
                if half == 0:
                    B.CP(o, i, [pk], [("h", t // 2)])
                else:
                    B.ACT(o, i, AF.Copy, [pk], [("h", t // 2)])

        self.modulation()
        for l in range(FLAGS["LAYERS"]):
            if FLAGS["FFN"]:
                self.ffn(l, 0)
            self.mix_layer(l)
            if FLAGS["FFN"]:
                self.ffn(l, 1)
        self.final_out()

    def hkeys(self, off, w):
        return [("h", b) for b in range(off // 256, (off + w) // 256)]

    def modulation(self):
        B = self
        self.P.phase = "mod"
        self.aoff = 4096
        wb = [self.carve(4096, BF16).rearrange("p (k n) -> p k n", n=1024) for _ in range(2)]
        B.ACT(self.cvb[:], self.cv[:], AF.Silu, ["cv"], ["cvb"])
        for l in range(2):
            wv = self.w_ada[l].rearrange("(k p) n -> p k n", p=128)
            pb, pk = B.ps("b")
            for blk in range(9):
                w = wb[blk % 2]
                key = ("wada", blk % 2)
                B.DMA("pool", w, wv[:, :, blk * 1024:(blk + 1) * 1024], writes=[key])
                for fc in range(8):
                    j = blk * 8 + fc
                    for k in range(8):
                        B.MM(pb[:, 2 * j:2 * j + 2], w[:, k, fc * 128:(fc + 1) * 128], self.cvb[:, k, :], k == 0, k == 7, [key, "cvb"], [pk])
            B.TT(self.modc[:, l], pb[:, 0:144].rearrange("p (j c) -> p j c", c=2),
                 self.bada[:, l].unsqueeze(2).to_broadcast([128, 72, 2]), ALU.add, [pk, "bada"], ["modc"])
            for i in range(3):
                sc = self.modc[:, l, (3 * i + 1) * 8:(3 * i + 2) * 8, :]
                B.TS(self.nsc[:, l, i], sc, 1.0, None, ALU.add, None, ["modc"], ["nsc"])
                B.TT(self.nsc[:, l, i], self.nsc[:, l, i],
                     self.gn[:, (l * 3 + i) * 8:(l * 3 + i + 1) * 8].unsqueeze(2).to_broadcast([128, 8, 2]), ALU.mult, ["nsc", "gn"], ["nsc"])
                g = self.modc[:, l, (3 * i + 2) * 8:(3 * i + 3) * 8, :]
                B.TS(self.gt[:, l, i], g, (1.0 if i == 1 else 0.5), None, ALU.mult, None, ["modc"], ["gt"])

    def norm_mod(self, off, w, scale_col, shift_col, u_out, ukeys, tmps):
        B = self
        sq, rs, tmp = tmps
        hk = self.hkeys(off, w)
        pb, pk = B.ps("b")
        for c in range(8):
            s = sq[c % 2]
            B.ACT(s[:, 0:w], self.h[:, c, off:off + w], AF.Square, hk, [("sq", c % 2)])
            B.MM(pb[:, 0:w], self.ones[:], s[:, 0:w], c == 0, c == 7, [("sq", c % 2), "ones"], [pk])
        B.ACT(rs[:, 0:w], pb[:, 0:w], AF.Ln, [pk, "epsc"], ["rs"], bias=self.epsc[:, 0:1], scale=1.0 / D)
        B.ACT(rs[:, 0:w], rs[:, 0:w], AF.Exp, ["rs"], ["rs"], scale=-0.5)
        for c in range(8):
            t = tmp[c % 2]
            B.STT(t[:, 0:w], self.h[:, c, off:off + w], scale_col(c), rs[:, 0:w], ALU.mult, ALU.mult, hk + ["rs", "nsc", "gfin"], [("tmp", c % 2)])
            if shift_col is None:
                if c % 2 == 0:
                    B.ACT(u_out(c), t[:, 0:w], AF.Copy, [("tmp", c % 2)], ukeys(c))
                else:
                    B.CP(u_out(c), t[:, 0:w], [("tmp", c % 2)], ukeys(c))
            else:
                if c % 2 == 0:
                    B.ACT(u_out(c), t[:, 0:w], AF.Identity, [("tmp", c % 2), "modc"], ukeys(c), bias=shift_col(c))
                else:
                    B.TS(u_out(c), t[:, 0:w], shift_col(c), None, ALU.add, None, [("tmp", c % 2), "modc"], ukeys(c))

    def mk_eps(self):
        if not hasattr(self, "epsc"):
            self.epsc = self.small[:, 40:41]
            self.MEMSET(self.small[:, 40:41], EPS, ["epsc"])

    def ffn(self, l, j):
        B = self
        self.P.phase = "ffn%d%d" % (l, j)
        ni = 0 if j == 0 else 2
        self.barrier()
        self.mk_eps()
        self.carve_reset()
        U = self.carve(10240, BF16).rearrange("p (k t) -> p k t", t=NT)
        HID = self.carve(7680, BF16).rearrange("p (f t) -> p f t", t=NT)
        W1 = [self.carve(2048, BF16).rearrange("p (k g n) -> p k g n", g=2, n=256) for _ in range(2)]
        W2 = self.carve(3072, BF16).rearrange("p (f n) -> p f n", n=D)
        sq = [self.carve(512, F32) for _ in range(2)]
        rs = self.carve(512, F32)
        tmp = [self.carve(512, F32) for _ in range(2)]
        sg = [self.carve(512, F32) for _ in range(2)]
        tiles = [(0, 512, 0), (512, 512, 0), (1024, 512, 0), (1536, 512, 0), (2048, 512, 1)]
        for (off, w, mc) in tiles:
            self.norm_mod(off, w,
                          lambda c, mc=mc: self.nsc[:, l, ni, c, mc:mc + 1],
                          lambda c, mc=mc: self.modc[:, l, (3 * ni) * 8 + c, mc:mc + 1],
                          lambda c, off=off, w=w: U[:, c, off:off + w],
                          lambda c, off=off: [("u", c, off // 512)], (sq, rs, tmp))
        w1v = self.w_ff_in[l, j].rearrange("(k p) n -> p k n", p=128)
        w2v = self.w_ff_out[l, j].rearrange("(f p) n -> p f n", p=128)
        passes = [(0, 6), (6, 6), (12, 6), (18, 4)]
        w1i = 0
        sgi = 0
        for (f0, nf) in passes:
            for fp in range(nf // 2):
                f = f0 + 2 * fp
                wt = W1[w1i % 2]
                wk = ("w1", w1i % 2)
                w1i += 1
                B.DMA("pool", wt[:, :, 0, :], w1v[:, :, f * 128:f * 128 + 256], writes=[wk + (0,)])
                B.DMA("pool", wt[:, :, 1, :], w1v[:, :, DFF + f * 128:DFF + f * 128 + 256], writes=[wk + (1,)])
                if fp == 0:
                    B.DMA("pool", W2[:, 0:nf, :], w2v[:, f0:f0 + nf, :], writes=["w2"])
                for sub in range(2):
                    fl = 2 * fp + sub
                    for ti, (off, w, mc) in enumerate(tiles):
                        pg, pgk = B.ps("a")
                        pu, puk = B.ps("c")
                        for k in range(8):
                            B.MM(pg[:, 0:w], wt[:, k, 0, sub * 128:(sub + 1) * 128], U[:, k, off:off + w], k == 0, k == 7, [wk + (0,), ("u", k, ti)], [pgk])
                        for k in range(8):
                            B.MM(pu[:, 0:w], wt[:, k, 1, sub * 128:(sub + 1) * 128], U[:, k, off:off + w], k == 0, k == 7, [wk + (1,), ("u", k, ti)], [puk])
                        s = sg[sgi % 2]
                        sk = ("sg", sgi % 2)
                        sgi += 1
                        B.ACT(s[:, 0:w], pg[:, 0:w], AF.Silu, [pgk], [sk])
                        B.TT(HID[:, fl, off:off + w], s[:, 0:w], pu[:, 0:w], ALU.mult, [sk, puk], [("hid", fl, ti)])
            for ti, (off, w, mc) in enumerate(tiles):
                hk = self.hkeys(off, w)
                for d in range(8):
                    pb, pk = B.ps("b")
                    for fl in range(nf):
                        B.MM(pb[:, 0:w], W2[:, fl, d * 128:(d + 1) * 128], HID[:, fl, off:off + w], fl == 0, fl == nf - 1, ["w2", ("hid", fl, ti)], [pk])
                    hv = self.h[:, d, off:off + w]
                    B.STT(hv, pb[:, 0:w], self.gt[:, l, ni, d, mc:mc + 1], hv, ALU.mult, ALU.add, [pk, "gt"] + hk, hk)

    def final_out(self):
        B = self
        self.P.phase = "final"
        self.barrier()
        self.mk_eps()
        self.carve_reset()
        sq = [self.carve(512, F32) for _ in range(2)]
        rs = self.carve(512, F32)
        tmp = [self.carve(512, F32) for _ in range(2)]
        yf = self.carve(4096, F32).rearrange("p (k t) -> p k t", t=512)
        ot = [self.carve(1024, F32) for _ in range(2)]
        oi = 0
        for ti in range(5):
            off = ti * 512
            self.norm_mod(off, 512, lambda c: self.gfin[:, c:c + 1], None,
                          lambda c: yf[:, c, :], lambda c: [("yf", c)], (sq, rs, tmp))
            for tt in range(4):
                o = ot[oi % 2]
                ok = ("ot", oi % 2)
                oi += 1
                for half in range(2):
                    pb, pk = B.ps("a")
                    for c in range(4):
                        cc = half * 4 + c
                        B.TR(pb[:, c * 128:(c + 1) * 128], yf[:, cc, tt * 128:(tt + 1) * 128], self.ident[:], [("yf", cc), "ident"], [pk])
                    if half == 0:
                        B.CP(o[:, 0:512], pb[:], [pk], [ok])
                    else:
                        B.ACT(o[:, 512:1024], pb[:], AF.Copy, [pk], [ok])
                r0 = off + tt * 128
                B.DMA("sp", self.o_y[r0:r0 + 128, :], o, reads=[ok])

    def mix_layer(self, l):
        B = self
        self.mk_eps()
        for (seq, off, L) in ((0, 0, LS), (1, LS, LP), (2, LS + LP, LP)):
            self.mix_seq(l, seq, off, L)

    def mix_seq(self, l, seq, off, L):
        B = self
        h = self.h
        sample = (seq == 0)
        mc = 0 if sample else 1
        TW = 512 if sample else 256
        ntile = L // TW
        nch = L // 128
        self.P.phase = "mix%d_s%d_norm" % (l, seq)
        self.barrier()
        self.carve_reset()
        U = self.carve(8 * L // 2, BF16).rearrange("p (k t) -> p k t", t=L)
        Y = self.carve(8 * L // 2, BF16).rearrange("p (k t) -> p k t", t=L)
        sq = [self.carve(512, F32) for _ in range(2)]
        rs = self.carve(512, F32)
        tmp = [self.carve(512, F32) for _ in range(2)]
        base_off = self.aoff
        for t in range(ntile):
            o = t * TW
            self.norm_mod(off + o, TW,
                          lambda c: self.nsc[:, l, 1, c, mc:mc + 1],
                          lambda c: self.modc[:, l, 24 + c, mc:mc + 1],
                          lambda c, o=o: U[:, c, o:o + TW],
                          lambda c, t=t: [("u", c, t)], (sq, rs, tmp))
        wv = self.w_in[l].rearrange("(k p) n -> p k n", p=128)
        Wall = None
        if not sample:
            Wall = self.carve(8 * NW // 2, BF16).rearrange("p (k n) -> p k n", n=NW)
            base_off = self.aoff
            if seq == 1:
                half = NW // 2
                B.DMA("pool", Wall[:, :, 0:half], wv[:, :, 0:half], writes=["wall"])
                B.DMA("pool", Wall[:, :, half:NW], wv[:, :, half:NW], writes=["wall"])
        ctx = dict(l=l, seq=seq, off=off, L=L, sample=sample, TW=TW, ntile=ntile, nch=nch, U=U, Y=Y, wv=wv,
                   sq=sq, rs=rs, tmp=tmp, Wall=Wall)
        for grp, fn in (("A", self.group_A), ("B", self.group_B), ("C", self.group_C)):
            self.aoff = base_off
            if FLAGS[grp]:
                self.P.phase = "mix%d_s%d_%s" % (l, seq, grp)
                self.barrier()
                fn(ctx)
            else:
                c0, c1 = {"A": (0, 4), "B": (4, 6), "C": (6, 8)}[grp]
                for c in range(c0, c1):
                    B.MEMSET(Y[:, c, :], 0.0, [("y", c)])
        self.P.phase = "mix%d_s%d_out" % (l, seq)
        self.barrier()
        self.aoff = base_off
        WO = self.carve(4096, BF16).rearrange("p (k n) -> p k n", n=D)
        B.DMA("pool", WO, self.w_out[l].rearrange("(k p) n -> p k n", p=128), writes=["wo"])
        for t in range(ntile):
            o = t * TW
            hk = self.hkeys(off + o, TW)
            for d in range(8):
                pb, pk = B.ps("b")
                for k in range(8):
                    B.MM(pb[:, 0:TW], WO[:, k, d * 128:(d + 1) * 128], Y[:, k, o:o + TW], k == 0, k == 7, ["wo", ("y", k)], [pk])
                hv = h[:, d, off + o:off + o + TW]
                B.STT(hv, pb[:, 0:TW], self.gt[:, l, 1, d, mc:mc + 1], hv, ALU.mult, ALU.add, [pk, "gt"] + hk, hk)

    def proj_fm(self, W, wkey, c0, ncol, U, o, TW, pool="a"):
        pb, pk = self.ps(pool)
        for k in range(8):
            self.MM(pb[0:ncol, 0:TW], W[:, k, c0:c0 + ncol], U[:, k, o:o + TW], k == 0, k == 7, [wkey] + [("u", k, o // TW)], [pk])
        return pb, pk

    def headnorm(self, src, srck, w, gcol, outs, ctx):
        B = self
        sq, rs = ctx["sq"], ctx["rs"]
        B.ACT(sq[0][:, 0:w], src, AF.Square, [srck], [("sq", 0)])
        pb, pk = B.ps("b")
        B.MM(pb[:, 0:w], self.bd[:], sq[0][:, 0:w], True, True, [("sq", 0), "bd"], [pk])
        B.ACT(rs[:, 0:w], pb[:, 0:w], AF.Ln, [pk, "epsc"], ["rs"], bias=self.epsc[:, 0:1], scale=1.0 / 64)
        B.ACT(rs[:, 0:w], rs[:, 0:w], AF.Exp, ["rs"], ["rs"], scale=-0.5)
        for ent in outs:
            o, ok = ent[0], ent[1]
            psl = ent[2] if len(ent) > 2 else slice(0, 128)
            B.STT(o, src[psl], gcol[psl], rs[psl, 0:w], ALU.mult, ALU.mult, [srck, "rs", "gc", "lamc"], ok)

    def rope(self, x, xk, w, psw, pswk, cos, sin, ck, out, outk, ctx):
        B = self
        tmp = ctx["tmp"]
        pb, pk = B.ps("b")
        B.MM(pb[:, 0:w], psw[:], x, True, True, [xk, pswk], [pk])
        cks = ck if isinstance(ck, list) else [ck]
        B.TT(tmp[0][:, 0:w], x, cos, ALU.mult, [xk] + cks, [("tmp", 0)])
        B.TT(tmp[1][:, 0:w], pb[:, 0:w], sin, ALU.mult, [pk] + cks, [("tmp", 1)])
        if isinstance(out, list):
            for (o, ok, psl) in out:
                B.TT(o, tmp[0][psl, 0:w], tmp[1][psl, 0:w], ALU.add, [("tmp", 0), ("tmp", 1)], ok)
        else:
            B.TT(out, tmp[0][:, 0:w], tmp[1][:, 0:w], ALU.add, [("tmp", 0), ("tmp", 1)], outk)

    def attn_scores_pv(self, qT, qk, half, KT, kkey, nk, vfun, vkey, TW, scale, PT, ctx, side=None, it0=0):
        B = self
        LAG = len(PT) - 1
        acc, acck = B.ps("c")
        pend = []
        for c in range(nk + LAG):
            while side and side[0][0] <= it0 + c:
                side.pop(0)[1]()
            if c < nk:
                sp_, spk = B.ps("a")
                B.MM(sp_[:, 0:TW], KT[:, c * 128:(c + 1) * 128], qT[:, 0:TW], True, True, [kkey, qk], [spk])
                pt = PT[c % len(PT)]
                ptk = self.ptkeys[c % len(PT)] if self.ptkeys else ("pt", c % len(PT))
                B.ACT(pt[:, 0:TW], sp_[:, 0:TW], AF.Exp, [spk], [ptk], scale=scale)
                pend.append((c, pt, ptk))
            if c >= LAG:
                cc, pt, ptk = pend.pop(0)
                B.MM(acc[:, 0:TW], vfun(cc), pt[:, 0:TW], cc == 0, cc == nk - 1, [vkey, ptk], [acck])
        return acc, acck

    def prep_stages(self, W, wkey, c0, U, o, TW, gcol, psw, pswk, cs, sn, outs, ctx, norm):
        B = self
        sq, rs, tmp = ctx["sq"], ctx["rs"], ctx["tmp"]
        xn = sq[1].bitcast(BF16)
        xnk = ("sq", 1)
        st = {}

        def s0():
            st["pb"], st["pk"] = self.proj_fm(W, wkey, c0, 128, U, o, TW, pool="b")

        def s1():
            B.ACT(sq[0][:, 0:TW], st["pb"][:, 0:TW], AF.Square, [st["pk"]], [("sq", 0)])

        def s2():
            st["pd"], st["pdk"] = B.ps("b")
            B.MM(st["pd"][:, 0:TW], self.bd[:], sq[0][:, 0:TW], True, True, [("sq", 0), "bd"], [st["pdk"]])

        def s3():
            B.ACT(rs[:, 0:TW], st["pd"][:, 0:TW], AF.Ln, [st["pdk"], "epsc"], ["rs"], bias=self.epsc[:, 0:1], scale=1.0 / 64)
            B.ACT(rs[:, 0:TW], rs[:, 0:TW], AF.Exp, ["rs"], ["rs"], scale=-0.5)

        def s4():
            if norm:
                B.STT(xn[:, 0:TW], st["pb"][:, 0:TW], gcol, rs[:, 0:TW], ALU.mult, ALU.mult, [st["pk"], "rs", "gc"], [xnk])
            else:
                B.CP(xn[:, 0:TW], st["pb"][:, 0:TW], [st["pk"]], [xnk])

        def s5():
            st["pr"], st["prk"] = B.ps("b")
            B.MM(st["pr"][:, 0:TW], psw[:], xn[:, 0:TW], True, True, [xnk, pswk], [st["prk"]])
            B.TT(tmp[0][:, 0:TW], xn[:, 0:TW], cs[:, 0:TW], ALU.mult, [xnk, "cs"], [("tmp", 0)])

        def s6():
            B.TT(tmp[1][:, 0:TW], st["pr"][:, 0:TW], sn[:, 0:TW], ALU.mult, [st["prk"], "cs2"], [("tmp", 1)])

        def s7():
            for (oo, ok, psl) in outs:
                B.TT(oo, tmp[0][psl, 0:TW], tmp[1][psl, 0:TW], ALU.add, [("tmp", 0), ("tmp", 1)], ok)

        if norm:
            return [s0, s1, s2, s3, s4, s5, s6, s7]
        return [s0, s4, s5, s6, s7]

    def group_A(self, ctx):
        B = self
        l, seq, off, L, sample, TW, ntile, nch, U, Y, wv = (ctx[k] for k in ("l", "seq", "off", "L", "sample", "TW", "ntile", "nch", "U", "Y", "wv"))
        npast = PAST if sample else 0
        nk = (npast + L) // 128
        wAk = "wall" if ctx["Wall"] is not None else "wA"
        if ctx["Wall"] is not None:
            W = ctx["Wall"][:, :, 0:768]
        else:
            W = self.carve(8 * 768 // 2, BF16).rearrange("p (k n) -> p k n", n=768)
            B.DMA("pool", W, wv[:, :, 0:768], writes=[wAk])
        KT = self.carve((npast + L) // 2, BF16)
        V = self.carve(nk * 2 * 128 // 2, BF16).rearrange("p (c g n) -> p c g n", g=2, n=128)
        QT = [[self.carve(256, BF16) for _ in range(2)] for _ in range(2)]
        PT = [self.carve(256, BF16) for _ in range(3)]
        xn = ctx["sq"][1].bitcast(BF16)
        xnk = ("sq", 1)
        rc = self.carve(512, F32)
        rc2 = rc
        cs = self.carve(512, F32)
        sn = self.carve(512, F32)
        stg = self.carve(512, F32)
        PT.append(stg.bitcast(BF16))
        self.ptkeys = [("pt", 0), ("pt", 1), ("pt", 2), "stg"]
        saved_pools = self.pspools
        self.pspools = {"a": [0, 1, 2, 3], "b": [4, 5], "c": [6, 7]}
        self.psidx = {}
        for i in range(2):
            B.MEMSET(QT[i][0][64:128, :], 0.0, [("qt", i, 0)])
            B.MEMSET(QT[i][1][0:64, :], 0.0, [("qt", i, 1)])
        B.MEMSET(V[:, :, 0, 64:128], 1.0, ["V"])
        B.MEMSET(V[:, :, 1, 0:64], 1.0, ["V"])
        vf = lambda c, g: V[:, c, g, :]
        if sample:
            for c in range(PAST // 128):
                B.DMA("sp", stg[:, 0:128], self.cak[l, c * 128:(c + 1) * 128, :], writes=["stg"])
                pb, pk = B.ps("b")
                B.TR(pb[:, 0:128], stg[:, 0:128], self.ident[:], ["stg", "ident"], [pk])
                B.CP(KT[:, c * 128:(c + 1) * 128], pb[:, 0:128], [pk], ["KT"])
                B.DMA("sp", stg[:, 128:256], self.cav[l, c * 128:(c + 1) * 128, :], writes=["stg"])
                B.CP(V[:, c, 0, 0:64], stg[:, 128:192], ["stg"], ["V"])
                B.CP(V[:, c, 1, 64:128], stg[:, 192:256], ["stg"], ["V"])
        gq = self.gc[:, 4 * l + 0:4 * l + 1]
        gk = self.gc[:, 4 * l + 1:4 * l + 2]
        for t in range(ntile):
            o = t * TW
            pb, pk = self.proj_fm(W, wAk, KA0, 128, U, o, TW)
            if sample:
                self.headnorm(pb[:, 0:TW], pk, TW, gk, [(xn[:, 0:TW], [xnk])], ctx)
                B.DMA("sp", cs[:, 0:TW], self.cn["cosA"][:, o:o + TW], writes=["cs"])
                B.DMA("sp", sn[:, 0:TW], self.cn["sinA"][:, o:o + TW], writes=["cs2"])
                self.rope(xn[:, 0:TW], xnk, TW, self.pswA, "pswA_b", cs[:, 0:TW], sn[:, 0:TW], ["cs", "cs2"], KT[:, npast + o:npast + o + TW], ["KT"], ctx)
            else:
                self.headnorm(pb[:, 0:TW], pk, TW, gk, [(KT[:, o:o + TW], ["KT"]), (stg[:, 0:TW], ["stg"])], ctx)
                for s in range(TW // 128):
                    p2, p2k = B.ps("b")
                    B.TR(p2[:, 0:128], stg[:, s * 128:(s + 1) * 128], self.ident[:], ["stg", "ident"], [p2k])
                    B.CP(rc[:, 0:128], p2[:, 0:128], [p2k], ["rc"])
                    B.DMA("sp", self.o_ak[seq - 1, l, o + s * 128:o + (s + 1) * 128, :], rc[:, 0:128], reads=["rc"])
        for c in range(nch):
            pb, pk = B.ps("a")
            for k in range(8):
                B.MM(pb[:, 0:128], U[:, k, c * 128:(c + 1) * 128], W[:, k, VA0:VA0 + 128], k == 0, k == 7, [wAk, ("u", k, (c * 128) // TW)], [pk])
            cc = npast // 128 + c
            B.CP(V[:, cc, 0, 0:64], pb[:, 0:64], [pk], ["V"])
            B.CP(V[:, cc, 1, 64:128], pb[:, 64:128], [pk], ["V"])
            if not sample:
                B.ACT(sn[:, 0:128], pb[:, 0:128], AF.Copy, [pk], ["cs2"])
                B.DMA("sp", self.o_av[seq - 1, l, c * 128:(c + 1) * 128, :], sn[:, 0:128], reads=["cs2"])
        if sample:
            jobs = [(t, m) for t in range(ntile) for m in range(4)]

            def stages_for(t, m):
                o = t * TW
                q = QT[m % 2]
                qouts = [(q[0][0:64, 0:TW], [("qt", m % 2, 0)], slice(0, 64)), (q[1][64:128, 0:TW], [("qt", m % 2, 1)], slice(64, 128))]
                stl = self.prep_stages(W, wAk, QA0 + m * 128, U, o, TW, gq, self.pswA, "pswA_b", cs, sn, qouts, ctx, True)
                if m == 0:
                    def ld(o=o):
                        B.DMA("sp", cs[:, 0:TW], self.cn["cosA"][:, o:o + TW], writes=["cs"])
                        B.DMA("sp", sn[:, 0:TW], self.cn["sinA"][:, o:o + TW], writes=["cs2"])
                    stl = [ld] + stl
                return stl
            for f in stages_for(0, 0):
                f()
            for ji, (t, m) in enumerate(jobs):
                o = t * TW
                q = QT[m % 2]
                side = []
                if ji + 1 < len(jobs):
                    stl = stages_for(*jobs[ji + 1])
                    step = max(1, (2 * nk - 6) // len(stl))
                    side = [[2 + i * step, f] for i, f in enumerate(stl)]
                for g in range(2):
                    acc, acck = self.attn_scores_pv(q[g], ("qt", m % 2, g), g, KT, "KT", nk, lambda c, g=g: vf(c, g), "V", TW, 0.125, PT, ctx, side=side, it0=g * nk)
                    self.softmax_norm(acc, acck, g, TW, Y[:, m, o:o + TW], [("y", m)], rc, rc2)
                while side:
                    side.pop(0)[1]()
        else:
            for t in range(ntile):
                o = t * TW
                for m in range(4):
                    q = QT[m % 2]
                    qouts = [(q[0][0:64, 0:TW], [("qt", m % 2, 0)], slice(0, 64)), (q[1][64:128, 0:TW], [("qt", m % 2, 1)], slice(64, 128))]
                    pb, pk = self.proj_fm(W, wAk, QA0 + m * 128, 128, U, o, TW)
                    self.headnorm(pb[:, 0:TW], pk, TW, gq, qouts, ctx)
                    for g in range(2):
                        acc, acck = self.attn_scores_pv(q[g], ("qt", m % 2, g), g, KT, "KT", nk, lambda c, g=g: vf(c, g), "V", TW, 0.125, PT, ctx)
                        self.softmax_norm(acc, acck, g, TW, Y[:, m, o:o + TW], [("y", m)], rc, rc2)
        self.pspools = saved_pools
        self.psidx = {}
        self.ptkeys = None

    def softmax_norm(self, acc, acck, nh, TW, yout, ykeys, rc, rc2):
        B = self
        dh = 1 - nh
        ds = slice(dh * 64, dh * 64 + 64)
        ns = slice(nh * 64, nh * 64 + 64)
        B.RCP(rc[ds, 0:TW], acc[ds, 0:TW], [acck], ["rc"])
        B.CP(rc[ns, 0:TW], rc[ds, 0:TW], ["rc"], ["rc"])
        B.TT(yout[ns, :], acc[ns, 0:TW], rc[ns, 0:TW], ALU.mult, [acck, "rc"], ykeys)

    def group_C(self, ctx):
        B = self
        l, seq, off, L, sample, TW, ntile, nch, U, Y, wv = (ctx[k] for k in ("l", "seq", "off", "L", "sample", "TW", "ntile", "nch", "U", "Y", "wv"))
        npast = PAST if sample else 0
        nk = (npast + L) // 128
        Wall = ctx["Wall"]
        wCk = "wall" if Wall is not None else "wC"
        W = None if Wall is not None else self.carve(8 * 512 // 2, BF16).rearrange("p (k n) -> p k n", n=512)
        KT = self.carve((npast + L) // 2, BF16)
        V = self.carve(nk * 2 * 128 // 2, BF16).rearrange("p (c g n) -> p c g n", g=2, n=128)
        QT = [[self.carve(256, BF16) for _ in range(2)] for _ in range(2)]
        PT = [self.carve(256, BF16) for _ in range(4)]
        saved_pools = self.pspools
        self.pspools = {"a": [0, 1, 2, 3], "b": [4, 5], "c": [6, 7]}
        self.psidx = {}
        xn = ctx["sq"][1].bitcast(BF16)
        xnk = ("sq", 1)
        rc = self.carve(512, F32)
        rc2 = rc
        for n in range(2):
            B.MEMSET(QT[0][n][64:128, :], 0.0, [("qtc", 0, n)])
            B.MEMSET(QT[1][n][0:64, :], 0.0, [("qtc", 1, n)])
        cs = self.carve(512, F32)
        sn = self.carve(512, F32)
        r0 = self.carve(512, F32)
        r1 = self.carve(512, F32)
        stg = r1
        scale = 32 ** -0.5
        nlam = self.lamc[:, 4 * l + 1:4 * l + 2]
        gcl = self.lamc[:, 4 * l + 2:4 * l + 3]
        B.MEMSET(V[:, :, 0, 64:128], 1.0, ["V"])
        B.MEMSET(V[:, :, 1, 0:64], 1.0, ["V"])
        for hp in range(2):
            if Wall is not None:
                W = Wall[:, :, KC0:KC0 + 512]
            else:
                B.DMA("pool", W, wv[:, :, KC0:KC0 + 512], writes=["wC"])
            if sample:
                for c in range(PAST // 128):
                    B.DMA("sp", stg[:, 0:128], self.cck[l, c * 128:(c + 1) * 128, hp * 128:(hp + 1) * 128], writes=["r1"])
                    pb, pk = B.ps("b")
                    B.TR(pb[:, 0:128], stg[:, 0:128], self.ident[:], ["r1", "ident"], [pk])
                    B.CP(KT[:, c * 128:(c + 1) * 128], pb[:, 0:128], [pk], ["KT"])
                    B.DMA("sp", rc[:, 0:128], self.ccv[l, c * 128:(c + 1) * 128, hp * 128:(hp + 1) * 128], writes=["rc"])
                    B.CP(V[:, c, 0, 0:64], rc[:, 0:64], ["rc"], ["V"])
                    B.CP(V[:, c, 1, 64:128], rc[:, 64:128], ["rc"], ["V"])
            for t in range(ntile):
                o = t * TW
                pb, pk = self.proj_fm(W, wCk, hp * 128, 128, U, o, TW)
                if sample:
                    B.DMA("sp", cs[:, 0:TW], self.cn["cosC"][:, o:o + TW], writes=["cs"])
                    B.DMA("sp", sn[:, 0:TW], self.cn["sinC"][:, o:o + TW], writes=["cs2"])
                    B.ACT(xn[:, 0:TW], pb[:, 0:TW], AF.Copy, [pk], [xnk])
                    self.rope(xn[:, 0:TW], xnk, TW, self.pswC, "pswC_b", cs[:, 0:TW], sn[:, 0:TW], ["cs", "cs2"], KT[:, npast + o:npast + o + TW], ["KT"], ctx)
                else:
                    B.ACT(KT[:, o:o + TW], pb[:, 0:TW], AF.Copy, [pk], ["KT"])
            for c in range(nch):
                pb, pk = B.ps("a")
                for k in range(8):
                    B.MM(pb[:, 0:128], U[:, k, c * 128:(c + 1) * 128], W[:, k, 256 + hp * 128:256 + (hp + 1) * 128], k == 0, k == 7, [wCk, ("u", k, (c * 128) // TW)], [pk])
                cc = npast // 128 + c
                B.CP(V[:, cc, 0, 0:64], pb[:, 0:64], [pk], ["V"])
                B.ACT(V[:, cc, 1, 64:128], pb[:, 64:128], AF.Copy, [pk], ["V"])
                if (not sample) and hp == 0:
                    pb2, pk2 = B.ps("a")
                    for k in range(8):
                        B.MM(pb2[:, 0:512], U[:, k, c * 128:(c + 1) * 128], W[:, k, 0:512], k == 0, k == 7, [wCk, ("u", k, (c * 128) // TW)], [pk2])
                    B.CP(stg[:, 0:512], pb2[:, 0:512], [pk2], ["r1"])
                    B.DMA("sp", self.o_ck[seq - 1, l, c * 128:(c + 1) * 128, :], stg[:, 0:256], reads=["r1"])
                    B.DMA("sp", self.o_cv[seq - 1, l, c * 128:(c + 1) * 128, :], stg[:, 256:512], reads=["r1"])
            if Wall is not None:
                W = Wall[:, :, QC0 + hp * 256:QC0 + (hp + 1) * 256]
            else:
                B.DMA("pool", W[:, :, 0:256], wv[:, :, QC0 + hp * 256:QC0 + (hp + 1) * 256], writes=["wC"])
            def c_stages(t, n):
                o = t * TW
                qouts = [(QT[0][n][0:64, 0:TW], [("qtc", 0, n)], slice(0, 64)), (QT[1][n][64:128, 0:TW], [("qtc", 1, n)], slice(64, 128))]
                stl = self.prep_stages(W, wCk, n * 128, U, o, TW, None, self.pswC, "pswC_b", cs, sn, qouts, ctx, False)
                if n == 0:
                    def ld(o=o):
                        B.DMA("sp", cs[:, 0:TW], self.cn["cosC"][:, o:o + TW], writes=["cs"])
                        B.DMA("sp", sn[:, 0:TW], self.cn["sinC"][:, o:o + TW], writes=["cs2"])
                    stl = [ld] + stl
                return stl
            if sample:
                for n in range(2):
                    for f in c_stages(0, n):
                        f()
            deferred = None
            for t in range(ntile):
                o = t * TW
                side = []
                last = []
                if deferred is not None:
                    side += [[1, deferred[0]], [5, deferred[1]], [9, deferred[2]]]
                    deferred = None
                if sample:
                    if t + 1 < ntile:
                        st0 = c_stages(t + 1, 0)
                        st1 = c_stages(t + 1, 1)
                        pos = 2 * nk + 2
                        for f in st0 + st1[:-1]:
                            side.append([pos, f])
                            pos += 3
                        last = [st1[-1]]
                else:
                    for n in range(2):
                        pb, pk = self.proj_fm(W, wCk, n * 128, 128, U, o, TW)
                        B.ACT(QT[0][n][0:64, 0:TW], pb[0:64, 0:TW], AF.Copy, [pk], [("qtc", 0, n)])
                        B.CP(QT[1][n][64:128, 0:TW], pb[64:128, 0:TW], [pk], [("qtc", 1, n)])
                for hi, (hl, n) in enumerate(((0, 0), (1, 0), (0, 1), (1, 1))):
                    acc, acck = self.attn_scores_pv(QT[hl][n], ("qtc", hl, n), hl, KT, "KT", nk, lambda c, hl=hl: V[:, c, hl, :], "V", TW, scale, PT, ctx, side=side, it0=hi * nk)
                    dst = r0 if n == 0 else r1
                    self.softmax_norm(acc, acck, hl, TW, dst[:, 0:TW], ["r%d" % n], rc, rc2)
                while side:
                    side.pop(0)[1]()
                for f in last:
                    f()
                def mk_fin(o=o):
                    sq0, rs_ = ctx["sq"][0], ctx["rs"]
                    st = {}

                    def f0():
                        B.STT(r0[:, 0:TW], r1[:, 0:TW], nlam, r0[:, 0:TW], ALU.mult, ALU.add, ["r0", "r1", "lamc"], ["r0"])
                        B.ACT(sq0[:, 0:TW], r0[:, 0:TW], AF.Square, ["r0"], [("sq", 0)])

                    def f1():
                        st["pd"], st["pdk"] = B.ps("b")
                        B.MM(st["pd"][:, 0:TW], self.bd[:], sq0[:, 0:TW], True, True, [("sq", 0), "bd"], [st["pdk"]])

                    def f2():
                        B.ACT(rs_[:, 0:TW], st["pd"][:, 0:TW], AF.Ln, [st["pdk"], "epsc"], ["rs"], bias=self.epsc[:, 0:1], scale=1.0 / 64)
                        B.ACT(rs_[:, 0:TW], rs_[:, 0:TW], AF.Exp, ["rs"], ["rs"], scale=-0.5)
                        B.STT(Y[:, 6 + hp, o:o + TW], r0[:, 0:TW], gcl, rs_[:, 0:TW], ALU.mult, ALU.mult, ["r0", "rs", "lamc"], [("y", 6 + hp)])
                    return [f0, f1, f2]
                fin = mk_fin()
                if sample and t + 1 < ntile:
                    deferred = fin
                else:
                    for f in fin:
                        f()
        self.pspools = saved_pools
        self.psidx = {}

    def group_B(self, ctx):
        B = self
        l, seq, off, L, sample, TW, ntile, nch, U, Y, wv = (ctx[k] for k in ("l", "seq", "off", "L", "sample", "TW", "ntile", "nch", "U", "Y", "wv"))
        sm = self.small
        nblk = TW // 128
        saved_pools = self.pspools
        self.pspools = {"a": [0, 1], "b": [2, 3], "c": [4, 5, 6, 7]}
        self.psidx = {}
        thrb = [self.carve(L // 2, BF16) for _ in range(2)]
        wcol = self.carve(nch * 8, F32).rearrange("p (c d h) -> p c d h", d=2, h=4)
        rsm = self.carve(64, F32)
        B.MEMSET(rsm, 0.0, ["rsm"])
        seldb = self.carve(128, BF16)
        B.DMA("pool", seldb[0:4, :], self.cn["selden"], writes=["seldb"])
        Wall = ctx["Wall"]
        wGk = "wall" if Wall is not None else "wG"
        wBk = "wall" if Wall is not None else "wB"
        p2off = self.aoff
        if Wall is not None:
            WG = Wall[:, :, GB0:GB0 + 16]
        else:
            WG = self.carve(64, BF16).rearrange("p (k n) -> p k n", n=16)
            B.DMA("pool", WG, wv[:, :, GB0:GB0 + 16], writes=["wG"])
        order = {0: list(range(ntile)), 1: list(range(ntile - 1, -1, -1))}
        ig = self.carve(L, F32)
        lp = self.carve(L, F32)
        th = self.carve(L, F32)
        B.MEMSET(sm[0:4, 48:49], 1.0, ["sm1"])
        for d in range(2):
            for t in range(ntile):
                o = t * TW
                pb, pk = self.proj_fm(WG, wGk, 8 * d, 4, U, o, TW)
                B.ACT(ig[0:4, o:o + TW], pb[0:4, 0:TW], AF.Identity, [pk, "bgs"], ["ig"], bias=self.bgs[:, 4 * l + 2 * d:4 * l + 2 * d + 1])
                pb, pk = self.proj_fm(WG, wGk, 8 * d + 4, 4, U, o, TW)
                B.ACT(lp[0:4, o:o + TW], pb[0:4, 0:TW], AF.Exp, [pk, "nbg"], ["lp"], bias=self.nbg[:, 4 * l + 2 * d + 1:4 * l + 2 * d + 2], scale=-1.0)
            B.ACT(lp[0:4, :], lp[0:4, :], AF.Ln, ["lp", "sm1"], ["lp"], bias=sm[0:4, 48:49])
            rv = (lambda a: a) if d == 0 else (lambda a: a[:, ::-1])
            onesb = sm[0:4, 48:49].to_broadcast([4, L])
            B.SCAN(rv(th[0:4, :]), onesb, rv(lp[0:4, :]), 0.0, ALU.mult, ALU.add, ["lp", "sm1"], ["th"])
            B.TT(ig[0:4, :], ig[0:4, :], th[0:4, :], ALU.add, ["ig", "th"], ["ig"])
            init = self.sbms[:, 2 * l + d:2 * l + d + 1] if sample else 0.0
            B.SCAN(rv(lp[0:4, :]), rv(ig[0:4, :]), rv(ig[0:4, :]), init, ALU.max, ALU.max, ["ig", "sbms"], ["lp"])
            if not sample:
                Rl = lp[0:4, L - 1:L] if d == 0 else lp[0:4, 0:1]
                nfl = th[0:4, L - 1:L] if d == 0 else th[0:4, 0:1]
                B.TT(sm[0:4, 56 + d:57 + d], Rl, nfl, ALU.subtract, ["lp", "th"], [("mfin", d)])
                col = (seq - 1) * 4 + l * 2 + d
                B.DMA("sp", self.o_bm[col].unsqueeze(1), sm[0:4, 56 + d:57 + d], reads=[("mfin", d)])
            for j in range(ntile):
                o = j * TW
                rb = 4 * (2 * j + d)
                Rcol = lp[0:4, o + TW - 1:o + TW] if d == 0 else lp[0:4, o:o + 1]
                B.TS(rsm[0:4, rb:rb + 1], Rcol, -1.0, None, ALU.mult, None, ["lp"], ["rsm"])
                B.TS(rsm[0:4, rb + 1:rb + 2], Rcol, -1.0, -math.log(8.0), ALU.mult, ALU.add, ["lp"], ["rsm"])
                B.ACT(thrb[d][0:4, o:o + TW], th[0:4, o:o + TW], AF.Exp, ["th", "rsm"], [("thrb", d)], bias=rsm[0:4, rb:rb + 1])
            if sample:
                j0 = order[d][0]
                rb0 = 4 * (2 * j0 + d)
                B.ACT(rsm[0:4, rb0 + 2:rb0 + 3], self.sbms[:, 2 * l + d:2 * l + d + 1], AF.Exp, ["sbms", "rsm"], ["rsm"], bias=rsm[0:4, rb0:rb0 + 1])
                for i in range(ntile - 1):
                    ja, jb = order[d][i], order[d][i + 1]
                    ra, rbb = 4 * (2 * ja + d), 4 * (2 * jb + d)
                    B.TT(rsm[0:4, ra + 3:ra + 4], rsm[0:4, rbb:rbb + 1], rsm[0:4, ra:ra + 1], ALU.subtract, ["rsm"], ["rsm"])
                    B.ACT(rsm[0:4, ra + 3:ra + 4], rsm[0:4, ra + 3:ra + 4], AF.Exp, ["rsm"], ["rsm"])
            for j in range(ntile):
                o = j * TW
                rb = 4 * (2 * j + d)
                B.ACT(th[0:4, o:o + TW], ig[0:4, o:o + TW], AF.Exp, ["ig", "rsm", ("thrb", d)], ["th"], bias=rsm[0:4, rb + 1:rb + 2])
                for c in range(o // 128, (o + TW) // 128):
                    pb, pk = B.ps("b")
                    B.TR(pb[:, 0:4], th[0:4, c * 128:(c + 1) * 128], self.ident[0:4, 0:4], ["th", "ident"], [pk])
                    B.CP(wcol[:, c, d, :], pb[:, 0:4], [pk], ["wcol"])
        self.barrier()
        self.aoff = p2off
        W = None if Wall is not None else self.carve(8 * 512 // 2, BF16).rearrange("p (k n) -> p k n", n=512)
        KT = self.carve(L // 2, BF16)
        V = self.carve(nch * 2 * 128 // 2, BF16).rearrange("p (c g n) -> p c g n", g=2, n=128)
        QT = [self.carve(256, BF16) for _ in range(2)]
        B.MEMSET(QT[0][64:128, :], 0.0, [("qtb", 0)])
        B.MEMSET(QT[1][0:64, :], 0.0, [("qtb", 1)])
        PT = [self.carve(256, BF16) for _ in range(2)]
        ptkeys = [("pt", 0), ("pt", 1)]
        thrt = self.carve(512, F32)
        thrt2 = self.carve(512, F32)
        thrtb = [(thrt, "thrt"), (thrt2, "thrt2")]
        hf, hb = ctx["tmp"][0], ctx["tmp"][1]
        hfk, hbk = ("tmp", 0), ("tmp", 1)
        nkt = nblk if sample else nch
        ktok_raw = self.carve(nkt * 128 // 2, BF16)
        ktok = ktok_raw.rearrange("p (c n) -> p c n", n=128)
        vw = [self.carve(64, BF16)] * 2
        if sample:
            c0s_raw = self.carve(2 * 128, F32)
            c0s = c0s_raw.rearrange("p (d n) -> p d n", d=2)
            PT += [ktok_raw, c0s_raw.bitcast(BF16)]
            ptkeys += ["ktok", "c0s"]
            Sst = self.carve(128, F32)
            Slh = self.carve(2 * ntile * 64, BF16).rearrange("p (d j n) -> p d j n", d=2, j=ntile)
        else:
            PT += [self.carve(256, BF16) for _ in range(2)]
            ptkeys += [("pt", 2), ("pt", 3)]
            vwb = self.carve(nch * 2 * 2 * 66 // 2, BF16).rearrange("p (c d h n) -> p c d h n", d=2, h=2, n=66)
        gb = self.gc[:, 4 * l + 2:4 * l + 3]
        B.MEMSET(V[:, :, 0, 64:128], 1.0, ["V"])
        B.MEMSET(V[:, :, 1, 0:64], 1.0, ["V"])
        for m in range(2):
            if Wall is not None:
                W = Wall[:, :, QB0 + m * 512:QB0 + (m + 1) * 512]
            else:
                B.DMA("pool", W, wv[:, :, QB0 + m * 512:QB0 + (m + 1) * 512], writes=["wB"])
            for t in range(ntile):
                o = t * TW
                pb, pk = self.proj_fm(W, wBk, 128, 128, U, o, TW)
                B.ACT(KT[:, o:o + TW], pb[:, 0:TW], AF.Copy, [pk], ["KTb"])
            for c in range(nch):
                pb, pk = B.ps("a")
                for k in range(8):
                    B.MM(pb[:, 0:256], U[:, k, c * 128:(c + 1) * 128], W[:, k, 128:384], k == 0, k == 7, [wBk, ("u", k, (c * 128) // TW)], [pk])
                B.CP(V[:, c, 0, 0:64], pb[:, 128:192], [pk], ["V"])
                B.ACT(V[:, c, 1, 64:128], pb[:, 192:256], AF.Copy, [pk], ["V"])
                if not sample:
                    B.ACT(ktok[:, c, :], pb[:, 0:128], AF.Copy, [pk], ["ktok"])
                    for d in range(2):
                        for hl in range(2):
                            hh = 2 * m + hl
                            B.TS(vwb[:, c, d, hl, 0:64], pb[:, 128 + hl * 64:192 + hl * 64], wcol[:, c, d, hh:hh + 1], None, ALU.mult, None, [pk, "wcol"], ["vwb"])
                            B.CP(vwb[:, c, d, hl, 64:65], wcol[:, c, d, hh:hh + 1], ["wcol"], ["vwb"])
            if not sample:
                for d in range(2):
                    for hl in range(2):
                        hh = 2 * m + hl
                        pb, pk = B.ps("b")
                        for c in range(nch):
                            B.MM(pb[0:64, 0:65], ktok[:, c, hl * 64:(hl + 1) * 64], vwb[:, c, d, hl, 0:65], c == 0, c == nch - 1, ["ktok", "vwb"], [pk])
                        B.CP(thrt[0:64, 0:65], pb[0:64, 0:65], [pk], ["thrt"])
                        B.DMA("sp", self.o_bC[seq - 1, l, d, hh], thrt[0:64, 0:64], reads=["thrt"])
                        B.DMA("sp", self.o_bn[seq - 1, l, d, hh].unsqueeze(1), thrt[0:64, 64:65], reads=["thrt"])
            else:
                for d in range(2):
                    B.DMA("sp", c0s[:, d, 0:64], self.sbC[l, d, m], writes=["c0s"])
                    B.DMA("sp", c0s[:, d, 64:65], self.sbn[l, d, m], writes=["c0s"])
                    B.TS(c0s[:, d, 65:128], c0s[:, d, 64:65].to_broadcast([128, 63]), 1.0, None, ALU.mult, None, ["c0s"], ["c0s"])
                    j0 = order[d][0]
                    rb0 = 4 * (2 * j0 + d)
                    pb, pk = B.ps("b")
                    B.MM(pb[:, 0:2], self.selnum(m), rsm[0:4, rb0 + 2:rb0 + 4], True, True, ["selnum", "rsm"], [pk])
                    B.CP(sm[:, 60:61], pb[:, 0:1], [pk], ["e0c"])
                    B.TS(Sst[0:64, :], c0s[0:64, d, :], sm[0:64, 60:61], None, ALU.mult, None, ["c0s", "e0c"], ["Sst"])
                    B.TS(Sst[64:128, 64:128], c0s[64:128, d, 0:64], sm[64:128, 60:61], None, ALU.mult, None, ["c0s", "e0c"], ["Sst"])
                    B.TS(Sst[64:128, 0:64], c0s[64:128, d, 64:128], sm[64:128, 60:61], None, ALU.mult, None, ["c0s", "e0c"], ["Sst"])
                    for i, j in enumerate(order[d]):
                        B.CP(Slh[:, d, j, :], Sst[:, :], ["Sst"], [("Slh", d, j)])
                        if i == ntile - 1:
                            break
                        rb = 4 * (2 * j + d)
                        for ci in range(nblk):
                            c = j * nblk + ci
                            pb, pk = B.ps("a")
                            for k in range(8):
                                B.MM(pb[:, 0:128], U[:, k, c * 128:(c + 1) * 128], W[:, k, 128:256], k == 0, k == 7, [wBk, ("u", k, j)], [pk])
                            B.ACT(ktok[:, ci, :], pb[:, 0:128], AF.Copy, [pk], ["ktok"])
                        pst, pstk = B.ps("c")
                        vi = 0
                        for hl in range(2):
                            hh = 2 * m + hl
                            for ci in range(nblk):
                                c = j * nblk + ci
                                v_ = vw[0]
                                vk = ("vw", 0)
                                vi += 1
                                B.TS(v_[:, :], V[:, c, hl, :], wcol[:, c, d, hh:hh + 1], None, ALU.mult, None, ["V", "wcol"], [vk])
                                B.MM(pst[:, hl * 128:(hl + 1) * 128], ktok[:, ci, :], v_[:, :], ci == 0, ci == nblk - 1, ["ktok", vk], [pstk])
                        pf, pfk = B.ps("b")
                        B.MM(pf[:, 0:2], self.selnum(m), rsm[0:4, rb + 2:rb + 4], True, True, ["selnum", "rsm"], [pfk])
                        B.CP(sm[:, 61:62], pf[:, 1:2], [pfk], ["facc"])
                        for hl in range(2):
                            ps_ = slice(hl * 64, hl * 64 + 64)
                            B.TT(Sst[ps_, :], Sst[ps_, :], pst[ps_, hl * 128:(hl + 1) * 128], ALU.add, ["Sst", pstk], ["Sst"])
                        B.TS(Sst[:, :], Sst[:, :], sm[:, 61:62], None, ALU.mult, None, ["Sst", "facc"], ["Sst"])
            for t in range(ntile):
                o = t * TW
                pb, pk = self.proj_fm(W, wBk, 0, 128, U, o, TW)
                B.ACT(QT[0][0:64, 0:TW], pb[0:64, 0:TW], AF.Copy, [pk], [("qtb", 0)])
                B.CP(QT[1][64:128, 0:TW], pb[64:128, 0:TW], [pk], [("qtb", 1)])
                chs = list(range(t * nblk, (t + 1) * nblk)) if sample else list(range(nch))
                accs = {}
                for hl in range(2):
                    hh = 2 * m + hl
                    accf, acfk = B.ps("c")
                    accb, acbk = B.ps("c")
                    accs[hl] = (accf, acfk, accb, acbk)
                    first = {0: True, 1: True}
                    if sample:
                        for d, (acc, ak) in enumerate(((accf, acfk), (accb, acbk))):
                            B.MM(acc[:, 0:TW], Slh[:, d, t, :], QT[hl][:, 0:TW], True, False, [("Slh", d, t), ("qtb", hl)], [ak])
                        first = {0: False, 1: False}
                    pti = 0
                    pend = []
                    for c in chs + [None]:
                        if c is not None:
                            sp_, spk = B.ps("a")
                            B.MM(sp_[:, 0:TW], KT[:, c * 128:(c + 1) * 128], QT[hl][:, 0:TW], True, True, ["KTb", ("qtb", hl)], [spk])
                            rel = c - t * nblk
                            for d, acc, ak, mk, mkk in ((0, accf, acfk, self.mkf, "mkf_b"), (1, accb, acbk, self.mkb, "mkb_b")):
                                pt = PT[pti % 4]
                                ptk = ptkeys[pti % 4]
                                pti += 1
                                wc = wcol[:, c, d, hh:hh + 1]
                                mo = 384 - 128 * rel
                                lo, hi = 0, TW
                                if sample:
                                    lo, hi = (128 * rel, TW) if d == 0 else (0, 128 * (rel + 1))
                                B.STT(pt[:, lo:hi], sp_[:, lo:hi], wc, mk[:, mo + lo:mo + hi], ALU.mult, ALU.mult, [spk, "wcol", mkk], [ptk])
                                pend.append((c, d, acc, ak, pt, ptk, lo, hi))
                        while pend and (c is None or pend[0][0] < c):
                            cc, d, acc, ak, pt, ptk, lo, hi = pend.pop(0)
                            B.MM(acc[:, lo:hi], V[:, cc, hl, :], pt[:, lo:hi], first[d], cc == chs[-1], ["V", ptk], [ak])
                            first[d] = False
                thrp = {}
                for d in range(2):
                    pbt, pbk = B.ps("b")
                    B.MM(pbt[:, 0:TW], seldb[0:4, m * 128:(m + 1) * 128], thrb[d][0:4, o:o + TW], True, True, ["seldb", ("thrb", d)], [pbk])
                    thrp[d] = (pbt, pbk)
                for hl in range(2):
                    hs = slice(hl * 64, hl * 64 + 64)
                    ds = slice((1 - hl) * 64, (1 - hl) * 64 + 64)
                    accf, acfk, accb, acbk = accs[hl]
                    chains = ((accf, acfk, hf, hfk), (accb, acbk, hb, hbk))
                    for d, (acc, ak, dst, dk) in enumerate(chains):
                        tb, tk = thrtb[d]
                        pbt, pbk = thrp[d]
                        B.ACT(tb[ds, 0:TW], pbt[ds, 0:TW], AF.Copy, [pbk], [tk])
                    for d, (acc, ak, dst, dk) in enumerate(chains):
                        tb, tk = thrtb[d]
                        B.TT(tb[ds, 0:TW], acc[ds, 0:TW], tb[ds, 0:TW], ALU.max, [ak, tk], [tk])
                        B.STT(tb[ds, 0:TW], acc[ds, 0:TW], -1.0, tb[ds, 0:TW], ALU.mult, ALU.max, [ak, tk], [tk])
                    for d, (acc, ak, dst, dk) in enumerate(chains):
                        tb, tk = thrtb[d]
                        B.ACT(tb[ds, 0:TW], tb[ds, 0:TW], AF.Ln, [tk], [tk])
                        B.ACT(tb[ds, 0:TW], tb[ds, 0:TW], AF.Exp, [tk], [tk], scale=-1.0)
                    for d, (acc, ak, dst, dk) in enumerate(chains):
                        tb, tk = thrtb[d]
                        B.CP(tb[hs, 0:TW], tb[ds, 0:TW], [tk], [tk])
                        B.TT(dst[hs, 0:TW], acc[hs, 0:TW], tb[hs, 0:TW], ALU.mult, [ak, tk], [dk])
                    B.TT(hf[hs, 0:TW], hf[hs, 0:TW], hb[hs, 0:TW], ALU.add, [hfk, hbk], [hfk])
                self.headnorm(hf[:, 0:TW], hfk, TW, gb, [(hb[:, 0:TW], [hbk])], ctx)
                pb, pk = self.proj_fm(W, wBk, 384, 128, U, o, TW)
                B.ACT(thrt[:, 0:TW], pb[:, 0:TW], AF.Sigmoid, [pk], ["thrt"])
                B.TT(Y[:, 4 + m, o:o + TW], hb[:, 0:TW], thrt[:, 0:TW], ALU.mult, [hbk, "thrt"], [("y", 4 + m)])
        self.pspools = saved_pools
        self.psidx = {}

    def selnum(self, m):
        return self.selnum_t[:, m * 128:(m + 1) * 128]


_CACHE = {}


def _get_nc():
    if "nc" not in _CACHE:
        b = Builder()
        _CACHE["nc"] = b.build()
    return _CACHE["nc"]


def kernel(x_prompt, x_sample, cache_a_k, cache_a_v, cache_c_k, cache_c_v, state_b_C, state_b_n,
           state_b_m, c, c_ctx, w_ada, b_ada, g_norm, w_ff_in, w_ff_out, w_in, w_out, g_qa, g_ka,
           b_gates, g_b, lam_q1, lam_k1, lam_q2, lam_k2, g_c, g_final):
    f = lambda a: np.ascontiguousarray(np.asarray(a, dtype=np.float32))
    x_prompt, x_sample = f(x_prompt), f(x_sample)
    consts = _consts()
    shared = {}
    shared["w_ada"] = f(w_ada)
    shared["b_ada"] = f(np.asarray(b_ada).reshape(2, 72, 128).transpose(0, 2, 1))
    shared["g_norm"] = f(np.asarray(g_norm).reshape(6, 8, 128).transpose(2, 0, 1).reshape(128, 48))
    shared["g_final"] = f(np.asarray(g_final).reshape(8, 128).T)
    shared["w_ff_in"] = f(w_ff_in)
    shared["w_ff_out"] = f(w_ff_out)
    shared["w_in"] = f(np.stack([_perm_w_in(np.asarray(w_in[l])) for l in range(2)]))
    shared["w_out"] = f(np.stack([_perm_w_out(np.asarray(w_out[l])) for l in range(2)]))
    gcols = np.zeros((128, 8), np.float32)
    for l in range(2):
        for i, g in enumerate((g_qa, g_ka, g_b, g_c)):
            gcols[:, 4 * l + i] = np.tile(np.asarray(g[l]), 2)
    shared["gcols"] = gcols
    bgv = np.zeros((4, 8), np.float32)
    for l in range(2):
        bgv[:, 4 * l:4 * l + 4] = np.asarray(b_gates[l]).reshape(4, 4).T
    shared["bg"] = bgv
    lamv = np.zeros((128, 256), np.float32)
    for l in range(2):
        for i, v in enumerate((lam_q1, lam_k1, lam_q2, lam_k2)):
            lamv[:, (l * 4 + i) * 32:(l * 4 + i + 1) * 32] = np.asarray(v[l])[None, :]
    shared["lamv"] = lamv
    for k, v in consts.items():
        shared["c_" + k] = f(v)
    in_maps = []
    for core in range(NCORES):
        b = SAMPLE_OF_CORE[core]
        real = b is not None
        b = 0 if b is None else b
        zl = (lambda a: np.zeros_like(a)) if not real else (lambda a: a)
        m = dict(shared)
        m["xin"] = f(np.concatenate([zl(x_sample[b]), x_prompt[2 * core], x_prompt[2 * core + 1]], axis=0))
        cv = np.stack([np.asarray(c[b]), np.asarray(c_ctx)], axis=-1)
        m["cvec"] = f(cv.reshape(8, 128, 2).transpose(1, 0, 2))
        m["cak"] = f(zl(np.asarray(cache_a_k[b])).reshape(2, PAST, 128))
        m["cav"] = f(zl(np.asarray(cache_a_v[b])).reshape(2, PAST, 128))
        m["cck"] = f(zl(np.asarray(cache_c_k[b])).reshape(2, PAST, 256))
        m["ccv"] = f(zl(np.asarray(cache_c_v[b])).reshape(2, PAST, 256))
        sC = zl(np.asarray(state_b_C[b]))
        m["sbC"] = f(sC.reshape(2, 2, 2, 2, 64, 64).reshape(2, 2, 2, 128, 64))
        sn = zl(np.asarray(state_b_n[b]))
        m["sbn"] = f(sn.reshape(2, 2, 2, 128, 1))
        sm_ = zl(np.asarray(state_b_m[b]))
        m["sbm"] = f(sm_.transpose(2, 0, 1).reshape(4, 4))
        in_maps.append(m)
    nc = _get_nc()
    res = run_bass_kernel_spmd(nc, in_maps, core_ids=list(range(NCORES)))
    R = res.results
    y_prompt = np.zeros((16, LP, D), np.float32)
    y_sample = np.zeros((4, LS, D), np.float32)
    nak = np.zeros((16, 2, LP, 2, 64), np.float32)
    nav = np.zeros((16, 2, LP, 2, 64), np.float32)
    nck = np.zeros((16, 2, LP, 4, 2, 32), np.float32)
    ncv = np.zeros((16, 2, LP, 4, 64), np.float32)
    nbC = np.zeros((16, 2, 2, 4, 64, 64), np.float32)
    nbn = np.zeros((16, 2, 2, 4, 64), np.float32)
    nbm = np.zeros((16, 2, 2, 4), np.float32)
    for core in range(NCORES):
        r = R[core]
        oy = np.asarray(r["o_y"])
        if SAMPLE_OF_CORE[core] is not None:
            y_sample[SAMPLE_OF_CORE[core]] = oy[0:LS]
        for s in range(2):
            bp = 2 * core + s
            y_prompt[bp] = oy[LS + s * LP:LS + (s + 1) * LP]
            nak[bp] = np.asarray(r["o_ak"])[s].reshape(2, LP, 2, 64)
            nav[bp] = np.asarray(r["o_av"])[s].reshape(2, LP, 2, 64)
            nck[bp] = np.asarray(r["o_ck"])[s].reshape(2, LP, 4, 2, 32)
            ncv[bp] = np.asarray(r["o_cv"])[s].reshape(2, LP, 4, 64)
            nbC[bp] = np.asarray(r["o_bC"])[s]
            nbn[bp] = np.asarray(r["o_bn"])[s]
            nbm[bp] = np.asarray(r["o_bm"]).reshape(2, 2, 2, 4)[s]
    return (y_prompt, y_sample, nak, nav, nck, ncv, nbC, nbn, nbm)
```

```python
import math
from contextlib import ExitStack
import numpy as np
import ml_dtypes
import concourse.bass as bass
import concourse.mybir as mybir
from concourse.bass_utils import run_bass_kernel_spmd

F32 = mybir.dt.float32
BF16 = mybir.dt.bfloat16
AF = mybir.ActivationFunctionType
ALU = mybir.AluOpType

D = 1024
DFF = 2816
NT = 2560
LS = 2048
LP = 256
PAST = 512
EPS = 1e-6
NCORES = 8
FLAGS = {"A": True, "B": True, "C": True, "FFN": True, "LAYERS": 2}
SAMPLE_OF_CORE = [0, 1, None, None, 2, 3, None, None]

QA0, KA0, VA0 = 0, 512, 640
QB0, KB0, VB0, OB0, GB0 = 768, 1024, 1280, 1536, 1792
QC0, KC0, VC0 = 1808, 2320, 2576
NW = 2832


class _Op:
    __slots__ = ("eng", "fn", "deps", "sig", "val", "dma", "dsem", "dval", "waits", "ph")

    def __init__(self, eng, fn, deps, dma):
        self.eng = eng
        self.fn = fn
        self.deps = deps
        self.sig = False
        self.val = 0
        self.dma = dma
        self.dsem = None
        self.dval = 0
        self.waits = None


class Prog:
    ENGS = ("pe", "act", "dve", "pool", "sp")
    NDMA = 24

    def __init__(self):
        self.ops = []
        self.track = {}
        self.dma_rr = 0
        self.dma_rr2 = [0, 0]
        self.phase = "init"
        self.dma_last = [None] * self.NDMA
        self.dma_cnt = [0] * self.NDMA
        self.bar = None

    def emit(self, eng, fn, reads=(), writes=(), dma=False):
        deps = set()
        for k in reads:
            t = self.track.get(k)
            if t is not None and t[0] is not None:
                deps.add(t[0])
        for k in writes:
            t = self.track.get(k)
            if t is not None:
                if t[0] is not None:
                    deps.add(t[0])
                deps.update(t[1].values())
                deps.update(t[2])
        if self.bar is not None:
            deps.add(self.bar)
        oid = len(self.ops)
        op = _Op(eng, fn, deps, dma)
        op.ph = self.phase
        if dma:
            half = self.NDMA // 2
            base = 0 if eng == "sp" else half
            r = self.dma_rr2[eng != "sp"]
            self.dma_rr2[eng != "sp"] = (r + 1) % half
            i = base + r
            if self.dma_last[i] is not None:
                deps.add(self.dma_last[i])
            self.dma_last[i] = oid
            self.dma_cnt[i] += 16
            op.dsem = i
            op.dval = self.dma_cnt[i]
        if eng == "pe" and not dma:
            op.deps = set(d for d in deps if not (self.ops[d].eng == "pe" and not self.ops[d].dma))
        self.ops.append(op)
        for k in reads:
            t = self.track.setdefault(k, [None, {}, []])
            if dma:
                t[2].append(oid)
            else:
                t[1][eng] = oid
        for k in writes:
            self.track[k] = [oid, {}, []]
        return oid

    def finalize(self):
        for op in self.ops:
            for d in op.deps:
                self.ops[d].sig = True
        cnt = {e: 0 for e in self.ENGS}
        for op in self.ops:
            if op.dma:
                continue
            if op.sig:
                cnt[op.eng] += 1
                op.val = cnt[op.eng]
        waited = {e: {} for e in self.ENGS}
        for op in self.ops:
            w = {}
            for d in op.deps:
                dop = self.ops[d]
                if dop.dma:
                    key = ("dma", dop.dsem)
                    v = dop.dval
                else:
                    key = ("eng", dop.eng)
                    v = dop.val
                if waited[op.eng].get(key, 0) >= v:
                    continue
                if w.get(key, 0) < v:
                    w[key] = v
            for key, v in w.items():
                waited[op.eng][key] = v
            op.waits = list(w.items())

    def replay(self, nc, esems, dsems):
        self.finalize()
        fw = [(i, self.dma_cnt[i]) for i in range(self.NDMA) if self.dma_cnt[i] > 0]
        handles = {"pe": "tensor", "act": "scalar", "dve": "vector", "pool": "gpsimd", "sp": "sync"}
        with nc.Block() as block:
            for en in self.ENGS:
                myops = [op for op in self.ops if op.eng == en]

                def body(e, myops=myops, en=en):
                    for op in myops:
                        for key, v in op.waits:
                            sem = dsems[key[1]] if key[0] == "dma" else esems[key[1]]
                            e.wait_ge(sem, v)
                        inst = op.fn(e)
                        if op.dma:
                            inst.then_inc(dsems[op.dsem], 16)
                        elif op.sig:
                            inst.then_inc(esems[en], 1)
                    if en == "sp":
                        for i, v in fw:
                            e.wait_ge(dsems[i], v)

                getattr(block, handles[en])(body)


def _rope_tables():
    def tab(dim):
        t = np.arange(LS)
        row = (t // 64).astype(np.float32)
        col = (t % 64).astype(np.float32)
        axis_dim = dim // 2
        freqs = (10000.0 ** (-np.arange(0, axis_dim, 2, dtype=np.float32) / axis_dim)).astype(np.float32)
        ang = np.concatenate([row[:, None] * freqs, col[:, None] * freqs], axis=-1)
        return np.cos(ang).astype(np.float32), np.sin(ang).astype(np.float32)
    cA, sA = tab(64)
    cC, sC = tab(32)
    cosA = np.zeros((128, LS), np.float32)
    sinA = np.zeros((128, LS), np.float32)
    cosC = np.ones((128, LS), np.float32)
    sinC = np.zeros((128, LS), np.float32)
    for p in range(128):
        d = p % 64
        i = d % 32
        cosA[p] = cA[:, i]
        sinA[p] = -sA[:, i] if d < 32 else sA[:, i]
        d2 = p % 32
        j = d2 % 16
        cosC[p] = cC[:, j]
        sinC[p] = -sC[:, j] if d2 < 16 else sC[:, j]
    return cosA, sinA, cosC, sinC


def _consts():
    c = {}
    c["ident"] = np.eye(128, dtype=np.float32)
    c["ones"] = np.ones((128, 128), np.float32)
    bd = np.zeros((128, 128), np.float32)
    bd[:64, :64] = 1.0
    bd[64:, 64:] = 1.0
    c["bd"] = bd
    pa = np.zeros((128, 128), np.float32)
    pc = np.zeros((128, 128), np.float32)
    for m in range(128):
        d = m % 64
        base = m - d
        pa[base + (d ^ 32), m] = 1.0
        pc[m ^ 16, m] = 1.0
    c["pswA"] = pa
    c["pswC"] = pc
    s = np.arange(128)[:, None]
    x = np.arange(896)[None, :]
    c["mkf"] = ((x - 384) >= s).astype(np.float32)
    c["mkb"] = ((x - 384) <= s).astype(np.float32)
    sel = np.zeros((2, 4, 128), np.float32)
    for m in range(2):
        sel[m, 2 * m + 1, :64] = 1.0
        sel[m, 2 * m, 64:] = 1.0
    c["selden"] = sel.transpose(1, 0, 2).reshape(4, 256).copy()
    sel2 = np.zeros((2, 4, 128), np.float32)
    for m in range(2):
        sel2[m, 2 * m, :64] = 1.0
        sel2[m, 2 * m + 1, 64:] = 1.0
    c["selnum"] = sel2.transpose(1, 0, 2).reshape(4, 256).copy()
    cosA, sinA, cosC, sinC = _rope_tables()
    c["cosA"], c["sinA"], c["cosC"], c["sinC"] = cosA, sinA, cosC, sinC
    return c


def _perm_w_in(w_in_l):
    W = np.zeros((D, NW), np.float32)
    for m in range(4):
        W[:, QA0 + m * 128: QA0 + m * 128 + 64] = w_in_l[:, m * 64:(m + 1) * 64]
        W[:, QA0 + m * 128 + 64: QA0 + (m + 1) * 128] = w_in_l[:, (4 + m) * 64:(5 + m) * 64]
    W[:, KA0:KA0 + 128] = w_in_l[:, 512:640]
    W[:, VA0:VA0 + 128] = w_in_l[:, 640:768]
    for m in range(2):
        for i in range(4):
            W[:, QB0 + m * 512 + i * 128: QB0 + m * 512 + (i + 1) * 128] = w_in_l[:, 768 + i * 256 + m * 128: 768 + i * 256 + (m + 1) * 128]
    W[:, GB0:GB0 + 16] = w_in_l[:, 1792:1808]
    for h in range(4):
        for n in range(2):
            src = (h * 2 + n) * 32
            hp, hl = h // 2, h % 2
            dst = QC0 + hp * 256 + n * 128 + (hl * 2 + n) * 32
            W[:, dst: dst + 32] = w_in_l[:, 1808 + src: 1808 + src + 32]
    W[:, KC0:KC0 + 256] = w_in_l[:, 2064:2320]
    W[:, VC0:VC0 + 256] = w_in_l[:, 2320:2576]
    return W


def _perm_w_out(w_out_l):
    idx = []
    for m in range(4):
        idx += list(range(m * 64, (m + 1) * 64)) + list(range((4 + m) * 64, (5 + m) * 64))
    idx += list(range(512, 1024))
    return np.ascontiguousarray(w_out_l[idx, :])


class Builder:
    def __init__(self):
        self.nc = bass.Bass("TRN2", target_bir_lowering=False)
        self.P = Prog()
        self.psrr = 0
        self.ptkeys = None

    def dram_in(self, name, shape, dt=F32):
        return self.nc.dram_tensor(name, list(shape), dt, kind="ExternalInput").ap()

    def dram_out(self, name, shape):
        return self.nc.dram_tensor(name, list(shape), F32, kind="ExternalOutput").ap()

    def E(self, eng, fn, reads=(), writes=()):
        return self.P.emit(eng, fn, reads, writes)

    def DMA(self, q, out, in_, reads=(), writes=()):
        return self.P.emit(q, lambda e, o=out, i=in_: e.dma_start(out=o, in_=i), reads, writes, dma=True)

    def MM(self, out, lhsT, rhs, start, stop, reads, writes):
        return self.P.emit("pe", lambda e, o=out, l=lhsT, r=rhs, s=start, t=stop: e.matmul(o, lhsT=l, rhs=r, start=s, stop=t), reads, writes)

    def TR(self, out, in_, ident, reads, writes):
        return self.P.emit("pe", lambda e, o=out, i=in_, d=ident: e.transpose(out=o, in_=i, identity=d), reads, writes)

    def ACT(self, out, in_, func, reads, writes, bias=None, scale=None):
        kw = {}
        if bias is not None:
            kw["bias"] = bias
        if scale is not None:
            kw["scale"] = scale
        return self.P.emit("act", lambda e, o=out, i=in_, f=func, kw=kw: e.activation(out=o, in_=i, func=f, **kw), reads, writes)

    def TT(self, out, in0, in1, op, reads, writes, eng="dve"):
        return self.P.emit(eng, lambda e, o=out, a=in0, b=in1, p=op: e.tensor_tensor(out=o, in0=a, in1=b, op=p), reads, writes)

    def TS(self, out, in0, s1, s2, op0, op1, reads, writes):
        if op1 is None:
            return self.P.emit("dve", lambda e, o=out, a=in0, x=s1, p=op0: e.tensor_scalar(out=o, in0=a, scalar1=x, scalar2=None, op0=p), reads, writes)
        return self.P.emit("dve", lambda e, o=out, a=in0, x=s1, y=s2, p=op0, q=op1: e.tensor_scalar(out=o, in0=a, scalar1=x, scalar2=y, op0=p, op1=q), reads, writes)

    def STT(self, out, in0, scalar, in1, op0, op1, reads, writes):
        return self.P.emit("dve", lambda e, o=out, a=in0, s=scalar, b=in1, p=op0, q=op1: e.scalar_tensor_tensor(out=o, in0=a, scalar=s, in1=b, op0=p, op1=q), reads, writes)

    def CP(self, out, in_, reads, writes, eng="dve"):
        return self.P.emit(eng, lambda e, o=out, i=in_: e.tensor_copy(out=o, in_=i), reads, writes)

    def RCP(self, out, in_, reads, writes):
        return self.P.emit("dve", lambda e, o=out, i=in_: e.reciprocal(out=o, in_=i), reads, writes)

    def RCPF(self, out, in_, reads, writes):
        return self.P.emit("dve", lambda e, o=out, i=in_: e.reciprocal_approx_fast(out=o, in_=i), reads, writes)

    def MEMSET(self, ap, val, writes, eng="dve"):
        return self.P.emit(eng, lambda e, a=ap, v=val: e.memset(a, v), (), writes)

    def SCAN(self, out, d0, d1, init, op0, op1, reads, writes):
        return self.P.emit("dve", lambda e, o=out, a=d0, b=d1, i=init, p=op0, q=op1: e.tensor_tensor_scan(out=o, data0=a, data1=b, initial=i, op0=p, op1=q), reads, writes)

    def ps(self, pool):
        lst = self.pspools[pool]
        i = self.psidx.get(pool, 0)
        self.psidx[pool] = (i + 1) % len(lst)
        b = lst[i]
        return self.psum[b], ("ps", b)

    def build(self):
        nc = self.nc
        di = self.dram_in
        self.xin = di("xin", [NT, D])
        self.cvec = di("cvec", [128, 8, 2])
        self.w_ada = di("w_ada", [2, D, 9 * D])
        self.b_ada = di("b_ada", [2, 128, 72])
        self.g_norm = di("g_norm", [128, 2 * 3 * 8])
        self.g_final = di("g_final", [128, 8])
        self.w_ff_in = di("w_ff_in", [2, 2, D, 2 * DFF])
        self.w_ff_out = di("w_ff_out", [2, 2, DFF, D])
        self.w_in = di("w_in", [2, D, NW])
        self.w_out = di("w_out", [2, D, D])
        self.gcols = di("gcols", [128, 2 * 4])
        self.bg = di("bg", [4, 2 * 4])
        self.lamv = di("lamv", [128, 2 * 4 * 32])
        self.cak = di("cak", [2, PAST, 128])
        self.cav = di("cav", [2, PAST, 128])
        self.cck = di("cck", [2, PAST, 256])
        self.ccv = di("ccv", [2, PAST, 256])
        self.sbC = di("sbC", [2, 2, 2, 128, 64])
        self.sbn = di("sbn", [2, 2, 2, 128, 1])
        self.sbm = di("sbm", [4, 2 * 2])
        cn = {}
        for k, shp in (("ident", [128, 128]), ("ones", [128, 128]), ("bd", [128, 128]), ("pswA", [128, 128]),
                       ("pswC", [128, 128]), ("mkf", [128, 896]), ("mkb", [128, 896]), ("selden", [4, 256]), ("selnum", [4, 256]),
                       ("cosA", [128, LS]), ("sinA", [128, LS]), ("cosC", [128, LS]), ("sinC", [128, LS])):
            cn[k] = di("c_" + k, shp)
        self.cn = cn
        do = self.dram_out
        self.o_y = do("o_y", [NT, D])
        self.o_ak = do("o_ak", [2, 2, LP, 128])
        self.o_av = do("o_av", [2, 2, LP, 128])
        self.o_ck = do("o_ck", [2, 2, LP, 256])
        self.o_cv = do("o_cv", [2, 2, LP, 256])
        self.o_bC = do("o_bC", [2, 2, 2, 4, 64, 64])
        self.o_bn = do("o_bn", [2, 2, 2, 4, 64])
        self.o_bm = do("o_bm", [8, 4])

        with ExitStack() as st:
            sb = lambda n, s, d: st.enter_context(nc.sbuf_tensor(n, s, d))
            self.h = sb("h", [128, 8, NT], F32)
            AW = 29696
            self.arena = sb("arena", [128, AW], F32)
            self.ident = sb("ident", [128, 128], F32)
            self.ones = sb("ones", [128, 128], F32)
            self.bd = sb("bd", [128, 128], F32)
            self.pswA = sb("pswA", [128, 128], BF16)
            self.pswC = sb("pswC", [128, 128], BF16)
            self.mkf = sb("mkf", [128, 896], BF16)
            self.mkb = sb("mkb", [128, 896], BF16)
            self.selden = sb("selden", [4, 256], F32)
            self.selnum_t = sb("selnum", [4, 256], F32)
            self.modc = sb("modc", [128, 2, 72, 2], F32)
            self.nsc = sb("nsc", [128, 2, 3, 8, 2], F32)
            self.gt = sb("gt", [128, 2, 3, 8, 2], F32)
            self.gn = sb("gn", [128, 48], F32)
            self.gfin = sb("gfin", [128, 8], F32)
            self.bada = sb("bada", [128, 2, 72], F32)
            self.gc = sb("gc", [128, 8], F32)
            self.bgs = sb("bgs", [4, 8], F32)
            self.nbg = sb("nbg", [4, 8], F32)
            self.lamc = sb("lamc", [128, 8], F32)
            self.sbms = sb("sbms", [4, 4], F32)
            self.cv = sb("cv", [128, 8, 2], F32)
            self.cvb = sb("cvb", [128, 8, 2], BF16)
            self.small = sb("small", [128, 64], F32)
            self.psum = [st.enter_context(nc.psum_tensor("ps%d" % i, [128, 512], F32)) for i in range(8)]
            self.pspools = {"a": [0, 1, 2], "b": [3, 4], "c": [5, 6, 7]}
            self.psidx = {}
            esems = {e: st.enter_context(nc.semaphore("s_" + e)) for e in Prog.ENGS}
            dsems = [st.enter_context(nc.semaphore("d%d" % i)) for i in range(Prog.NDMA)]
            self.body()
            self.P.replay(nc, esems, dsems)
        return nc

    def carve_reset(self):
        self.aoff = 0

    def carve(self, words, dt, shape=None):
        a = self.arena[:, self.aoff:self.aoff + words]
        self.aoff += words
        assert self.aoff <= 29696, self.aoff
        if dt is BF16:
            a = a.bitcast(BF16)
        return a

    def barrier(self):
        keys = list(self.P.track.keys())
        oid = self.P.emit("dve", lambda e, a=self.small[:, 63:64]: e.memset(a, 0.0), reads=(), writes=keys + ["__bar"])
        self.P.bar = oid

    def body(self):
        B = self
        h = self.h
        ld = [("ident", self.ident), ("ones", self.ones), ("bd", self.bd), ("selden", self.selden), ("selnum", self.selnum_t)]
        for k, t in ld:
            B.DMA("sp", t[:], self.cn[k], writes=[k])
        for k, t in (("pswA", self.pswA), ("pswC", self.pswC), ("mkf", self.mkf), ("mkb", self.mkb)):
            B.DMA("pool", t[:], self.cn[k], writes=[k + "_b"])
        B.DMA("sp", self.gn[:], self.g_norm, writes=["gn"])
        B.DMA("sp", self.gfin[:], self.g_final, writes=["gfin"])
        B.DMA("sp", self.bada[:], self.b_ada.rearrange("l p j -> p l j"), writes=["bada"])
        B.DMA("sp", self.gc[:], self.gcols, writes=["gc"])
        B.DMA("sp", self.bgs[:], self.bg, writes=["bgs"])
        self.carve_reset()
        self.aoff = 3072
        self.lamt = self.carve(256, F32)
        B.DMA("sp", self.lamt, self.lamv, writes=["lamt"])
        B.DMA("sp", self.sbms[:], self.sbm, writes=["sbms"])
        B.DMA("sp", self.cv[:], self.cvec, writes=["cv"])
        B.TS(self.nbg[:], self.bgs[:], -1.0, None, ALU.mult, None, ["bgs"], ["nbg"])
        sm = self.small
        for l in range(2):
            lam_init = 0.8 - 0.6 * math.exp(-0.3 * l)
            for j in range(2):
                a = self.lamt[:, (l * 4 + 2 * j) * 32:(l * 4 + 2 * j + 1) * 32]
                b = self.lamt[:, (l * 4 + 2 * j + 1) * 32:(l * 4 + 2 * j + 2) * 32]
                B.TT(sm[:, 0:32], a, b, ALU.mult, ["lamt"], ["sm"])
                B.E("dve", lambda e, o=sm[:, 32 + j:33 + j], i=sm[:, 0:32]: e.reduce_sum(out=o, in_=i, axis=mybir.AxisListType.X), ["sm"], ["sm"])
                B.ACT(sm[:, 34 + j:35 + j], sm[:, 32 + j:33 + j], AF.Exp, ["sm"], ["sm"])
            B.TT(sm[:, 36:37], sm[:, 34:35], sm[:, 35:36], ALU.subtract, ["sm"], ["sm"])
            B.TS(self.lamc[:, 4 * l:4 * l + 1], sm[:, 36:37], lam_init, None, ALU.add, None, ["sm"], ["lamc"])
            B.TS(self.lamc[:, 4 * l + 1:4 * l + 2], self.lamc[:, 4 * l:4 * l + 1], -1.0, None, ALU.mult, None, ["lamc"], ["lamc"])
            B.TS(self.lamc[:, 4 * l + 2:4 * l + 3], self.gc[:, 4 * l + 3:4 * l + 4], 1.0 - lam_init, None, ALU.mult, None, ["gc", "lamc"], ["lamc"])

        self.carve_reset()
        xt = [self.carve(1024, F32) for _ in range(3)]
        for t in range(NT // 128):
            xb = xt[t % 3]
            B.DMA("sp", xb, self.xin[t * 128:(t + 1) * 128, :], writes=[("xt", t % 3)])
            for half in range(2):
                pb, pk = B.ps("a")
                for c in range(4):
                    cc = half * 4 + c
                    B.TR(pb[:, c * 128:(c + 1) * 128], xb[:, cc * 128:(cc + 1) * 128], self.ident[:], [("xt", t % 3), "ident"], [pk])
                o = h[:, half * 4:half * 4 + 4, t * 128:(t + 1) * 128]
                i = pb[:].rearrange("p (c t) -> p c t", t=128)
                if half == 0:
                    B.CP(o, i, [pk], [("h", t // 2)])
                else:
                    B.ACT(o, i, AF.Copy, [pk], [("h", t // 2)])

        self.modulation()
        for l in range(FLAGS["LAYERS"]):
            if FLAGS["FFN"]:
                self.ffn(l, 0)
            self.mix_layer(l)
            if FLAGS["FFN"]:
                self.ffn(l, 1)
        self.final_out()

    def hkeys(self, off, w):
        return [("h", b) for b in range(off // 256, (off + w) // 256)]

    def modulation(self):
        B = self
        self.P.phase = "mod"
        self.aoff = 4096
        wb = [self.carve(4096, BF16).rearrange("p (k n) -> p k n", n=1024) for _ in range(2)]
        B.ACT(self.cvb[:], self.cv[:], AF.Silu, ["cv"], ["cvb"])
        for l in range(2):
            wv = self.w_ada[l].rearrange("(k p) n -> p k n", p=128)
            pb, pk = B.ps("b")
            for blk in range(9):
                w = wb[blk % 2]
                key = ("wada", blk % 2)
                B.DMA("pool", w, wv[:, :, blk * 1024:(blk + 1) * 1024], writes=[key])
                for fc in range(8):
                    j = blk * 8 + fc
                    for k in range(8):
                        B.MM(pb[:, 2 * j:2 * j + 2], w[:, k, fc * 128:(fc + 1) * 128], self.cvb[:, k, :], k == 0, k == 7, [key, "cvb"], [pk])
            B.TT(self.modc[:, l], pb[:, 0:144].rearrange("p (j c) -> p j c", c=2),
                 self.bada[:, l].unsqueeze(2).to_broadcast([128, 72, 2]), ALU.add, [pk, "bada"], ["modc"])
            for i in range(3):
                sc = self.modc[:, l, (3 * i + 1) * 8:(3 * i + 2) * 8, :]
                B.TS(self.nsc[:, l, i], sc, 1.0, None, ALU.add, None, ["modc"], ["nsc"])
                B.TT(self.nsc[:, l, i], self.nsc[:, l, i],
                     self.gn[:, (l * 3 + i) * 8:(l * 3 + i + 1) * 8].unsqueeze(2).to_broadcast([128, 8, 2]), ALU.mult, ["nsc", "gn"], ["nsc"])
                g = self.modc[:, l, (3 * i + 2) * 8:(3 * i + 3) * 8, :]
                B.TS(self.gt[:, l, i], g, (1.0 if i == 1 else 0.5), None, ALU.mult, None, ["modc"], ["gt"])

    def norm_mod(self, off, w, scale_col, shift_col, u_out, ukeys, tmps):
        B = self
        sq, rs, tmp = tmps
        hk = self.hkeys(off, w)
        pb, pk = B.ps("b")
        for c in range(8):
            s = sq[c % 2]
            B.ACT(s[:, 0:w], self.h[:, c, off:off + w], AF.Square, hk, [("sq", c % 2)])
            B.MM(pb[:, 0:w], self.ones[:], s[:, 0:w], c == 0, c == 7, [("sq", c % 2), "ones"], [pk])
        B.ACT(rs[:, 0:w], pb[:, 0:w], AF.Ln, [pk, "epsc"], ["rs"], bias=self.epsc[:, 0:1], scale=1.0 / D)
        B.ACT(rs[:, 0:w], rs[:, 0:w], AF.Exp, ["rs"], ["rs"], scale=-0.5)
        for c in range(8):
            t = tmp[c % 2]
            B.STT(t[:, 0:w], self.h[:, c, off:off + w], scale_col(c), rs[:, 0:w], ALU.mult, ALU.mult, hk + ["rs", "nsc", "gfin"], [("tmp", c % 2)])
            if shift_col is None:
                if c % 2 == 0:
                    B.ACT(u_out(c), t[:, 0:w], AF.Copy, [("tmp", c % 2)], ukeys(c))
                else:
                    B.CP(u_out(c), t[:, 0:w], [("tmp", c % 2)], ukeys(c))
            else:
                if c % 2 == 0:
                    B.ACT(u_out(c), t[:, 0:w], AF.Identity, [("tmp", c % 2), "modc"], ukeys(c), bias=shift_col(c))
                else:
                    B.TS(u_out(c), t[:, 0:w], shift_col(c), None, ALU.add, None, [("tmp", c % 2), "modc"], ukeys(c))

    def mk_eps(self):
        if not hasattr(self, "epsc"):
            self.epsc = self.small[:, 40:41]
            self.MEMSET(self.small[:, 40:41], EPS, ["epsc"])

    def ffn(self, l, j):
        B = self
        self.P.phase = "ffn%d%d" % (l, j)
        ni = 0 if j == 0 else 2
        self.barrier()
        self.mk_eps()
        self.carve_reset()
        U = self.carve(10240, BF16).rearrange("p (k t) -> p k t", t=NT)
        HID = self.carve(7680, BF16).rearrange("p (f t) -> p f t", t=NT)
        W1 = [self.carve(2048, BF16).rearrange("p (k g n) -> p k g n", g=2, n=256) for _ in range(2)]
        W2 = self.carve(3072, BF16).rearrange("p (f n) -> p f n", n=D)
        sq = [self.carve(512, F32) for _ in range(2)]
        rs = self.carve(512, F32)
        tmp = [self.carve(512, F32) for _ in range(2)]
        sg = [self.carve(512, F32) for _ in range(2)]
        tiles = [(0, 512, 0), (512, 512, 0), (1024, 512, 0), (1536, 512, 0), (2048, 512, 1)]
        for (off, w, mc) in tiles:
            self.norm_mod(off, w,
                          lambda c, mc=mc: self.nsc[:, l, ni, c, mc:mc + 1],
                          lambda c, mc=mc: self.modc[:, l, (3 * ni) * 8 + c, mc:mc + 1],
                          lambda c, off=off, w=w: U[:, c, off:off + w],
                          lambda c, off=off: [("u", c, off // 512)], (sq, rs, tmp))
        w1v = self.w_ff_in[l, j].rearrange("(k p) n -> p k n", p=128)
        w2v = self.w_ff_out[l, j].rearrange("(f p) n -> p f n", p=128)
        passes = [(0, 6), (6, 6), (12, 6), (18, 4)]
        w1i = 0
        sgi = 0
        for (f0, nf) in passes:
            for fp in range(nf // 2):
                f = f0 + 2 * fp
                wt = W1[w1i % 2]
                wk = ("w1", w1i % 2)
                w1i += 1
                B.DMA("pool", wt[:, :, 0, :], w1v[:, :, f * 128:f * 128 + 256], writes=[wk + (0,)])
                B.DMA("pool", wt[:, :, 1, :], w1v[:, :, DFF + f * 128:DFF + f * 128 + 256], writes=[wk + (1,)])
                if fp == 0:
                    B.DMA("pool", W2[:, 0:nf, :], w2v[:, f0:f0 + nf, :], writes=["w2"])
                for sub in range(2):
                    fl = 2 * fp + sub
                    for ti, (off, w, mc) in enumerate(tiles):
                        pg, pgk = B.ps("a")
                        pu, puk = B.ps("c")
                        for k in range(8):
                            B.MM(pg[:, 0:w], wt[:, k, 0, sub * 128:(sub + 1) * 128], U[:, k, off:off + w], k == 0, k == 7, [wk + (0,), ("u", k, ti)], [pgk])
                        for k in range(8):
                            B.MM(pu[:, 0:w], wt[:, k, 1, sub * 128:(sub + 1) * 128], U[:, k, off:off + w], k == 0, k == 7, [wk + (1,), ("u", k, ti)], [puk])
                        s = sg[sgi % 2]
                        sk = ("sg", sgi % 2)
                        sgi += 1
                        B.ACT(s[:, 0:w], pg[:, 0:w], AF.Silu, [pgk], [sk])
                        B.TT(HID[:, fl, off:off + w], s[:, 0:w], pu[:, 0:w], ALU.mult, [sk, puk], [("hid", fl, ti)])
            for ti, (off, w, mc) in enumerate(tiles):
                hk = self.hkeys(off, w)
                for d in range(8):
                    pb, pk = B.ps("b")
                    for fl in range(nf):
                        B.MM(pb[:, 0:w], W2[:, fl, d * 128:(d + 1) * 128], HID[:, fl, off:off + w], fl == 0, fl == nf - 1, ["w2", ("hid", fl, ti)], [pk])
                    hv = self.h[:, d, off:off + w]
                    B.STT(hv, pb[:, 0:w], self.gt[:, l, ni, d, mc:mc + 1], hv, ALU.mult, ALU.add, [pk, "gt"] + hk, hk)

    def final_out(self):
        B = self
        self.P.phase = "final"
        self.barrier()
        self.mk_eps()
        self.carve_reset()
        sq = [self.carve(512, F32) for _ in range(2)]
        rs = self.carve(512, F32)
        tmp = [self.carve(512, F32) for _ in range(2)]
        yf = self.carve(4096, F32).rearrange("p (k t) -> p k t", t=512)
        ot = [self.carve(1024, F32) for _ in range(2)]
        oi = 0
        for ti in range(5):
            off = ti * 512
            self.norm_mod(off, 512, lambda c: self.gfin[:, c:c + 1], None,
                          lambda c: yf[:, c, :], lambda c: [("yf", c)], (sq, rs, tmp))
            for tt in range(4):
                o = ot[oi % 2]
                ok = ("ot", oi % 2)
                oi += 1
                for half in range(2):
                    pb, pk = B.ps("a")
                    for c in range(4):
                        cc = half * 4 + c
                        B.TR(pb[:, c * 128:(c + 1) * 128], yf[:, cc, tt * 128:(tt + 1) * 128], self.ident[:], [("yf", cc), "ident"], [pk])
                    if half == 0:
                        B.CP(o[:, 0:512], pb[:], [pk], [ok])
                    else:
                        B.ACT(o[:, 512:1024], pb[:], AF.Copy, [pk], [ok])
                r0 = off + tt * 128
                B.DMA("sp", self.o_y[r0:r0 + 128, :], o, reads=[ok])

    def mix_layer(self, l):
        B = self
        self.mk_eps()
        for (seq, off, L) in ((0, 0, LS), (1, LS, LP), (2, LS + LP, LP)):
            self.mix_seq(l, seq, off, L)

    def mix_seq(self, l, seq, off, L):
        B = self
        h = self.h
        sample = (seq == 0)
        mc = 0 if sample else 1
        TW = 512 if sample else 256
        ntile = L // TW
        nch = L // 128
        self.P.phase = "mix%d_s%d_norm" % (l, seq)
        self.barrier()
        self.carve_reset()
        U = self.carve(8 * L // 2, BF16).rearrange("p (k t) -> p k t", t=L)
        Y = self.carve(8 * L // 2, BF16).rearrange("p (k t) -> p k t", t=L)
        sq = [self.carve(512, F32) for _ in range(2)]
        rs = self.carve(512, F32)
        tmp = [self.carve(512, F32) for _ in range(2)]
        base_off = self.aoff
        for t in range(ntile):
            o = t * TW
            self.norm_mod(off + o, TW,
                          lambda c: self.nsc[:, l, 1, c, mc:mc + 1],
                          lambda c: self.modc[:, l, 24 + c, mc:mc + 1],
                          lambda c, o=o: U[:, c, o:o + TW],
                          lambda c, t=t: [("u", c, t)], (sq, rs, tmp))
        wv = self.w_in[l].rearrange("(k p) n -> p k n", p=128)
        Wall = None
        if not sample:
            Wall = self.carve(8 * NW // 2, BF16).rearrange("p (k n) -> p k n", n=NW)
            base_off = self.aoff
            if seq == 1:
                half = NW // 2
                B.DMA("pool", Wall[:, :, 0:half], wv[:, :, 0:half], writes=["wall"])
                B.DMA("pool", Wall[:, :, half:NW], wv[:, :, half:NW], writes=["wall"])
        ctx = dict(l=l, seq=seq, off=off, L=L, sample=sample, TW=TW, ntile=ntile, nch=nch, U=U, Y=Y, wv=wv,
                   sq=sq, rs=rs, tmp=tmp, Wall=Wall)
        for grp, fn in (("A", self.group_A), ("B", self.group_B), ("C", self.group_C)):
            self.aoff = base_off
            if FLAGS[grp]:
                self.P.phase = "mix%d_s%d_%s" % (l, seq, grp)
                self.barrier()
                fn(ctx)
            else:
                c0, c1 = {"A": (0, 4), "B": (4, 6), "C": (6, 8)}[grp]
                for c in range(c0, c1):
                    B.MEMSET(Y[:, c, :], 0.0, [("y", c)])
        self.P.phase = "mix%d_s%d_out" % (l, seq)
        self.barrier()
        self.aoff = base_off
        WO = self.carve(4096, BF16).rearrange("p (k n) -> p k n", n=D)
        B.DMA("pool", WO, self.w_out[l].rearrange("(k p) n -> p k n", p=128), writes=["wo"])
        for t in range(ntile):
            o = t * TW
            hk = self.hkeys(off + o, TW)
            for d in range(8):
                pb, pk = B.ps("b")
                for k in range(8):
                    B.MM(pb[:, 0:TW], WO[:, k, d * 128:(d + 1) * 128], Y[:, k, o:o + TW], k == 0, k == 7, ["wo", ("y", k)], [pk])
                hv = h[:, d, off + o:off + o + TW]
                B.STT(hv, pb[:, 0:TW], self.gt[:, l, 1, d, mc:mc + 1], hv, ALU.mult, ALU.add, [pk, "gt"] + hk, hk)

    def proj_fm(self, W, wkey, c0, ncol, U, o, TW, pool="a"):
        pb, pk = self.ps(pool)
        for k in range(8):
            self.MM(pb[0:ncol, 0:TW], W[:, k, c0:c0 + ncol], U[:, k, o:o + TW], k == 0, k == 7, [wkey] + [("u", k, o // TW)], [pk])
        return pb, pk

    def headnorm(self, src, srck, w, gcol, outs, ctx):
        B = self
        sq, rs = ctx["sq"], ctx["rs"]
        B.ACT(sq[0][:, 0:w], src, AF.Square, [srck], [("sq", 0)])
        pb, pk = B.ps("b")
        B.MM(pb[:, 0:w], self.bd[:], sq[0][:, 0:w], True, True, [("sq", 0), "bd"], [pk])
        B.ACT(rs[:, 0:w], pb[:, 0:w], AF.Ln, [pk, "epsc"], ["rs"], bias=self.epsc[:, 0:1], scale=1.0 / 64)
        B.ACT(rs[:, 0:w], rs[:, 0:w], AF.Exp, ["rs"], ["rs"], scale=-0.5)
        for ent in outs:
            o, ok = ent[0], ent[1]
            psl = ent[2] if len(ent) > 2 else slice(0, 128)
            B.STT(o, src[psl], gcol[psl], rs[psl, 0:w], ALU.mult, ALU.mult, [srck, "rs", "gc", "lamc"], ok)

    def rope(self, x, xk, w, psw, pswk, cos, sin, ck, out, outk, ctx):
        B = self
        tmp = ctx["tmp"]
        pb, pk = B.ps("b")
        B.MM(pb[:, 0:w], psw[:], x, True, True, [xk, pswk], [pk])
        cks = ck if isinstance(ck, list) else [ck]
        B.TT(tmp[0][:, 0:w], x, cos, ALU.mult, [xk] + cks, [("tmp", 0)])
        B.TT(tmp[1][:, 0:w], pb[:, 0:w], sin, ALU.mult, [pk] + cks, [("tmp", 1)])
        if isinstance(out, list):
            for (o, ok, psl) in out:
                B.TT(o, tmp[0][psl, 0:w], tmp[1][psl, 0:w], ALU.add, [("tmp", 0), ("tmp", 1)], ok)
        else:
            B.TT(out, tmp[0][:, 0:w], tmp[1][:, 0:w], ALU.add, [("tmp", 0), ("tmp", 1)], outk)

    def attn_scores_pv(self, qT, qk, half, KT, kkey, nk, vfun, vkey, TW, scale, PT, ctx, side=None, it0=0):
        B = self
        LAG = len(PT) - 1
        acc, acck = B.ps("c")
        pend = []
        for c in range(nk + LAG):
            while side and side[0][0] <= it0 + c:
                side.pop(0)[1]()
            if c < nk:
                sp_, spk = B.ps("a")
                B.MM(sp_[:, 0:TW], KT[:, c * 128:(c + 1) * 128], qT[:, 0:TW], True, True, [kkey, qk], [spk])
                pt = PT[c % len(PT)]
                ptk = self.ptkeys[c % len(PT)] if self.ptkeys else ("pt", c % len(PT))
                B.ACT(pt[:, 0:TW], sp_[:, 0:TW], AF.Exp, [spk], [ptk], scale=scale)
                pend.append((c, pt, ptk))
            if c >= LAG:
                cc, pt, ptk = pend.pop(0)
                B.MM(acc[:, 0:TW], vfun(cc), pt[:, 0:TW], cc == 0, cc == nk - 1, [vkey, ptk], [acck])
        return acc, acck

    def prep_stages(self, W, wkey, c0, U, o, TW, gcol, psw, pswk, cs, sn, outs, ctx, norm):
        B = self
        sq, rs, tmp = ctx["sq"], ctx["rs"], ctx["tmp"]
        xn = sq[1].bitcast(BF16)
        xnk = ("sq", 1)
        st = {}

        def s0():
            st["pb"], st["pk"] = self.proj_fm(W, wkey, c0, 128, U, o, TW, pool="b")

        def s1():
            B.ACT(sq[0][:, 0:TW], st["pb"][:, 0:TW], AF.Square, [st["pk"]], [("sq", 0)])

        def s2():
            st["pd"], st["pdk"] = B.ps("b")
            B.MM(st["pd"][:, 0:TW], self.bd[:], sq[0][:, 0:TW], True, True, [("sq", 0), "bd"], [st["pdk"]])

        def s3():
            B.ACT(rs[:, 0:TW], st["pd"][:, 0:TW], AF.Ln, [st["pdk"], "epsc"], ["rs"], bias=self.epsc[:, 0:1], scale=1.0 / 64)
            B.ACT(rs[:, 0:TW], rs[:, 0:TW], AF.Exp, ["rs"], ["rs"], scale=-0.5)

        def s4():
            if norm:
                B.STT(xn[:, 0:TW], st["pb"][:, 0:TW], gcol, rs[:, 0:TW], ALU.mult, ALU.mult, [st["pk"], "rs", "gc"], [xnk])
            else:
                B.CP(xn[:, 0:TW], st["pb"][:, 0:TW], [st["pk"]], [xnk])

        def s5():
            st["pr"], st["prk"] = B.ps("b")
            B.MM(st["pr"][:, 0:TW], psw[:], xn[:, 0:TW], True, True, [xnk, pswk], [st["prk"]])
            B.TT(tmp[0][:, 0:TW], xn[:, 0:TW], cs[:, 0:TW], ALU.mult, [xnk, "cs"], [("tmp", 0)])

        def s6():
            B.TT(tmp[1][:, 0:TW], st["pr"][:, 0:TW], sn[:, 0:TW], ALU.mult, [st["prk"], "cs2"], [("tmp", 1)])

        def s7():
            for (oo, ok, psl) in outs:
                B.TT(oo, tmp[0][psl, 0:TW], tmp[1][psl, 0:TW], ALU.add, [("tmp", 0), ("tmp", 1)], ok)

        if norm:
            return [s0, s1, s2, s3, s4, s5, s6, s7]
        return [s0, s4, s5, s6, s7]

    def group_A(self, ctx):
        B = self
        l, seq, off, L, sample, TW, ntile, nch, U, Y, wv = (ctx[k] for k in ("l", "seq", "off", "L", "sample", "TW", "ntile", "nch", "U", "Y", "wv"))
        npast = PAST if sample else 0
        nk = (npast + L) // 128
        wAk = "wall" if ctx["Wall"] is not None else "wA"
        if ctx["Wall"] is not None:
            W = ctx["Wall"][:, :, 0:768]
        else:
            W = self.carve(8 * 768 // 2, BF16).rearrange("p (k n) -> p k n", n=768)
            B.DMA("pool", W, wv[:, :, 0:768], writes=[wAk])
        KT = self.carve((npast + L) // 2, BF16)
        V = self.carve(nk * 2 * 128 // 2, BF16).rearrange("p (c g n) -> p c g n", g=2, n=128)
        QT = [[self.carve(256, BF16) for _ in range(2)] for _ in range(2)]
        PT = [self.carve(256, BF16) for _ in range(3)]
        xn = ctx["sq"][1].bitcast(BF16)
        xnk = ("sq", 1)
        rc = self.carve(512, F32)
        rc2 = rc
        cs = self.carve(512, F32)
        sn = self.carve(512, F32)
        stg = self.carve(512, F32)
        PT.append(stg.bitcast(BF16))
        self.ptkeys = [("pt", 0), ("pt", 1), ("pt", 2), "stg"]
        saved_pools = self.pspools
        self.pspools = {"a": [0, 1, 2, 3], "b": [4, 5], "c": [6, 7]}
        self.psidx = {}
        for i in range(2):
            B.MEMSET(QT[i][0][64:128, :], 0.0, [("qt", i, 0)])
            B.MEMSET(QT[i][1][0:64, :], 0.0, [("qt", i, 1)])
        B.MEMSET(V[:, :, 0, 64:128], 1.0, ["V"])
        B.MEMSET(V[:, :, 1, 0:64], 1.0, ["V"])
        vf = lambda c, g: V[:, c, g, :]
        if sample:
            for c in range(PAST // 128):
                B.DMA("sp", stg[:, 0:128], self.cak[l, c * 128:(c + 1) * 128, :], writes=["stg"])
                pb, pk = B.ps("b")
                B.TR(pb[:, 0:128], stg[:, 0:128], self.ident[:], ["stg", "ident"], [pk])
                B.CP(KT[:, c * 128:(c + 1) * 128], pb[:, 0:128], [pk], ["KT"])
                B.DMA("sp", stg[:, 128:256], self.cav[l, c * 128:(c + 1) * 128, :], writes=["stg"])
                B.CP(V[:, c, 0, 0:64], stg[:, 128:192], ["stg"], ["V"])
                B.CP(V[:, c, 1, 64:128], stg[:, 192:256], ["stg"], ["V"])
        gq = self.gc[:, 4 * l + 0:4 * l + 1]
        gk = self.gc[:, 4 * l + 1:4 * l + 2]
        def k_tile(t):
            o = t * TW
            pb, pk = self.proj_fm(W, wAk, KA0, 128, U, o, TW)
            if sample:
                self.headnorm(pb[:, 0:TW], pk, TW, gk, [(xn[:, 0:TW], [xnk])], ctx)
                B.DMA("sp", cs[:, 0:TW], self.cn["cosA"][:, o:o + TW], writes=["cs"])
                B.DMA("sp", sn[:, 0:TW], self.cn["sinA"][:, o:o + TW], writes=["cs2"])
                self.rope(xn[:, 0:TW], xnk, TW, self.pswA, "pswA_b", cs[:, 0:TW], sn[:, 0:TW], ["cs", "cs2"], KT[:, npast + o:npast + o + TW], ["KT"], ctx)
            else:
                self.headnorm(pb[:, 0:TW], pk, TW, gk, [(KT[:, o:o + TW], ["KT"]), (stg[:, 0:TW], ["stg"])], ctx)
                for s in range(TW // 128):
                    p2, p2k = B.ps("b")
                    B.TR(p2[:, 0:128], stg[:, s * 128:(s + 1) * 128], self.ident[:], ["stg", "ident"], [p2k])
                    B.CP(rc[:, 0:128], p2[:, 0:128], [p2k], ["rc"])
                    B.DMA("sp", self.o_ak[seq - 1, l, o + s * 128:o + (s + 1) * 128, :], rc[:, 0:128], reads=["rc"])

        def v_chunk(c):
            pb, pk = B.ps("a")
            for k in range(8):
                B.MM(pb[:, 0:128], U[:, k, c * 128:(c + 1) * 128], W[:, k, VA0:VA0 + 128], k == 0, k == 7, [wAk, ("u", k, (c * 128) // TW)], [pk])
            cc = npast // 128 + c
            B.CP(V[:, cc, 0, 0:64], pb[:, 0:64], [pk], ["V"])
            B.CP(V[:, cc, 1, 64:128], pb[:, 64:128], [pk], ["V"])
            if not sample:
                B.ACT(sn[:, 0:128], pb[:, 0:128], AF.Copy, [pk], ["cs2"])
                B.DMA("sp", self.o_av[seq - 1, l, c * 128:(c + 1) * 128, :], sn[:, 0:128], reads=["cs2"])

        vper = nch // ntile
        for t in range(ntile):
            k_tile(t)
            for c in range(t * vper, (t + 1) * vper):
                v_chunk(c)
        if sample:
            jobs = [(t, m) for t in range(ntile) for m in range(4)]

            def stages_for(t, m):
                o = t * TW
                q = QT[m % 2]
                qouts = [(q[0][0:64, 0:TW], [("qt", m % 2, 0)], slice(0, 64)), (q[1][64:128, 0:TW], [("qt", m % 2, 1)], slice(64, 128))]
                stl = self.prep_stages(W, wAk, QA0 + m * 128, U, o, TW, gq, self.pswA, "pswA_b", cs, sn, qouts, ctx, True)
                if m == 0:
                    def ld(o=o):
                        B.DMA("sp", cs[:, 0:TW], self.cn["cosA"][:, o:o + TW], writes=["cs"])
                        B.DMA("sp", sn[:, 0:TW], self.cn["sinA"][:, o:o + TW], writes=["cs2"])
                    stl = [ld] + stl
                return stl
            for f in stages_for(0, 0):
                f()
            for ji, (t, m) in enumerate(jobs):
                o = t * TW
                q = QT[m % 2]
                side = []
                if ji + 1 < len(jobs):
                    stl = stages_for(*jobs[ji + 1])
                    step = max(1, (2 * nk - 6) // len(stl))
                    side = [[2 + i * step, f] for i, f in enumerate(stl)]
                for g in range(2):
                    acc, acck = self.attn_scores_pv(q[g], ("qt", m % 2, g), g, KT, "KT", nk, lambda c, g=g: vf(c, g), "V", TW, 0.125, PT, ctx, side=side, it0=g * nk)
                    self.softmax_norm(acc, acck, g, TW, Y[:, m, o:o + TW], [("y", m)], rc, rc2)
                while side:
                    side.pop(0)[1]()
        else:
            for t in range(ntile):
                o = t * TW
                for m in range(4):
                    q = QT[m % 2]
                    qouts = [(q[0][0:64, 0:TW], [("qt", m % 2, 0)], slice(0, 64)), (q[1][64:128, 0:TW], [("qt", m % 2, 1)], slice(64, 128))]
                    pb, pk = self.proj_fm(W, wAk, QA0 + m * 128, 128, U, o, TW)
                    self.headnorm(pb[:, 0:TW], pk, TW, gq, qouts, ctx)
                    for g in range(2):
                        acc, acck = self.attn_scores_pv(q[g], ("qt", m % 2, g), g, KT, "KT", nk, lambda c, g=g: vf(c, g), "V", TW, 0.125, PT, ctx)
                        self.softmax_norm(acc, acck, g, TW, Y[:, m, o:o + TW], [("y", m)], rc, rc2)
        self.pspools = saved_pools
        self.psidx = {}
        self.ptkeys = None

    def softmax_norm(self, acc, acck, nh, TW, yout, ykeys, rc, rc2):
        B = self
        dh = 1 - nh
        ds = slice(dh * 64, dh * 64 + 64)
        ns = slice(nh * 64, nh * 64 + 64)
        B.RCP(rc[ds, 0:TW], acc[ds, 0:TW], [acck], ["rc"])
        B.CP(rc[ns, 0:TW], rc[ds, 0:TW], ["rc"], ["rc"])
        B.TT(yout[ns, :], acc[ns, 0:TW], rc[ns, 0:TW], ALU.mult, [acck, "rc"], ykeys)

    def group_C(self, ctx):
        B = self
        l, seq, off, L, sample, TW, ntile, nch, U, Y, wv = (ctx[k] for k in ("l", "seq", "off", "L", "sample", "TW", "ntile", "nch", "U", "Y", "wv"))
        npast = PAST if sample else 0
        nk = (npast + L) // 128
        Wall = ctx["Wall"]
        wCk = "wall" if Wall is not None else "wC"
        W = None if Wall is not None else self.carve(8 * 512 // 2, BF16).rearrange("p (k n) -> p k n", n=512)
        KT = self.carve((npast + L) // 2, BF16)
        V = self.carve(nk * 2 * 128 // 2, BF16).rearrange("p (c g n) -> p c g n", g=2, n=128)
        QT = [[self.carve(256, BF16) for _ in range(2)] for _ in range(2)]
        PT = [self.carve(256, BF16) for _ in range(4)]
        saved_pools = self.pspools
        self.pspools = {"a": [0, 1, 2, 3], "b": [4, 5], "c": [6, 7]}
        self.psidx = {}
        xn = ctx["sq"][1].bitcast(BF16)
        xnk = ("sq", 1)
        rc = self.carve(512, F32)
        rc2 = rc
        for n in range(2):
            B.MEMSET(QT[0][n][64:128, :], 0.0, [("qtc", 0, n)])
            B.MEMSET(QT[1][n][0:64, :], 0.0, [("qtc", 1, n)])
        cs = self.carve(512, F32)
        sn = self.carve(512, F32)
        r0 = self.carve(512, F32)
        r1 = self.carve(512, F32)
        stg = r1
        scale = 32 ** -0.5
        nlam = self.lamc[:, 4 * l + 1:4 * l + 2]
        gcl = self.lamc[:, 4 * l + 2:4 * l + 3]
        B.MEMSET(V[:, :, 0, 64:128], 1.0, ["V"])
        B.MEMSET(V[:, :, 1, 0:64], 1.0, ["V"])
        for hp in range(2):
            if Wall is not None:
                W = Wall[:, :, KC0:KC0 + 512]
            else:
                B.DMA("pool", W, wv[:, :, KC0:KC0 + 512], writes=["wC"])
            if sample:
                for c in range(PAST // 128):
                    B.DMA("sp", stg[:, 0:128], self.cck[l, c * 128:(c + 1) * 128, hp * 128:(hp + 1) * 128], writes=["r1"])
                    pb, pk = B.ps("b")
                    B.TR(pb[:, 0:128], stg[:, 0:128], self.ident[:], ["r1", "ident"], [pk])
                    B.CP(KT[:, c * 128:(c + 1) * 128], pb[:, 0:128], [pk], ["KT"])
                    B.DMA("sp", rc[:, 0:128], self.ccv[l, c * 128:(c + 1) * 128, hp * 128:(hp + 1) * 128], writes=["rc"])
                    B.CP(V[:, c, 0, 0:64], rc[:, 0:64], ["rc"], ["V"])
                    B.CP(V[:, c, 1, 64:128], rc[:, 64:128], ["rc"], ["V"])
            def k_tile(t):
                o = t * TW
                pb, pk = self.proj_fm(W, wCk, hp * 128, 128, U, o, TW)
                if sample:
                    B.DMA("sp", cs[:, 0:TW], self.cn["cosC"][:, o:o + TW], writes=["cs"])
                    B.DMA("sp", sn[:, 0:TW], self.cn["sinC"][:, o:o + TW], writes=["cs2"])
                    B.ACT(xn[:, 0:TW], pb[:, 0:TW], AF.Copy, [pk], [xnk])
                    self.rope(xn[:, 0:TW], xnk, TW, self.pswC, "pswC_b", cs[:, 0:TW], sn[:, 0:TW], ["cs", "cs2"], KT[:, npast + o:npast + o + TW], ["KT"], ctx)
                else:
                    B.ACT(KT[:, o:o + TW], pb[:, 0:TW], AF.Copy, [pk], ["KT"])

            def v_chunk(c):
                pb, pk = B.ps("a")
                for k in range(8):
                    B.MM(pb[:, 0:128], U[:, k, c * 128:(c + 1) * 128], W[:, k, 256 + hp * 128:256 + (hp + 1) * 128], k == 0, k == 7, [wCk, ("u", k, (c * 128) // TW)], [pk])
                cc = npast // 128 + c
                B.CP(V[:, cc, 0, 0:64], pb[:, 0:64], [pk], ["V"])
                B.ACT(V[:, cc, 1, 64:128], pb[:, 64:128], AF.Copy, [pk], ["V"])
                if (not sample) and hp == 0:
                    pb2, pk2 = B.ps("a")
                    for k in range(8):
                        B.MM(pb2[:, 0:512], U[:, k, c * 128:(c + 1) * 128], W[:, k, 0:512], k == 0, k == 7, [wCk, ("u", k, (c * 128) // TW)], [pk2])
                    B.CP(stg[:, 0:512], pb2[:, 0:512], [pk2], ["r1"])
                    B.DMA("sp", self.o_ck[seq - 1, l, c * 128:(c + 1) * 128, :], stg[:, 0:256], reads=["r1"])
                    B.DMA("sp", self.o_cv[seq - 1, l, c * 128:(c + 1) * 128, :], stg[:, 256:512], reads=["r1"])

            vper = nch // ntile
            for t in range(ntile):
                k_tile(t)
                for c in range(t * vper, (t + 1) * vper):
                    v_chunk(c)
            if Wall is not None:
                W = Wall[:, :, QC0 + hp * 256:QC0 + (hp + 1) * 256]
            else:
                B.DMA("pool", W[:, :, 0:256], wv[:, :, QC0 + hp * 256:QC0 + (hp + 1) * 256], writes=["wC"])
            def c_stages(t, n):
                o = t * TW
                qouts = [(QT[0][n][0:64, 0:TW], [("qtc", 0, n)], slice(0, 64)), (QT[1][n][64:128, 0:TW], [("qtc", 1, n)], slice(64, 128))]
                stl = self.prep_stages(W, wCk, n * 128, U, o, TW, None, self.pswC, "pswC_b", cs, sn, qouts, ctx, False)
                if n == 0:
                    def ld(o=o):
                        B.DMA("sp", cs[:, 0:TW], self.cn["cosC"][:, o:o + TW], writes=["cs"])
                        B.DMA("sp", sn[:, 0:TW], self.cn["sinC"][:, o:o + TW], writes=["cs2"])
                    stl = [ld] + stl
                return stl
            if sample:
                for n in range(2):
                    for f in c_stages(0, n):
                        f()
            deferred = None
            for t in range(ntile):
                o = t * TW
                side = []
                last = []
                if deferred is not None:
                    side += [[1, deferred[0]], [5, deferred[1]], [9, deferred[2]]]
                    deferred = None
                if sample:
                    if t + 1 < ntile:
                        st0 = c_stages(t + 1, 0)
                        st1 = c_stages(t + 1, 1)
                        pos = 2 * nk + 2
                        for f in st0 + st1[:-1]:
                            side.append([pos, f])
                            pos += 3
                        last = [st1[-1]]
                else:
                    for n in range(2):
                        pb, pk = self.proj_fm(W, wCk, n * 128, 128, U, o, TW)
                        B.ACT(QT[0][n][0:64, 0:TW], pb[0:64, 0:TW], AF.Copy, [pk], [("qtc", 0, n)])
                        B.CP(QT[1][n][64:128, 0:TW], pb[64:128, 0:TW], [pk], [("qtc", 1, n)])
                for hi, (hl, n) in enumerate(((0, 0), (1, 0), (0, 1), (1, 1))):
                    acc, acck = self.attn_scores_pv(QT[hl][n], ("qtc", hl, n), hl, KT, "KT", nk, lambda c, hl=hl: V[:, c, hl, :], "V", TW, scale, PT, ctx, side=side, it0=hi * nk)
                    dst = r0 if n == 0 else r1
                    self.softmax_norm(acc, acck, hl, TW, dst[:, 0:TW], ["r%d" % n], rc, rc2)
                while side:
                    side.pop(0)[1]()
                for f in last:
                    f()
                def mk_fin(o=o):
                    sq0, rs_ = ctx["sq"][0], ctx["rs"]
                    st = {}

                    def f0():
                        B.STT(r0[:, 0:TW], r1[:, 0:TW], nlam, r0[:, 0:TW], ALU.mult, ALU.add, ["r0", "r1", "lamc"], ["r0"])
                        B.ACT(sq0[:, 0:TW], r0[:, 0:TW], AF.Square, ["r0"], [("sq", 0)])

                    def f1():
                        st["pd"], st["pdk"] = B.ps("b")
                        B.MM(st["pd"][:, 0:TW], self.bd[:], sq0[:, 0:TW], True, True, [("sq", 0), "bd"], [st["pdk"]])

                    def f2():
                        B.ACT(rs_[:, 0:TW], st["pd"][:, 0:TW], AF.Ln, [st["pdk"], "epsc"], ["rs"], bias=self.epsc[:, 0:1], scale=1.0 / 64)
                        B.ACT(rs_[:, 0:TW], rs_[:, 0:TW], AF.Exp, ["rs"], ["rs"], scale=-0.5)
                        B.STT(Y[:, 6 + hp, o:o + TW], r0[:, 0:TW], gcl, rs_[:, 0:TW], ALU.mult, ALU.mult, ["r0", "rs", "lamc"], [("y", 6 + hp)])
                    return [f0, f1, f2]
                fin = mk_fin()
                if sample and t + 1 < ntile:
                    deferred = fin
                else:
                    for f in fin:
                        f()
        self.pspools = saved_pools
        self.psidx = {}

    def group_B(self, ctx):
        B = self
        l, seq, off, L, sample, TW, ntile, nch, U, Y, wv = (ctx[k] for k in ("l", "seq", "off", "L", "sample", "TW", "ntile", "nch", "U", "Y", "wv"))
        sm = self.small
        nblk = TW // 128
        saved_pools = self.pspools
        self.pspools = {"a": [0, 1], "b": [2, 3], "c": [4, 5, 6, 7]}
        self.psidx = {}
        thrb = [self.carve(L // 2, BF16) for _ in range(2)]
        wcol = self.carve(nch * 8, F32).rearrange("p (c d h) -> p c d h", d=2, h=4)
        rsm = self.carve(64, F32)
        B.MEMSET(rsm, 0.0, ["rsm"])
        seldb = self.carve(128, BF16)
        B.DMA("pool", seldb[0:4, :], self.cn["selden"], writes=["seldb"])
        Wall = ctx["Wall"]
        wGk = "wall" if Wall is not None else "wG"
        wBk = "wall" if Wall is not None else "wB"
        p2off = self.aoff
        if Wall is not None:
            WG = Wall[:, :, GB0:GB0 + 16]
        else:
            WG = self.carve(64, BF16).rearrange("p (k n) -> p k n", n=16)
            B.DMA("pool", WG, wv[:, :, GB0:GB0 + 16], writes=["wG"])
        order = {0: list(range(ntile)), 1: list(range(ntile - 1, -1, -1))}
        ig = self.carve(L, F32)
        lp = self.carve(L, F32)
        th = self.carve(L, F32)
        B.MEMSET(sm[0:4, 48:49], 1.0, ["sm1"])
        for d in range(2):
            for t in range(ntile):
                o = t * TW
                pb, pk = self.proj_fm(WG, wGk, 8 * d, 4, U, o, TW)
                B.ACT(ig[0:4, o:o + TW], pb[0:4, 0:TW], AF.Identity, [pk, "bgs"], ["ig"], bias=self.bgs[:, 4 * l + 2 * d:4 * l + 2 * d + 1])
                pb, pk = self.proj_fm(WG, wGk, 8 * d + 4, 4, U, o, TW)
                B.ACT(lp[0:4, o:o + TW], pb[0:4, 0:TW], AF.Exp, [pk, "nbg"], ["lp"], bias=self.nbg[:, 4 * l + 2 * d + 1:4 * l + 2 * d + 2], scale=-1.0)
            B.ACT(lp[0:4, :], lp[0:4, :], AF.Ln, ["lp", "sm1"], ["lp"], bias=sm[0:4, 48:49])
            rv = (lambda a: a) if d == 0 else (lambda a: a[:, ::-1])
            onesb = sm[0:4, 48:49].to_broadcast([4, L])
            B.SCAN(rv(th[0:4, :]), onesb, rv(lp[0:4, :]), 0.0, ALU.mult, ALU.add, ["lp", "sm1"], ["th"])
            B.TT(ig[0:4, :], ig[0:4, :], th[0:4, :], ALU.add, ["ig", "th"], ["ig"])
            init = self.sbms[:, 2 * l + d:2 * l + d + 1] if sample else 0.0
            B.SCAN(rv(lp[0:4, :]), rv(ig[0:4, :]), rv(ig[0:4, :]), init, ALU.max, ALU.max, ["ig", "sbms"], ["lp"])
            if not sample:
                Rl = lp[0:4, L - 1:L] if d == 0 else lp[0:4, 0:1]
                nfl = th[0:4, L - 1:L] if d == 0 else th[0:4, 0:1]
                B.TT(sm[0:4, 56 + d:57 + d], Rl, nfl, ALU.subtract, ["lp", "th"], [("mfin", d)])
                col = (seq - 1) * 4 + l * 2 + d
                B.DMA("sp", self.o_bm[col].unsqueeze(1), sm[0:4, 56 + d:57 + d], reads=[("mfin", d)])
            for j in range(ntile):
                o = j * TW
                rb = 4 * (2 * j + d)
                Rcol = lp[0:4, o + TW - 1:o + TW] if d == 0 else lp[0:4, o:o + 1]
                B.TS(rsm[0:4, rb:rb + 1], Rcol, -1.0, None, ALU.mult, None, ["lp"], ["rsm"])
                B.TS(rsm[0:4, rb + 1:rb + 2], Rcol, -1.0, -math.log(8.0), ALU.mult, ALU.add, ["lp"], ["rsm"])
                B.ACT(thrb[d][0:4, o:o + TW], th[0:4, o:o + TW], AF.Exp, ["th", "rsm"], [("thrb", d)], bias=rsm[0:4, rb:rb + 1])
            if sample:
                j0 = order[d][0]
                rb0 = 4 * (2 * j0 + d)
                B.ACT(rsm[0:4, rb0 + 2:rb0 + 3], self.sbms[:, 2 * l + d:2 * l + d + 1], AF.Exp, ["sbms", "rsm"], ["rsm"], bias=rsm[0:4, rb0:rb0 + 1])
                for i in range(ntile - 1):
                    ja, jb = order[d][i], order[d][i + 1]
                    ra, rbb = 4 * (2 * ja + d), 4 * (2 * jb + d)
                    B.TT(rsm[0:4, ra + 3:ra + 4], rsm[0:4, rbb:rbb + 1], rsm[0:4, ra:ra + 1], ALU.subtract, ["rsm"], ["rsm"])
                    B.ACT(rsm[0:4, ra + 3:ra + 4], rsm[0:4, ra + 3:ra + 4], AF.Exp, ["rsm"], ["rsm"])
            for j in range(ntile):
                o = j * TW
                rb = 4 * (2 * j + d)
                B.ACT(th[0:4, o:o + TW], ig[0:4, o:o + TW], AF.Exp, ["ig", "rsm", ("thrb", d)], ["th"], bias=rsm[0:4, rb + 1:rb + 2])
                for c in range(o // 128, (o + TW) // 128):
                    pb, pk = B.ps("b")
                    B.TR(pb[:, 0:4], th[0:4, c * 128:(c + 1) * 128], self.ident[0:4, 0:4], ["th", "ident"], [pk])
                    B.CP(wcol[:, c, d, :], pb[:, 0:4], [pk], ["wcol"])
        self.barrier()
        self.aoff = p2off
        W = None if Wall is not None else self.carve(8 * 512 // 2, BF16).rearrange("p (k n) -> p k n", n=512)
        KT = self.carve(L // 2, BF16)
        V = self.carve(nch * 2 * 128 // 2, BF16).rearrange("p (c g n) -> p c g n", g=2, n=128)
        QT = [self.carve(256, BF16) for _ in range(2)]
        B.MEMSET(QT[0][64:128, :], 0.0, [("qtb", 0)])
        B.MEMSET(QT[1][0:64, :], 0.0, [("qtb", 1)])
        PT = [self.carve(256, BF16) for _ in range(2)]
        ptkeys = [("pt", 0), ("pt", 1)]
        thrt = self.carve(512, F32)
        thrt2 = self.carve(512, F32)
        thrtb = [(thrt, "thrt"), (thrt2, "thrt2")]
        hf, hb = ctx["tmp"][0], ctx["tmp"][1]
        hfk, hbk = ("tmp", 0), ("tmp", 1)
        nkt = nblk if sample else nch
        ktok_raw = self.carve(nkt * 128 // 2, BF16)
        ktok = ktok_raw.rearrange("p (c n) -> p c n", n=128)
        vw = [self.carve(64, BF16)] * 2
        if sample:
            c0s_raw = self.carve(2 * 128, F32)
            c0s = c0s_raw.rearrange("p (d n) -> p d n", d=2)
            PT += [ktok_raw, c0s_raw.bitcast(BF16)]
            ptkeys += ["ktok", "c0s"]
            Sst = self.carve(128, F32)
            Slh = self.carve(2 * ntile * 64, BF16).rearrange("p (d j n) -> p d j n", d=2, j=ntile)
        else:
            PT += [self.carve(256, BF16) for _ in range(2)]
            ptkeys += [("pt", 2), ("pt", 3)]
            vwb = self.carve(nch * 2 * 2 * 66 // 2, BF16).rearrange("p (c d h n) -> p c d h n", d=2, h=2, n=66)
        gb = self.gc[:, 4 * l + 2:4 * l + 3]
        B.MEMSET(V[:, :, 0, 64:128], 1.0, ["V"])
        B.MEMSET(V[:, :, 1, 0:64], 1.0, ["V"])
        for m in range(2):
            if Wall is not None:
                W = Wall[:, :, QB0 + m * 512:QB0 + (m + 1) * 512]
            else:
                B.DMA("pool", W, wv[:, :, QB0 + m * 512:QB0 + (m + 1) * 512], writes=["wB"])
            for t in range(ntile):
                o = t * TW
                pb, pk = self.proj_fm(W, wBk, 128, 128, U, o, TW)
                B.ACT(KT[:, o:o + TW], pb[:, 0:TW], AF.Copy, [pk], ["KTb"])
            for c in range(nch):
                pb, pk = B.ps("a")
                for k in range(8):
                    B.MM(pb[:, 0:256], U[:, k, c * 128:(c + 1) * 128], W[:, k, 128:384], k == 0, k == 7, [wBk, ("u", k, (c * 128) // TW)], [pk])
                B.CP(V[:, c, 0, 0:64], pb[:, 128:192], [pk], ["V"])
                B.ACT(V[:, c, 1, 64:128], pb[:, 192:256], AF.Copy, [pk], ["V"])
                if not sample:
                    B.ACT(ktok[:, c, :], pb[:, 0:128], AF.Copy, [pk], ["ktok"])
                    for d in range(2):
                        for hl in range(2):
                            hh = 2 * m + hl
                            B.TS(vwb[:, c, d, hl, 0:64], pb[:, 128 + hl * 64:192 + hl * 64], wcol[:, c, d, hh:hh + 1], None, ALU.mult, None, [pk, "wcol"], ["vwb"])
                            B.CP(vwb[:, c, d, hl, 64:65], wcol[:, c, d, hh:hh + 1], ["wcol"], ["vwb"])
            if not sample:
                for d in range(2):
                    for hl in range(2):
                        hh = 2 * m + hl
                        pb, pk = B.ps("b")
                        for c in range(nch):
                            B.MM(pb[0:64, 0:65], ktok[:, c, hl * 64:(hl + 1) * 64], vwb[:, c, d, hl, 0:65], c == 0, c == nch - 1, ["ktok", "vwb"], [pk])
                        B.CP(thrt[0:64, 0:65], pb[0:64, 0:65], [pk], ["thrt"])
                        B.DMA("sp", self.o_bC[seq - 1, l, d, hh], thrt[0:64, 0:64], reads=["thrt"])
                        B.DMA("sp", self.o_bn[seq - 1, l, d, hh].unsqueeze(1), thrt[0:64, 64:65], reads=["thrt"])
            else:
                for d in range(2):
                    B.DMA("sp", c0s[:, d, 0:64], self.sbC[l, d, m], writes=["c0s"])
                    B.DMA("sp", c0s[:, d, 64:65], self.sbn[l, d, m], writes=["c0s"])
                    B.TS(c0s[:, d, 65:128], c0s[:, d, 64:65].to_broadcast([128, 63]), 1.0, None, ALU.mult, None, ["c0s"], ["c0s"])
                    j0 = order[d][0]
                    rb0 = 4 * (2 * j0 + d)
                    pb, pk = B.ps("b")
                    B.MM(pb[:, 0:2], self.selnum(m), rsm[0:4, rb0 + 2:rb0 + 4], True, True, ["selnum", "rsm"], [pk])
                    B.CP(sm[:, 60:61], pb[:, 0:1], [pk], ["e0c"])
                    B.TS(Sst[0:64, :], c0s[0:64, d, :], sm[0:64, 60:61], None, ALU.mult, None, ["c0s", "e0c"], ["Sst"])
                    B.TS(Sst[64:128, 64:128], c0s[64:128, d, 0:64], sm[64:128, 60:61], None, ALU.mult, None, ["c0s", "e0c"], ["Sst"])
                    B.TS(Sst[64:128, 0:64], c0s[64:128, d, 64:128], sm[64:128, 60:61], None, ALU.mult, None, ["c0s", "e0c"], ["Sst"])
                    for i, j in enumerate(order[d]):
                        B.CP(Slh[:, d, j, :], Sst[:, :], ["Sst"], [("Slh", d, j)])
                        if i == ntile - 1:
                            break
                        rb = 4 * (2 * j + d)
                        for ci in range(nblk):
                            c = j * nblk + ci
                            pb, pk = B.ps("a")
                            for k in range(8):
                                B.MM(pb[:, 0:128], U[:, k, c * 128:(c + 1) * 128], W[:, k, 128:256], k == 0, k == 7, [wBk, ("u", k, j)], [pk])
                            B.ACT(ktok[:, ci, :], pb[:, 0:128], AF.Copy, [pk], ["ktok"])
                        pst, pstk = B.ps("c")
                        vi = 0
                        for hl in range(2):
                            hh = 2 * m + hl
                            for ci in range(nblk):
                                c = j * nblk + ci
                                v_ = vw[0]
                                vk = ("vw", 0)
                                vi += 1
                                B.TS(v_[:, :], V[:, c, hl, :], wcol[:, c, d, hh:hh + 1], None, ALU.mult, None, ["V", "wcol"], [vk])
                                B.MM(pst[:, hl * 128:(hl + 1) * 128], ktok[:, ci, :], v_[:, :], ci == 0, ci == nblk - 1, ["ktok", vk], [pstk])
                        pf, pfk = B.ps("b")
                        B.MM(pf[:, 0:2], self.selnum(m), rsm[0:4, rb + 2:rb + 4], True, True, ["selnum", "rsm"], [pfk])
                        B.CP(sm[:, 61:62], pf[:, 1:2], [pfk], ["facc"])
                        for hl in range(2):
                            ps_ = slice(hl * 64, hl * 64 + 64)
                            B.TT(Sst[ps_, :], Sst[ps_, :], pst[ps_, hl * 128:(hl + 1) * 128], ALU.add, ["Sst", pstk], ["Sst"])
                        B.TS(Sst[:, :], Sst[:, :], sm[:, 61:62], None, ALU.mult, None, ["Sst", "facc"], ["Sst"])
            for t in range(ntile):
                o = t * TW
                pb, pk = self.proj_fm(W, wBk, 0, 128, U, o, TW)
                B.ACT(QT[0][0:64, 0:TW], pb[0:64, 0:TW], AF.Copy, [pk], [("qtb", 0)])
                B.CP(QT[1][64:128, 0:TW], pb[64:128, 0:TW], [pk], [("qtb", 1)])
                chs = list(range(t * nblk, (t + 1) * nblk)) if sample else list(range(nch))
                accs = {}
                for hl in range(2):
                    hh = 2 * m + hl
                    accf, acfk = B.ps("c")
                    accb, acbk = B.ps("c")
                    accs[hl] = (accf, acfk, accb, acbk)
                    first = {0: True, 1: True}
                    if sample:
                        for d, (acc, ak) in enumerate(((accf, acfk), (accb, acbk))):
                            B.MM(acc[:, 0:TW], Slh[:, d, t, :], QT[hl][:, 0:TW], True, False, [("Slh", d, t), ("qtb", hl)], [ak])
                        first = {0: False, 1: False}
                    pti = 0
                    pend = []
                    for c in chs + [None]:
                        if c is not None:
                            sp_, spk = B.ps("a")
                            B.MM(sp_[:, 0:TW], KT[:, c * 128:(c + 1) * 128], QT[hl][:, 0:TW], True, True, ["KTb", ("qtb", hl)], [spk])
                            rel = c - t * nblk
                            for d, acc, ak, mk, mkk in ((0, accf, acfk, self.mkf, "mkf_b"), (1, accb, acbk, self.mkb, "mkb_b")):
                                pt = PT[pti % 4]
                                ptk = ptkeys[pti % 4]
                                pti += 1
                                wc = wcol[:, c, d, hh:hh + 1]
                                mo = 384 - 128 * rel
                                lo, hi = 0, TW
                                if sample:
                                    lo, hi = (128 * rel, TW) if d == 0 else (0, 128 * (rel + 1))
                                B.STT(pt[:, lo:hi], sp_[:, lo:hi], wc, mk[:, mo + lo:mo + hi], ALU.mult, ALU.mult, [spk, "wcol", mkk], [ptk])
                                pend.append((c, d, acc, ak, pt, ptk, lo, hi))
                        while pend and (c is None or pend[0][0] < c):
                            cc, d, acc, ak, pt, ptk, lo, hi = pend.pop(0)
                            B.MM(acc[:, lo:hi], V[:, cc, hl, :], pt[:, lo:hi], first[d], cc == chs[-1], ["V", ptk], [ak])
                            first[d] = False
                thrp = {}
                for d in range(2):
                    pbt, pbk = B.ps("b")
                    B.MM(pbt[:, 0:TW], seldb[0:4, m * 128:(m + 1) * 128], thrb[d][0:4, o:o + TW], True, True, ["seldb", ("thrb", d)], [pbk])
                    thrp[d] = (pbt, pbk)
                for hl in range(2):
                    hs = slice(hl * 64, hl * 64 + 64)
                    ds = slice((1 - hl) * 64, (1 - hl) * 64 + 64)
                    accf, acfk, accb, acbk = accs[hl]
                    chains = ((accf, acfk, hf, hfk), (accb, acbk, hb, hbk))
                    for d, (acc, ak, dst, dk) in enumerate(chains):
                        tb, tk = thrtb[d]
                        pbt, pbk = thrp[d]
                        B.ACT(tb[ds, 0:TW], pbt[ds, 0:TW], AF.Copy, [pbk], [tk])
                    for d, (acc, ak, dst, dk) in enumerate(chains):
                        tb, tk = thrtb[d]
                        B.TT(tb[ds, 0:TW], acc[ds, 0:TW], tb[ds, 0:TW], ALU.max, [ak, tk], [tk])
                        B.STT(tb[ds, 0:TW], acc[ds, 0:TW], -1.0, tb[ds, 0:TW], ALU.mult, ALU.max, [ak, tk], [tk])
                    for d, (acc, ak, dst, dk) in enumerate(chains):
                        tb, tk = thrtb[d]
                        B.ACT(tb[ds, 0:TW], tb[ds, 0:TW], AF.Ln, [tk], [tk])
                        B.ACT(tb[ds, 0:TW], tb[ds, 0:TW], AF.Exp, [tk], [tk], scale=-1.0)
                    for d, (acc, ak, dst, dk) in enumerate(chains):
                        tb, tk = thrtb[d]
                        B.CP(tb[hs, 0:TW], tb[ds, 0:TW], [tk], [tk])
                        B.TT(dst[hs, 0:TW], acc[hs, 0:TW], tb[hs, 0:TW], ALU.mult, [ak, tk], [dk])
                    B.TT(hf[hs, 0:TW], hf[hs, 0:TW], hb[hs, 0:TW], ALU.add, [hfk, hbk], [hfk])
                self.headnorm(hf[:, 0:TW], hfk, TW, gb, [(hb[:, 0:TW], [hbk])], ctx)
                pb, pk = self.proj_fm(W, wBk, 384, 128, U, o, TW)
                B.ACT(thrt[:, 0:TW], pb[:, 0:TW], AF.Sigmoid, [pk], ["thrt"])
                B.TT(Y[:, 4 + m, o:o + TW], hb[:, 0:TW], thrt[:, 0:TW], ALU.mult, [hbk, "thrt"], [("y", 4 + m)])
        self.pspools = saved_pools
        self.psidx = {}

    def selnum(self, m):
        return self.selnum_t[:, m * 128:(m + 1) * 128]


_CACHE = {}


def _get_nc():
    if "nc" not in _CACHE:
        b = Builder()
        _CACHE["nc"] = b.build()
    return _CACHE["nc"]


def kernel(x_prompt, x_sample, cache_a_k, cache_a_v, cache_c_k, cache_c_v, state_b_C, state_b_n,
           state_b_m, c, c_ctx, w_ada, b_ada, g_norm, w_ff_in, w_ff_out, w_in, w_out, g_qa, g_ka,
           b_gates, g_b, lam_q1, lam_k1, lam_q2, lam_k2, g_c, g_final):
    f = lambda a: np.ascontiguousarray(np.asarray(a, dtype=np.float32))
    x_prompt, x_sample = f(x_prompt), f(x_sample)
    consts = _consts()
    shared = {}
    shared["w_ada"] = f(w_ada)
    shared["b_ada"] = f(np.asarray(b_ada).reshape(2, 72, 128).transpose(0, 2, 1))
    shared["g_norm"] = f(np.asarray(g_norm).reshape(6, 8, 128).transpose(2, 0, 1).reshape(128, 48))
    shared["g_final"] = f(np.asarray(g_final).reshape(8, 128).T)
    shared["w_ff_in"] = f(w_ff_in)
    shared["w_ff_out"] = f(w_ff_out)
    shared["w_in"] = f(np.stack([_perm_w_in(np.asarray(w_in[l])) for l in range(2)]))
    shared["w_out"] = f(np.stack([_perm_w_out(np.asarray(w_out[l])) for l in range(2)]))
    gcols = np.zeros((128, 8), np.float32)
    for l in range(2):
        for i, g in enumerate((g_qa, g_ka, g_b, g_c)):
            gcols[:, 4 * l + i] = np.tile(np.asarray(g[l]), 2)
    shared["gcols"] = gcols
    bgv = np.zeros((4, 8), np.float32)
    for l in range(2):
        bgv[:, 4 * l:4 * l + 4] = np.asarray(b_gates[l]).reshape(4, 4).T
    shared["bg"] = bgv
    lamv = np.zeros((128, 256), np.float32)
    for l in range(2):
        for i, v in enumerate((lam_q1, lam_k1, lam_q2, lam_k2)):
            lamv[:, (l * 4 + i) * 32:(l * 4 + i + 1) * 32] = np.asarray(v[l])[None, :]
    shared["lamv"] = lamv
    for k, v in consts.items():
        shared["c_" + k] = f(v)
    in_maps = []
    for core in range(NCORES):
        b = SAMPLE_OF_CORE[core]
        real = b is not None
        b = 0 if b is None else b
        zl = (lambda a: np.zeros_like(a)) if not real else (lambda a: a)
        m = dict(shared)
        m["xin"] = f(np.concatenate([zl(x_sample[b]), x_prompt[2 * core], x_prompt[2 * core + 1]], axis=0))
        cv = np.stack([np.asarray(c[b]), np.asarray(c_ctx)], axis=-1)
        m["cvec"] = f(cv.reshape(8, 128, 2).transpose(1, 0, 2))
        m["cak"] = f(zl(np.asarray(cache_a_k[b])).reshape(2, PAST, 128))
        m["cav"] = f(zl(np.asarray(cache_a_v[b])).reshape(2, PAST, 128))
        m["cck"] = f(zl(np.asarray(cache_c_k[b])).reshape(2, PAST, 256))
        m["ccv"] = f(zl(np.asarray(cache_c_v[b])).reshape(2, PAST, 256))
        sC = zl(np.asarray(state_b_C[b]))
        m["sbC"] = f(sC.reshape(2, 2, 2, 2, 64, 64).reshape(2, 2, 2, 128, 64))
        sn = zl(np.asarray(state_b_n[b]))
        m["sbn"] = f(sn.reshape(2, 2, 2, 128, 1))
        sm_ = zl(np.asarray(state_b_m[b]))
        m["sbm"] = f(sm_.transpose(2, 0, 1).reshape(4, 4))
        in_maps.append(m)
    nc = _get_nc()
    res = run_bass_kernel_spmd(nc, in_maps, core_ids=list(range(NCORES)))
    R = res.results
    y_prompt = np.zeros((16, LP, D), np.float32)
    y_sample = np.zeros((4, LS, D), np.float32)
    nak = np.zeros((16, 2, LP, 2, 64), np.float32)
    nav = np.zeros((16, 2, LP, 2, 64), np.float32)
    nck = np.zeros((16, 2, LP, 4, 2, 32), np.float32)
    ncv = np.zeros((16, 2, LP, 4, 64), np.float32)
    nbC = np.zeros((16, 2, 2, 4, 64, 64), np.float32)
    nbn = np.zeros((16, 2, 2, 4, 64), np.float32)
    nbm = np.zeros((16, 2, 2, 4), np.float32)
    for core in range(NCORES):
        r = R[core]
        oy = np.asarray(r["o_y"])
        if SAMPLE_OF_CORE[core] is not None:
            y_sample[SAMPLE_OF_CORE[core]] = oy[0:LS]
        for s in range(2):
            bp = 2 * core + s
            y_prompt[bp] = oy[LS + s * LP:LS + (s + 1) * LP]
            nak[bp] = np.asarray(r["o_ak"])[s].reshape(2, LP, 2, 64)
            nav[bp] = np.asarray(r["o_av"])[s].reshape(2, LP, 2, 64)
            nck[bp] = np.asarray(r["o_ck"])[s].reshape(2, LP, 4, 2, 32)
            ncv[bp] = np.asarray(r["o_cv"])[s].reshape(2, LP, 4, 64)
            nbC[bp] = np.asarray(r["o_bC"])[s]
            nbn[bp] = np.asarray(r["o_bn"])[s]
            nbm[bp] = np.asarray(r["o_bm"]).reshape(2, 2, 2, 4)[s]
    return (y_prompt, y_sample, nak, nav, nck, ncv, nbC, nbn, nbm)
```

```python
import math
from contextlib import ExitStack
import numpy as np
import ml_dtypes
import concourse.bass as bass
import concourse.mybir as mybir
from concourse.bass_utils import run_bass_kernel_spmd

F32 = mybir.dt.float32
BF16 = mybir.dt.bfloat16
AF = mybir.ActivationFunctionType
ALU = mybir.AluOpType

D = 1024
DFF = 2816
NT = 2560
LS = 2048
LP = 256
PAST = 512
EPS = 1e-6
NCORES = 8
FLAGS = {"A": True, "B": True, "C": True, "FFN": True, "LAYERS": 2}
SAMPLE_OF_CORE = [0, 1, None, None, 2, 3, None, None]

QA0, KA0, VA0 = 0, 512, 640
QB0, KB0, VB0, OB0, GB0 = 768, 1024, 1280, 1536, 1792
QC0, KC0, VC0 = 1808, 2320, 2576
NWALL = 2832
PK0 = 2832
NW = 2832 + 1024


class _Op:
    __slots__ = ("eng", "fn", "deps", "sig", "val", "dma", "dsem", "dval", "waits", "ph")

    def __init__(self, eng, fn, deps, dma):
        self.eng = eng
        self.fn = fn
        self.deps = deps
        self.sig = False
        self.val = 0
        self.dma = dma
        self.dsem = None
        self.dval = 0
        self.waits = None


class Prog:
    ENGS = ("pe", "act", "dve", "pool", "sp")
    NDMA = 24

    def __init__(self):
        self.ops = []
        self.track = {}
        self.dma_rr = 0
        self.dma_rr2 = [0, 0]
        self.phase = "init"
        self.dma_last = [None] * self.NDMA
        self.dma_cnt = [0] * self.NDMA
        self.bar = None

    def emit(self, eng, fn, reads=(), writes=(), dma=False):
        deps = set()
        for k in reads:
            t = self.track.get(k)
            if t is not None and t[0] is not None:
                deps.add(t[0])
        for k in writes:
            t = self.track.get(k)
            if t is not None:
                if t[0] is not None:
                    deps.add(t[0])
                deps.update(t[1].values())
                deps.update(t[2])
        if self.bar is not None:
            deps.add(self.bar)
        oid = len(self.ops)
        op = _Op(eng, fn, deps, dma)
        op.ph = self.phase
        if dma:
            half = self.NDMA // 2
            base = 0 if eng == "sp" else half
            r = self.dma_rr2[eng != "sp"]
            self.dma_rr2[eng != "sp"] = (r + 1) % half
            i = base + r
            if self.dma_last[i] is not None:
                deps.add(self.dma_last[i])
            self.dma_last[i] = oid
            self.dma_cnt[i] += 16
            op.dsem = i
            op.dval = self.dma_cnt[i]
        if eng == "pe" and not dma:
            op.deps = set(d for d in deps if not (self.ops[d].eng == "pe" and not self.ops[d].dma))
        self.ops.append(op)
        for k in reads:
            t = self.track.setdefault(k, [None, {}, []])
            if dma:
                t[2].append(oid)
            else:
                t[1][eng] = oid
        for k in writes:
            self.track[k] = [oid, {}, []]
        return oid

    def finalize(self):
        for op in self.ops:
            for d in op.deps:
                self.ops[d].sig = True
        cnt = {e: 0 for e in self.ENGS}
        for op in self.ops:
            if op.dma:
                continue
            if op.sig:
                cnt[op.eng] += 1
                op.val = cnt[op.eng]
        waited = {e: {} for e in self.ENGS}
        for op in self.ops:
            w = {}
            for d in op.deps:
                dop = self.ops[d]
                if dop.dma:
                    key = ("dma", dop.dsem)
                    v = dop.dval
                else:
                    key = ("eng", dop.eng)
                    v = dop.val
                if waited[op.eng].get(key, 0) >= v:
                    continue
                if w.get(key, 0) < v:
                    w[key] = v
            for key, v in w.items():
                waited[op.eng][key] = v
            op.waits = list(w.items())

    def replay(self, nc, esems, dsems):
        self.finalize()
        fw = [(i, self.dma_cnt[i]) for i in range(self.NDMA) if self.dma_cnt[i] > 0]
        handles = {"pe": "tensor", "act": "scalar", "dve": "vector", "pool": "gpsimd", "sp": "sync"}
        with nc.Block() as block:
            for en in self.ENGS:
                myops = [op for op in self.ops if op.eng == en]

                def body(e, myops=myops, en=en):
                    for op in myops:
                        for key, v in op.waits:
                            sem = dsems[key[1]] if key[0] == "dma" else esems[key[1]]
                            e.wait_ge(sem, v)
                        inst = op.fn(e)
                        if op.dma:
                            inst.then_inc(dsems[op.dsem], 16)
                        elif op.sig:
                            inst.then_inc(esems[en], 1)
                    if en == "sp":
                        for i, v in fw:
                            e.wait_ge(dsems[i], v)

                getattr(block, handles[en])(body)


def _rope_tables():
    def tab(dim):
        t = np.arange(LS)
        row = (t // 64).astype(np.float32)
        col = (t % 64).astype(np.float32)
        axis_dim = dim // 2
        freqs = (10000.0 ** (-np.arange(0, axis_dim, 2, dtype=np.float32) / axis_dim)).astype(np.float32)
        ang = np.concatenate([row[:, None] * freqs, col[:, None] * freqs], axis=-1)
        return np.cos(ang).astype(np.float32), np.sin(ang).astype(np.float32)
    cA, sA = tab(64)
    cC, sC = tab(32)
    cosA = np.zeros((128, LS), np.float32)
    sinA = np.zeros((128, LS), np.float32)
    cosC = np.ones((128, LS), np.float32)
    sinC = np.zeros((128, LS), np.float32)
    for p in range(128):
        d = p % 64
        i = d % 32
        cosA[p] = cA[:, i]
        sinA[p] = -sA[:, i] if d < 32 else sA[:, i]
        d2 = p % 32
        j = d2 % 16
        cosC[p] = cC[:, j]
        sinC[p] = -sC[:, j] if d2 < 16 else sC[:, j]
    return cosA, sinA, cosC, sinC


def _consts():
    c = {}
    c["ident"] = np.eye(128, dtype=np.float32)
    c["ones"] = np.ones((128, 128), np.float32)
    bd = np.zeros((128, 128), np.float32)
    bd[:64, :64] = 1.0
    bd[64:, 64:] = 1.0
    c["bd"] = bd
    pa = np.zeros((128, 128), np.float32)
    pc = np.zeros((128, 128), np.float32)
    for m in range(128):
        d = m % 64
        base = m - d
        pa[base + (d ^ 32), m] = 1.0
        pc[m ^ 16, m] = 1.0
    c["pswA"] = pa
    c["pswC"] = pc
    s = np.arange(128)[:, None]
    x = np.arange(896)[None, :]
    c["mkf"] = ((x - 384) >= s).astype(np.float32)
    c["mkb"] = ((x - 384) <= s).astype(np.float32)
    sel = np.zeros((2, 4, 128), np.float32)
    for m in range(2):
        sel[m, 2 * m + 1, :64] = 1.0
        sel[m, 2 * m, 64:] = 1.0
    c["selden"] = sel.transpose(1, 0, 2).reshape(4, 256).copy()
    sel2 = np.zeros((2, 4, 128), np.float32)
    for m in range(2):
        sel2[m, 2 * m, :64] = 1.0
        sel2[m, 2 * m + 1, 64:] = 1.0
    c["selnum"] = sel2.transpose(1, 0, 2).reshape(4, 256).copy()
    cosA, sinA, cosC, sinC = _rope_tables()
    c["cosA"], c["sinA"], c["cosC"], c["sinC"] = cosA, sinA, cosC, sinC
    return c


def _perm_w_in(w_in_l):
    W = np.zeros((D, NW), np.float32)
    for m in range(4):
        W[:, QA0 + m * 128: QA0 + m * 128 + 64] = w_in_l[:, m * 64:(m + 1) * 64]
        W[:, QA0 + m * 128 + 64: QA0 + (m + 1) * 128] = w_in_l[:, (4 + m) * 64:(5 + m) * 64]
    W[:, KA0:KA0 + 128] = w_in_l[:, 512:640]
    W[:, VA0:VA0 + 128] = w_in_l[:, 640:768]
    for m in range(2):
        for i in range(4):
            W[:, QB0 + m * 512 + i * 128: QB0 + m * 512 + (i + 1) * 128] = w_in_l[:, 768 + i * 256 + m * 128: 768 + i * 256 + (m + 1) * 128]
    W[:, GB0:GB0 + 16] = w_in_l[:, 1792:1808]
    for h in range(4):
        for n in range(2):
            src = (h * 2 + n) * 32
            hp, hl = h // 2, h % 2
            dst = QC0 + hp * 256 + n * 128 + (hl * 2 + n) * 32
            W[:, dst: dst + 32] = w_in_l[:, 1808 + src: 1808 + src + 32]
    W[:, KC0:KC0 + 256] = w_in_l[:, 2064:2320]
    W[:, VC0:VC0 + 256] = w_in_l[:, 2320:2576]
    for hp in range(2):
        b0 = PK0 + hp * 512
        W[:, b0:b0 + 128] = W[:, KC0 + hp * 128:KC0 + (hp + 1) * 128]
        W[:, b0 + 128:b0 + 256] = W[:, VC0 + hp * 128:VC0 + (hp + 1) * 128]
        W[:, b0 + 256:b0 + 512] = W[:, QC0 + hp * 256:QC0 + (hp + 1) * 256]
    return W


def _perm_w_out(w_out_l):
    idx = []
    for m in range(4):
        idx += list(range(m * 64, (m + 1) * 64)) + list(range((4 + m) * 64, (5 + m) * 64))
    idx += list(range(512, 1024))
    return np.ascontiguousarray(w_out_l[idx, :])


class Builder:
    def __init__(self):
        self.nc = bass.Bass("TRN2", target_bir_lowering=False)
        self.P = Prog()
        self.psrr = 0
        self.ptkeys = None

    def dram_in(self, name, shape, dt=F32):
        return self.nc.dram_tensor(name, list(shape), dt, kind="ExternalInput").ap()

    def dram_out(self, name, shape):
        return self.nc.dram_tensor(name, list(shape), F32, kind="ExternalOutput").ap()

    def E(self, eng, fn, reads=(), writes=()):
        return self.P.emit(eng, fn, reads, writes)

    def DMA(self, q, out, in_, reads=(), writes=()):
        return self.P.emit(q, lambda e, o=out, i=in_: e.dma_start(out=o, in_=i), reads, writes, dma=True)

    def MM(self, out, lhsT, rhs, start, stop, reads, writes):
        return self.P.emit("pe", lambda e, o=out, l=lhsT, r=rhs, s=start, t=stop: e.matmul(o, lhsT=l, rhs=r, start=s, stop=t), reads, writes)

    def TR(self, out, in_, ident, reads, writes):
        return self.P.emit("pe", lambda e, o=out, i=in_, d=ident: e.transpose(out=o, in_=i, identity=d), reads, writes)

    def ACT(self, out, in_, func, reads, writes, bias=None, scale=None):
        kw = {}
        if bias is not None:
            kw["bias"] = bias
        if scale is not None:
            kw["scale"] = scale
        return self.P.emit("act", lambda e, o=out, i=in_, f=func, kw=kw: e.activation(out=o, in_=i, func=f, **kw), reads, writes)

    def TT(self, out, in0, in1, op, reads, writes, eng="dve"):
        return self.P.emit(eng, lambda e, o=out, a=in0, b=in1, p=op: e.tensor_tensor(out=o, in0=a, in1=b, op=p), reads, writes)

    def TS(self, out, in0, s1, s2, op0, op1, reads, writes):
        if op1 is None:
            return self.P.emit("dve", lambda e, o=out, a=in0, x=s1, p=op0: e.tensor_scalar(out=o, in0=a, scalar1=x, scalar2=None, op0=p), reads, writes)
        return self.P.emit("dve", lambda e, o=out, a=in0, x=s1, y=s2, p=op0, q=op1: e.tensor_scalar(out=o, in0=a, scalar1=x, scalar2=y, op0=p, op1=q), reads, writes)

    def STT(self, out, in0, scalar, in1, op0, op1, reads, writes):
        return self.P.emit("dve", lambda e, o=out, a=in0, s=scalar, b=in1, p=op0, q=op1: e.scalar_tensor_tensor(out=o, in0=a, scalar=s, in1=b, op0=p, op1=q), reads, writes)

    def CP(self, out, in_, reads, writes, eng="dve"):
        return self.P.emit(eng, lambda e, o=out, i=in_: e.tensor_copy(out=o, in_=i), reads, writes)

    def RCP(self, out, in_, reads, writes):
        return self.P.emit("dve", lambda e, o=out, i=in_: e.reciprocal(out=o, in_=i), reads, writes)

    def RCPF(self, out, in_, reads, writes):
        return self.P.emit("dve", lambda e, o=out, i=in_: e.reciprocal_approx_fast(out=o, in_=i), reads, writes)

    def MEMSET(self, ap, val, writes, eng="dve"):
        return self.P.emit(eng, lambda e, a=ap, v=val: e.memset(a, v), (), writes)

    def SCAN(self, out, d0, d1, init, op0, op1, reads, writes):
        return self.P.emit("dve", lambda e, o=out, a=d0, b=d1, i=init, p=op0, q=op1: e.tensor_tensor_scan(out=o, data0=a, data1=b, initial=i, op0=p, op1=q), reads, writes)

    def ps(self, pool):
        lst = self.pspools[pool]
        i = self.psidx.get(pool, 0)
        self.psidx[pool] = (i + 1) % len(lst)
        b = lst[i]
        return self.psum[b], ("ps", b)

    def build(self):
        nc = self.nc
        di = self.dram_in
        self.xin = di("xin", [NT, D])
        self.cvec = di("cvec", [128, 8, 2])
        self.w_ada = di("w_ada", [2, D, 9 * D])
        self.b_ada = di("b_ada", [2, 128, 72])
        self.g_norm = di("g_norm", [128, 2 * 3 * 8])
        self.g_final = di("g_final", [128, 8])
        self.w_ff_in = di("w_ff_in", [2, 2, D, 2 * DFF])
        self.w_ff_out = di("w_ff_out", [2, 2, DFF, D])
        self.w_in = di("w_in", [2, D, NW])
        self.w_out = di("w_out", [2, D, D])
        self.gcols = di("gcols", [128, 2 * 4])
        self.bg = di("bg", [4, 2 * 4])
        self.lamv = di("lamv", [128, 2 * 4 * 32])
        self.cak = di("cak", [2, PAST, 128])
        self.cav = di("cav", [2, PAST, 128])
        self.cck = di("cck", [2, PAST, 256])
        self.ccv = di("ccv", [2, PAST, 256])
        self.sbC = di("sbC", [2, 2, 2, 128, 64])
        self.sbn = di("sbn", [2, 2, 2, 128, 1])
        self.sbm = di("sbm", [4, 2 * 2])
        cn = {}
        for k, shp in (("ident", [128, 128]), ("ones", [128, 128]), ("bd", [128, 128]), ("pswA", [128, 128]),
                       ("pswC", [128, 128]), ("mkf", [128, 896]), ("mkb", [128, 896]), ("selden", [4, 256]), ("selnum", [4, 256]),
                       ("cosA", [128, LS]), ("sinA", [128, LS]), ("cosC", [128, LS]), ("sinC", [128, LS])):
            cn[k] = di("c_" + k, shp)
        self.cn = cn
        do = self.dram_out
        self.o_y = do("o_y", [NT, D])
        self.o_ak = do("o_ak", [2, 2, LP, 128])
        self.o_av = do("o_av", [2, 2, LP, 128])
        self.o_ck = do("o_ck", [2, 2, LP, 256])
        self.o_cv = do("o_cv", [2, 2, LP, 256])
        self.o_bC = do("o_bC", [2, 2, 2, 4, 64, 64])
        self.o_bn = do("o_bn", [2, 2, 2, 4, 64])
        self.o_bm = do("o_bm", [8, 4])

        with ExitStack() as st:
            sb = lambda n, s, d: st.enter_context(nc.sbuf_tensor(n, s, d))
            self.h = sb("h", [128, 8, NT], F32)
            AW = 29696
            self.arena = sb("arena", [128, AW], F32)
            self.ident = sb("ident", [128, 128], F32)
            self.ones = sb("ones", [128, 128], F32)
            self.bd = sb("bd", [128, 128], F32)
            self.pswA = sb("pswA", [128, 128], BF16)
            self.pswC = sb("pswC", [128, 128], BF16)
            self.mkf = sb("mkf", [128, 896], BF16)
            self.mkb = sb("mkb", [128, 896], BF16)
            self.selden = sb("selden", [4, 256], F32)
            self.selnum_t = sb("selnum", [4, 256], F32)
            self.modc = sb("modc", [128, 2, 72, 2], F32)
            self.nsc = sb("nsc", [128, 2, 3, 8, 2], F32)
            self.gt = sb("gt", [128, 2, 3, 8, 2], F32)
            self.gn = sb("gn", [128, 48], F32)
            self.gfin = sb("gfin", [128, 8], F32)
            self.bada = sb("bada", [128, 2, 72], F32)
            self.gc = sb("gc", [128, 8], F32)
            self.bgs = sb("bgs", [4, 8], F32)
            self.nbg = sb("nbg", [4, 8], F32)
            self.lamc = sb("lamc", [128, 8], F32)
            self.sbms = sb("sbms", [4, 4], F32)
            self.cv = sb("cv", [128, 8, 2], F32)
            self.cvb = sb("cvb", [128, 8, 2], BF16)
            self.small = sb("small", [128, 64], F32)
            self.psum = [st.enter_context(nc.psum_tensor("ps%d" % i, [128, 512], F32)) for i in range(8)]
            self.pspools = {"a": [0, 1, 2], "b": [3, 4], "c": [5, 6, 7]}
            self.psidx = {}
            esems = {e: st.enter_context(nc.semaphore("s_" + e)) for e in Prog.ENGS}
            dsems = [st.enter_context(nc.semaphore("d%d" % i)) for i in range(Prog.NDMA)]
            self.body()
            self.P.replay(nc, esems, dsems)
        return nc

    def carve_reset(self):
        self.aoff = 0

    def carve(self, words, dt, shape=None):
        a = self.arena[:, self.aoff:self.aoff + words]
        self.aoff += words
        assert self.aoff <= 29696, self.aoff
        if dt is BF16:
            a = a.bitcast(BF16)
        return a

    def barrier(self):
        keys = list(self.P.track.keys())
        oid = self.P.emit("dve", lambda e, a=self.small[:, 63:64]: e.memset(a, 0.0), reads=(), writes=keys + ["__bar"])
        self.P.bar = oid

    def body(self):
        B = self
        h = self.h
        ld = [("ident", self.ident), ("ones", self.ones), ("bd", self.bd), ("selden", self.selden), ("selnum", self.selnum_t)]
        for k, t in ld:
            B.DMA("sp", t[:], self.cn[k], writes=[k])
        for k, t in (("pswA", self.pswA), ("pswC", self.pswC), ("mkf", self.mkf), ("mkb", self.mkb)):
            B.DMA("pool", t[:], self.cn[k], writes=[k + "_b"])
        B.DMA("sp", self.gn[:], self.g_norm, writes=["gn"])
        B.DMA("sp", self.gfin[:], self.g_final, writes=["gfin"])
        B.DMA("sp", self.bada[:], self.b_ada.rearrange("l p j -> p l j"), writes=["bada"])
        B.DMA("sp", self.gc[:], self.gcols, writes=["gc"])
        B.DMA("sp", self.bgs[:], self.bg, writes=["bgs"])
        self.carve_reset()
        self.aoff = 3072
        self.lamt = self.carve(256, F32)
        B.DMA("sp", self.lamt, self.lamv, writes=["lamt"])
        B.DMA("sp", self.sbms[:], self.sbm, writes=["sbms"])
        B.DMA("sp", self.cv[:], self.cvec, writes=["cv"])
        B.TS(self.nbg[:], self.bgs[:], -1.0, None, ALU.mult, None, ["bgs"], ["nbg"])
        sm = self.small
        for l in range(2):
            lam_init = 0.8 - 0.6 * math.exp(-0.3 * l)
            for j in range(2):
                a = self.lamt[:, (l * 4 + 2 * j) * 32:(l * 4 + 2 * j + 1) * 32]
                b = self.lamt[:, (l * 4 + 2 * j + 1) * 32:(l * 4 + 2 * j + 2) * 32]
                B.TT(sm[:, 0:32], a, b, ALU.mult, ["lamt"], ["sm"])
                B.E("dve", lambda e, o=sm[:, 32 + j:33 + j], i=sm[:, 0:32]: e.reduce_sum(out=o, in_=i, axis=mybir.AxisListType.X), ["sm"], ["sm"])
                B.ACT(sm[:, 34 + j:35 + j], sm[:, 32 + j:33 + j], AF.Exp, ["sm"], ["sm"])
            B.TT(sm[:, 36:37], sm[:, 34:35], sm[:, 35:36], ALU.subtract, ["sm"], ["sm"])
            B.TS(self.lamc[:, 4 * l:4 * l + 1], sm[:, 36:37], lam_init, None, ALU.add, None, ["sm"], ["lamc"])
            B.TS(self.lamc[:, 4 * l + 1:4 * l + 2], self.lamc[:, 4 * l:4 * l + 1], -1.0, None, ALU.mult, None, ["lamc"], ["lamc"])
            B.TS(self.lamc[:, 4 * l + 2:4 * l + 3], self.gc[:, 4 * l + 3:4 * l + 4], 1.0 - lam_init, None, ALU.mult, None, ["gc", "lamc"], ["lamc"])

        self.carve_reset()
        xt = [self.carve(1024, F32) for _ in range(3)]
        for t in range(NT // 128):
            xb = xt[t % 3]
            B.DMA("sp", xb, self.xin[t * 128:(t + 1) * 128, :], writes=[("xt", t % 3)])
            for half in range(2):
                pb, pk = B.ps("a")
                for c in range(4):
                    cc = half * 4 + c
                    B.TR(pb[:, c * 128:(c + 1) * 128], xb[:, cc * 128:(cc + 1) * 128], self.ident[:], [("xt", t % 3), "ident"], [pk])
                o = h[:, half * 4:half * 4 + 4, t * 128:(t + 1) * 128]
                i = pb[:].rearrange("p (c t) -> p c t", t=128)
                if half == 0:
                    B.CP(o, i, [pk], [("h", t // 2)])
                else:
                    B.ACT(o, i, AF.Copy, [pk], [("h", t // 2)])

        self.modulation()
        for l in range(FLAGS["LAYERS"]):
            if FLAGS["FFN"]:
                self.ffn(l, 0)
            self.mix_layer(l)
            if FLAGS["FFN"]:
                self.ffn(l, 1)
        self.final_out()

    def hkeys(self, off, w):
        return [("h", b) for b in range(off // 256, (off + w) // 256)]

    def modulation(self):
        B = self
        self.P.phase = "mod"
        self.aoff = 4096
        wb = [self.carve(4096, BF16).rearrange("p (k n) -> p k n", n=1024) for _ in range(2)]
        B.ACT(self.cvb[:], self.cv[:], AF.Silu, ["cv"], ["cvb"])
        for l in range(2):
            wv = self.w_ada[l].rearrange("(k p) n -> p k n", p=128)
            pb, pk = B.ps("b")
            for blk in range(9):
                w = wb[blk % 2]
                key = ("wada", blk % 2)
                B.DMA("pool", w, wv[:, :, blk * 1024:(blk + 1) * 1024], writes=[key])
                for fc in range(8):
                    j = blk * 8 + fc
                    for k in range(8):
                        B.MM(pb[:, 2 * j:2 * j + 2], w[:, k, fc * 128:(fc + 1) * 128], self.cvb[:, k, :], k == 0, k == 7, [key, "cvb"], [pk])
            B.TT(self.modc[:, l], pb[:, 0:144].rearrange("p (j c) -> p j c", c=2),
                 self.bada[:, l].unsqueeze(2).to_broadcast([128, 72, 2]), ALU.add, [pk, "bada"], ["modc"])
            for i in range(3):
                sc = self.modc[:, l, (3 * i + 1) * 8:(3 * i + 2) * 8, :]
                B.TS(self.nsc[:, l, i], sc, 1.0, None, ALU.add, None, ["modc"], ["nsc"])
                B.TT(self.nsc[:, l, i], self.nsc[:, l, i],
                     self.gn[:, (l * 3 + i) * 8:(l * 3 + i + 1) * 8].unsqueeze(2).to_broadcast([128, 8, 2]), ALU.mult, ["nsc", "gn"], ["nsc"])
                g = self.modc[:, l, (3 * i + 2) * 8:(3 * i + 3) * 8, :]
                B.TS(self.gt[:, l, i], g, (1.0 if i == 1 else 0.5), None, ALU.mult, None, ["modc"], ["gt"])

    def norm_mod(self, off, w, scale_col, shift_col, u_out, ukeys, tmps):
        B = self
        sq, rs, tmp = tmps
        hk = self.hkeys(off, w)
        pb, pk = B.ps("b")
        for c in range(8):
            s = sq[c % 2]
            B.ACT(s[:, 0:w], self.h[:, c, off:off + w], AF.Square, hk, [("sq", c % 2)])
            B.MM(pb[:, 0:w], self.ones[:], s[:, 0:w], c == 0, c == 7, [("sq", c % 2), "ones"], [pk])
        B.ACT(rs[:, 0:w], pb[:, 0:w], AF.Ln, [pk, "epsc"], ["rs"], bias=self.epsc[:, 0:1], scale=1.0 / D)
        B.ACT(rs[:, 0:w], rs[:, 0:w], AF.Exp, ["rs"], ["rs"], scale=-0.5)
        for c in range(8):
            t = tmp[c % 2]
            B.STT(t[:, 0:w], self.h[:, c, off:off + w], scale_col(c), rs[:, 0:w], ALU.mult, ALU.mult, hk + ["rs", "nsc", "gfin"], [("tmp", c % 2)])
            if shift_col is None:
                if c % 2 == 0:
                    B.ACT(u_out(c), t[:, 0:w], AF.Copy, [("tmp", c % 2)], ukeys(c))
                else:
                    B.CP(u_out(c), t[:, 0:w], [("tmp", c % 2)], ukeys(c))
            else:
                if c % 2 == 0:
                    B.ACT(u_out(c), t[:, 0:w], AF.Identity, [("tmp", c % 2), "modc"], ukeys(c), bias=shift_col(c))
                else:
                    B.TS(u_out(c), t[:, 0:w], shift_col(c), None, ALU.add, None, [("tmp", c % 2), "modc"], ukeys(c))

    def mk_eps(self):
        if not hasattr(self, "epsc"):
            self.epsc = self.small[:, 40:41]
            self.MEMSET(self.small[:, 40:41], EPS, ["epsc"])

    def ffn(self, l, j):
        B = self
        self.P.phase = "ffn%d%d" % (l, j)
        ni = 0 if j == 0 else 2
        self.barrier()
        self.mk_eps()
        self.carve_reset()
        U = self.carve(10240, BF16).rearrange("p (k t) -> p k t", t=NT)
        HID = self.carve(7680, BF16).rearrange("p (f t) -> p f t", t=NT)
        W1 = [self.carve(2048, BF16).rearrange("p (k g n) -> p k g n", g=2, n=256) for _ in range(2)]
        W2 = self.carve(3072, BF16).rearrange("p (f n) -> p f n", n=D)
        sq = [self.carve(512, F32) for _ in range(2)]
        rs = self.carve(512, F32)
        tmp = [self.carve(512, F32) for _ in range(2)]
        sg = [self.carve(512, F32) for _ in range(2)]
        tiles = [(0, 512, 0), (512, 512, 0), (1024, 512, 0), (1536, 512, 0), (2048, 512, 1)]
        for (off, w, mc) in tiles:
            self.norm_mod(off, w,
                          lambda c, mc=mc: self.nsc[:, l, ni, c, mc:mc + 1],
                          lambda c, mc=mc: self.modc[:, l, (3 * ni) * 8 + c, mc:mc + 1],
                          lambda c, off=off, w=w: U[:, c, off:off + w],
                          lambda c, off=off: [("u", c, off // 512)], (sq, rs, tmp))
        w1v = self.w_ff_in[l, j].rearrange("(k p) n -> p k n", p=128)
        w2v = self.w_ff_out[l, j].rearrange("(f p) n -> p f n", p=128)
        passes = [(0, 6), (6, 6), (12, 6), (18, 4)]
        w1i = 0
        sgi = 0
        for (f0, nf) in passes:
            for fp in range(nf // 2):
                f = f0 + 2 * fp
                wt = W1[w1i % 2]
                wk = ("w1", w1i % 2)
                w1i += 1
                B.DMA("pool", wt[:, :, 0, :], w1v[:, :, f * 128:f * 128 + 256], writes=[wk + (0,)])
                B.DMA("pool", wt[:, :, 1, :], w1v[:, :, DFF + f * 128:DFF + f * 128 + 256], writes=[wk + (1,)])
                if fp == 0:
                    B.DMA("pool", W2[:, 0:nf, :], w2v[:, f0:f0 + nf, :], writes=["w2"])
                for sub in range(2):
                    fl = 2 * fp + sub
                    for ti, (off, w, mc) in enumerate(tiles):
                        pg, pgk = B.ps("a")
                        pu, puk = B.ps("c")
                        for k in range(8):
                            B.MM(pg[:, 0:w], wt[:, k, 0, sub * 128:(sub + 1) * 128], U[:, k, off:off + w], k == 0, k == 7, [wk + (0,), ("u", k, ti)], [pgk])
                        for k in range(8):
                            B.MM(pu[:, 0:w], wt[:, k, 1, sub * 128:(sub + 1) * 128], U[:, k, off:off + w], k == 0, k == 7, [wk + (1,), ("u", k, ti)], [puk])
                        s = sg[sgi % 2]
                        sk = ("sg", sgi % 2)
                        sgi += 1
                        B.ACT(s[:, 0:w], pg[:, 0:w], AF.Silu, [pgk], [sk])
                        B.TT(HID[:, fl, off:off + w], s[:, 0:w], pu[:, 0:w], ALU.mult, [sk, puk], [("hid", fl, ti)])
            for ti, (off, w, mc) in enumerate(tiles):
                hk = self.hkeys(off, w)
                for d in range(8):
                    pb, pk = B.ps("b")
                    for fl in range(nf):
                        B.MM(pb[:, 0:w], W2[:, fl, d * 128:(d + 1) * 128], HID[:, fl, off:off + w], fl == 0, fl == nf - 1, ["w2", ("hid", fl, ti)], [pk])
                    hv = self.h[:, d, off:off + w]
                    B.STT(hv, pb[:, 0:w], self.gt[:, l, ni, d, mc:mc + 1], hv, ALU.mult, ALU.add, [pk, "gt"] + hk, hk)

    def final_out(self):
        B = self
        self.P.phase = "final"
        self.barrier()
        self.mk_eps()
        self.carve_reset()
        sq = [self.carve(512, F32) for _ in range(2)]
        rs = self.carve(512, F32)
        tmp = [self.carve(512, F32) for _ in range(2)]
        yf = self.carve(4096, F32).rearrange("p (k t) -> p k t", t=512)
        ot = [self.carve(1024, F32) for _ in range(2)]
        oi = 0
        for ti in range(5):
            off = ti * 512
            self.norm_mod(off, 512, lambda c: self.gfin[:, c:c + 1], None,
                          lambda c: yf[:, c, :], lambda c: [("yf", c)], (sq, rs, tmp))
            for tt in range(4):
                o = ot[oi % 2]
                ok = ("ot", oi % 2)
                oi += 1
                for half in range(2):
                    pb, pk = B.ps("a")
                    for c in range(4):
                        cc = half * 4 + c
                        B.TR(pb[:, c * 128:(c + 1) * 128], yf[:, cc, tt * 128:(tt + 1) * 128], self.ident[:], [("yf", cc), "ident"], [pk])
                    if half == 0:
                        B.CP(o[:, 0:512], pb[:], [pk], [ok])
                    else:
                        B.ACT(o[:, 512:1024], pb[:], AF.Copy, [pk], [ok])
                r0 = off + tt * 128
                B.DMA("sp", self.o_y[r0:r0 + 128, :], o, reads=[ok])

    def mix_layer(self, l):
        B = self
        self.mk_eps()
        for (seq, off, L) in ((0, 0, LS), (1, LS, LP), (2, LS + LP, LP)):
            self.mix_seq(l, seq, off, L)

    def mix_seq(self, l, seq, off, L):
        B = self
        h = self.h
        sample = (seq == 0)
        mc = 0 if sample else 1
        TW = 512 if sample else 256
        ntile = L // TW
        nch = L // 128
        self.P.phase = "mix%d_s%d_norm" % (l, seq)
        self.barrier()
        self.carve_reset()
        U = self.carve(8 * L // 2, BF16).rearrange("p (k t) -> p k t", t=L)
        Y = self.carve(8 * L // 2, BF16).rearrange("p (k t) -> p k t", t=L)
        sq = [self.carve(512, F32) for _ in range(2)]
        rs = self.carve(512, F32)
        tmp = [self.carve(512, F32) for _ in range(2)]
        base_off = self.aoff
        for t in range(ntile):
            o = t * TW
            self.norm_mod(off + o, TW,
                          lambda c: self.nsc[:, l, 1, c, mc:mc + 1],
                          lambda c: self.modc[:, l, 24 + c, mc:mc + 1],
                          lambda c, o=o: U[:, c, o:o + TW],
                          lambda c, t=t: [("u", c, t)], (sq, rs, tmp))
        wv = self.w_in[l].rearrange("(k p) n -> p k n", p=128)
        Wall = None
        if not sample:
            Wall = self.carve(8 * NWALL // 2, BF16).rearrange("p (k n) -> p k n", n=NWALL)
            base_off = self.aoff
            if seq == 1:
                half = NWALL // 2
                B.DMA("pool", Wall[:, :, 0:half], wv[:, :, 0:half], writes=["wall"])
                B.DMA("pool", Wall[:, :, half:NWALL], wv[:, :, half:NWALL], writes=["wall"])
        ctx = dict(l=l, seq=seq, off=off, L=L, sample=sample, TW=TW, ntile=ntile, nch=nch, U=U, Y=Y, wv=wv,
                   sq=sq, rs=rs, tmp=tmp, Wall=Wall)
        for grp, fn in (("A", self.group_A), ("B", self.group_B), ("C", self.group_C)):
            self.aoff = base_off
            if FLAGS[grp]:
                self.P.phase = "mix%d_s%d_%s" % (l, seq, grp)
                self.barrier()
                fn(ctx)
            else:
                c0, c1 = {"A": (0, 4), "B": (4, 6), "C": (6, 8)}[grp]
                for c in range(c0, c1):
                    B.MEMSET(Y[:, c, :], 0.0, [("y", c)])
        self.P.phase = "mix%d_s%d_out" % (l, seq)
        self.barrier()
        self.aoff = base_off
        WO = self.carve(4096, BF16).rearrange("p (k n) -> p k n", n=D)
        B.DMA("pool", WO, self.w_out[l].rearrange("(k p) n -> p k n", p=128), writes=["wo"])
        for t in range(ntile):
            o = t * TW
            hk = self.hkeys(off + o, TW)
            for d in range(8):
                pb, pk = B.ps("b")
                for k in range(8):
                    B.MM(pb[:, 0:TW], WO[:, k, d * 128:(d + 1) * 128], Y[:, k, o:o + TW], k == 0, k == 7, ["wo", ("y", k)], [pk])
                hv = h[:, d, off + o:off + o + TW]
                B.STT(hv, pb[:, 0:TW], self.gt[:, l, 1, d, mc:mc + 1], hv, ALU.mult, ALU.add, [pk, "gt"] + hk, hk)

    def proj_fm(self, W, wkey, c0, ncol, U, o, TW, pool="a"):
        pb, pk = self.ps(pool)
        for k in range(8):
            self.MM(pb[0:ncol, 0:TW], W[:, k, c0:c0 + ncol], U[:, k, o:o + TW], k == 0, k == 7, [wkey] + [("u", k, o // TW)], [pk])
        return pb, pk

    def headnorm(self, src, srck, w, gcol, outs, ctx):
        B = self
        sq, rs = ctx["sq"], ctx["rs"]
        B.ACT(sq[0][:, 0:w], src, AF.Square, [srck], [("sq", 0)])
        pb, pk = B.ps("b")
        B.MM(pb[:, 0:w], self.bd[:], sq[0][:, 0:w], True, True, [("sq", 0), "bd"], [pk])
        B.ACT(rs[:, 0:w], pb[:, 0:w], AF.Ln, [pk, "epsc"], ["rs"], bias=self.epsc[:, 0:1], scale=1.0 / 64)
        B.ACT(rs[:, 0:w], rs[:, 0:w], AF.Exp, ["rs"], ["rs"], scale=-0.5)
        for ent in outs:
            o, ok = ent[0], ent[1]
            psl = ent[2] if len(ent) > 2 else slice(0, 128)
            B.STT(o, src[psl], gcol[psl], rs[psl, 0:w], ALU.mult, ALU.mult, [srck, "rs", "gc", "lamc"], ok)

    def rope(self, x, xk, w, psw, pswk, cos, sin, ck, out, outk, ctx):
        B = self
        tmp = ctx["tmp"]
        pb, pk = B.ps("b")
        B.MM(pb[:, 0:w], psw[:], x, True, True, [xk, pswk], [pk])
        cks = ck if isinstance(ck, list) else [ck]
        B.TT(tmp[0][:, 0:w], x, cos, ALU.mult, [xk] + cks, [("tmp", 0)])
        B.TT(tmp[1][:, 0:w], pb[:, 0:w], sin, ALU.mult, [pk] + cks, [("tmp", 1)])
        if isinstance(out, list):
            for (o, ok, psl) in out:
                B.TT(o, tmp[0][psl, 0:w], tmp[1][psl, 0:w], ALU.add, [("tmp", 0), ("tmp", 1)], ok)
        else:
            B.TT(out, tmp[0][:, 0:w], tmp[1][:, 0:w], ALU.add, [("tmp", 0), ("tmp", 1)], outk)

    def attn_scores_pv(self, qT, qk, half, KT, kkey, nk, vfun, vkey, TW, scale, PT, ctx, side=None, it0=0):
        B = self
        LAG = len(PT) - 1
        acc, acck = B.ps("c")
        pend = []
        for c in range(nk + LAG):
            while side and side[0][0] <= it0 + c:
                side.pop(0)[1]()
            if c < nk:
                sp_, spk = B.ps("a")
                B.MM(sp_[:, 0:TW], KT[:, c * 128:(c + 1) * 128], qT[:, 0:TW], True, True, [kkey, qk], [spk])
                pt = PT[c % len(PT)]
                ptk = self.ptkeys[c % len(PT)] if self.ptkeys else ("pt", c % len(PT))
                B.ACT(pt[:, 0:TW], sp_[:, 0:TW], AF.Exp, [spk], [ptk], scale=scale)
                pend.append((c, pt, ptk))
            if c >= LAG:
                cc, pt, ptk = pend.pop(0)
                B.MM(acc[:, 0:TW], vfun(cc), pt[:, 0:TW], cc == 0, cc == nk - 1, [vkey, ptk], [acck])
        return acc, acck

    def prep_stages(self, W, wkey, c0, U, o, TW, gcol, psw, pswk, cs, sn, outs, ctx, norm):
        B = self
        sq, rs, tmp = ctx["sq"], ctx["rs"], ctx["tmp"]
        xn = sq[1].bitcast(BF16)
        xnk = ("sq", 1)
        st = {}

        def s0():
            st["pb"], st["pk"] = self.proj_fm(W, wkey, c0, 128, U, o, TW, pool="b")

        def s1():
            B.ACT(sq[0][:, 0:TW], st["pb"][:, 0:TW], AF.Square, [st["pk"]], [("sq", 0)])

        def s2():
            st["pd"], st["pdk"] = B.ps("b")
            B.MM(st["pd"][:, 0:TW], self.bd[:], sq[0][:, 0:TW], True, True, [("sq", 0), "bd"], [st["pdk"]])

        def s3():
            B.ACT(rs[:, 0:TW], st["pd"][:, 0:TW], AF.Ln, [st["pdk"], "epsc"], ["rs"], bias=self.epsc[:, 0:1], scale=1.0 / 64)
            B.ACT(rs[:, 0:TW], rs[:, 0:TW], AF.Exp, ["rs"], ["rs"], scale=-0.5)

        def s4():
            if norm:
                B.STT(xn[:, 0:TW], st["pb"][:, 0:TW], gcol, rs[:, 0:TW], ALU.mult, ALU.mult, [st["pk"], "rs", "gc"], [xnk])
            else:
                B.CP(xn[:, 0:TW], st["pb"][:, 0:TW], [st["pk"]], [xnk])

        def s5():
            st["pr"], st["prk"] = B.ps("b")
            B.MM(st["pr"][:, 0:TW], psw[:], xn[:, 0:TW], True, True, [xnk, pswk], [st["prk"]])
            B.TT(tmp[0][:, 0:TW], xn[:, 0:TW], cs[:, 0:TW], ALU.mult, [xnk, "cs"], [("tmp", 0)])

        def s6():
            B.TT(tmp[1][:, 0:TW], st["pr"][:, 0:TW], sn[:, 0:TW], ALU.mult, [st["prk"], "cs2"], [("tmp", 1)])

        def s7():
            for (oo, ok, psl) in outs:
                B.TT(oo, tmp[0][psl, 0:TW], tmp[1][psl, 0:TW], ALU.add, [("tmp", 0), ("tmp", 1)], ok)

        if norm:
            return [s0, s1, s2, s3, s4, s5, s6, s7]
        return [s0, s4, s5, s6, s7]

    def group_A(self, ctx):
        B = self
        l, seq, off, L, sample, TW, ntile, nch, U, Y, wv = (ctx[k] for k in ("l", "seq", "off", "L", "sample", "TW", "ntile", "nch", "U", "Y", "wv"))
        npast = PAST if sample else 0
        nk = (npast + L) // 128
        wAk = "wall" if ctx["Wall"] is not None else "wA"
        if ctx["Wall"] is not None:
            W = ctx["Wall"][:, :, 0:768]
        else:
            W = self.carve(8 * 768 // 2, BF16).rearrange("p (k n) -> p k n", n=768)
            B.DMA("pool", W, wv[:, :, 0:768], writes=[wAk])
        KT = self.carve((npast + L) // 2, BF16)
        V = self.carve(nk * 2 * 128 // 2, BF16).rearrange("p (c g n) -> p c g n", g=2, n=128)
        QT = [[self.carve(256, BF16) for _ in range(2)] for _ in range(2)]
        PT = [self.carve(256, BF16) for _ in range(3)]
        xn = ctx["sq"][1].bitcast(BF16)
        xnk = ("sq", 1)
        rc = self.carve(512, F32)
        rc2 = rc
        cs = self.carve(512, F32)
        sn = self.carve(512, F32)
        stg = self.carve(512, F32)
        PT.append(stg.bitcast(BF16))
        self.ptkeys = [("pt", 0), ("pt", 1), ("pt", 2), "stg"]
        saved_pools = self.pspools
        self.pspools = {"a": [0, 1, 2, 3], "b": [4, 5], "c": [6, 7]}
        self.psidx = {}
        for i in range(2):
            B.MEMSET(QT[i][0][64:128, :], 0.0, [("qt", i, 0)])
            B.MEMSET(QT[i][1][0:64, :], 0.0, [("qt", i, 1)])
        B.MEMSET(V[:, :, 0, 64:128], 1.0, ["V"])
        B.MEMSET(V[:, :, 1, 0:64], 1.0, ["V"])
        vf = lambda c, g: V[:, c, g, :]
        if sample:
            for c in range(PAST // 128):
                B.DMA("sp", stg[:, 0:128], self.cak[l, c * 128:(c + 1) * 128, :], writes=["stg"])
                pb, pk = B.ps("b")
                B.TR(pb[:, 0:128], stg[:, 0:128], self.ident[:], ["stg", "ident"], [pk])
                B.CP(KT[:, c * 128:(c + 1) * 128], pb[:, 0:128], [pk], ["KT"])
                B.DMA("sp", stg[:, 128:256], self.cav[l, c * 128:(c + 1) * 128, :], writes=["stg"])
                B.CP(V[:, c, 0, 0:64], stg[:, 128:192], ["stg"], ["V"])
                B.CP(V[:, c, 1, 64:128], stg[:, 192:256], ["stg"], ["V"])
        gq = self.gc[:, 4 * l + 0:4 * l + 1]
        gk = self.gc[:, 4 * l + 1:4 * l + 2]
        def k_tile(t):
            o = t * TW
            pb, pk = self.proj_fm(W, wAk, KA0, 128, U, o, TW)
            if sample:
                self.headnorm(pb[:, 0:TW], pk, TW, gk, [(xn[:, 0:TW], [xnk])], ctx)
                B.DMA("sp", cs[:, 0:TW], self.cn["cosA"][:, o:o + TW], writes=["cs"])
                B.DMA("sp", sn[:, 0:TW], self.cn["sinA"][:, o:o + TW], writes=["cs2"])
                self.rope(xn[:, 0:TW], xnk, TW, self.pswA, "pswA_b", cs[:, 0:TW], sn[:, 0:TW], ["cs", "cs2"], KT[:, npast + o:npast + o + TW], ["KT"], ctx)
            else:
                self.headnorm(pb[:, 0:TW], pk, TW, gk, [(KT[:, o:o + TW], ["KT"]), (stg[:, 0:TW], ["stg"])], ctx)
                for s in range(TW // 128):
                    p2, p2k = B.ps("b")
                    B.TR(p2[:, 0:128], stg[:, s * 128:(s + 1) * 128], self.ident[:], ["stg", "ident"], [p2k])
                    B.CP(rc[:, 0:128], p2[:, 0:128], [p2k], ["rc"])
                    B.DMA("sp", self.o_ak[seq - 1, l, o + s * 128:o + (s + 1) * 128, :], rc[:, 0:128], reads=["rc"])

        def v_chunk(c):
            pb, pk = B.ps("a")
            for k in range(8):
                B.MM(pb[:, 0:128], U[:, k, c * 128:(c + 1) * 128], W[:, k, VA0:VA0 + 128], k == 0, k == 7, [wAk, ("u", k, (c * 128) // TW)], [pk])
            cc = npast // 128 + c
            B.CP(V[:, cc, 0, 0:64], pb[:, 0:64], [pk], ["V"])
            B.CP(V[:, cc, 1, 64:128], pb[:, 64:128], [pk], ["V"])
            if not sample:
                B.ACT(sn[:, 0:128], pb[:, 0:128], AF.Copy, [pk], ["cs2"])
                B.DMA("sp", self.o_av[seq - 1, l, c * 128:(c + 1) * 128, :], sn[:, 0:128], reads=["cs2"])

        vper = nch // ntile
        for t in range(ntile):
            k_tile(t)
            for c in range(t * vper, (t + 1) * vper):
                v_chunk(c)
        if sample:
            jobs = [(t, m) for t in range(ntile) for m in range(4)]

            def stages_for(t, m):
                o = t * TW
                q = QT[m % 2]
                qouts = [(q[0][0:64, 0:TW], [("qt", m % 2, 0)], slice(0, 64)), (q[1][64:128, 0:TW], [("qt", m % 2, 1)], slice(64, 128))]
                stl = self.prep_stages(W, wAk, QA0 + m * 128, U, o, TW, gq, self.pswA, "pswA_b", cs, sn, qouts, ctx, True)
                if m == 0:
                    def ld(o=o):
                        B.DMA("sp", cs[:, 0:TW], self.cn["cosA"][:, o:o + TW], writes=["cs"])
                        B.DMA("sp", sn[:, 0:TW], self.cn["sinA"][:, o:o + TW], writes=["cs2"])
                    stl = [ld] + stl
                return stl
            for f in stages_for(0, 0):
                f()
            for ji, (t, m) in enumerate(jobs):
                o = t * TW
                q = QT[m % 2]
                side = []
                if ji + 1 < len(jobs):
                    stl = stages_for(*jobs[ji + 1])
                    step = max(1, (2 * nk - 6) // len(stl))
                    side = [[2 + i * step, f] for i, f in enumerate(stl)]
                for g in range(2):
                    acc, acck = self.attn_scores_pv(q[g], ("qt", m % 2, g), g, KT, "KT", nk, lambda c, g=g: vf(c, g), "V", TW, 0.125, PT, ctx, side=side, it0=g * nk)
                    self.softmax_norm(acc, acck, g, TW, Y[:, m, o:o + TW], [("y", m)], rc, rc2)
                while side:
                    side.pop(0)[1]()
        else:
            for t in range(ntile):
                o = t * TW
                for m in range(4):
                    q = QT[m % 2]
                    qouts = [(q[0][0:64, 0:TW], [("qt", m % 2, 0)], slice(0, 64)), (q[1][64:128, 0:TW], [("qt", m % 2, 1)], slice(64, 128))]
                    pb, pk = self.proj_fm(W, wAk, QA0 + m * 128, 128, U, o, TW)
                    self.headnorm(pb[:, 0:TW], pk, TW, gq, qouts, ctx)
                    for g in range(2):
                        acc, acck = self.attn_scores_pv(q[g], ("qt", m % 2, g), g, KT, "KT", nk, lambda c, g=g: vf(c, g), "V", TW, 0.125, PT, ctx)
                        self.softmax_norm(acc, acck, g, TW, Y[:, m, o:o + TW], [("y", m)], rc, rc2)
        self.pspools = saved_pools
        self.psidx = {}
        self.ptkeys = None

    def softmax_norm(self, acc, acck, nh, TW, yout, ykeys, rc, rc2):
        B = self
        dh = 1 - nh
        ds = slice(dh * 64, dh * 64 + 64)
        ns = slice(nh * 64, nh * 64 + 64)
        B.RCP(rc[ds, 0:TW], acc[ds, 0:TW], [acck], ["rc"])
        B.CP(rc[ns, 0:TW], rc[ds, 0:TW], ["rc"], ["rc"])
        B.TT(yout[ns, :], acc[ns, 0:TW], rc[ns, 0:TW], ALU.mult, [acck, "rc"], ykeys)

    def group_C(self, ctx):
        B = self
        l, seq, off, L, sample, TW, ntile, nch, U, Y, wv = (ctx[k] for k in ("l", "seq", "off", "L", "sample", "TW", "ntile", "nch", "U", "Y", "wv"))
        npast = PAST if sample else 0
        nk = (npast + L) // 128
        Wall = ctx["Wall"]
        wCk = "wall" if Wall is not None else "wC"
        W = None if Wall is not None else self.carve(8 * 512 // 2, BF16).rearrange("p (k n) -> p k n", n=512)
        KT = self.carve((npast + L) // 2, BF16)
        V = self.carve(nk * 2 * 128 // 2, BF16).rearrange("p (c g n) -> p c g n", g=2, n=128)
        QT = [[self.carve(256, BF16) for _ in range(2)] for _ in range(2)]
        PT = [self.carve(256, BF16) for _ in range(4)]
        saved_pools = self.pspools
        self.pspools = {"a": [0, 1, 2, 3], "b": [4, 5], "c": [6, 7]}
        self.psidx = {}
        xn = ctx["sq"][1].bitcast(BF16)
        xnk = ("sq", 1)
        rc = self.carve(512, F32)
        rc2 = rc
        for n in range(2):
            B.MEMSET(QT[0][n][64:128, :], 0.0, [("qtc", 0, n)])
            B.MEMSET(QT[1][n][0:64, :], 0.0, [("qtc", 1, n)])
        cs = self.carve(512, F32)
        sn = self.carve(512, F32)
        r0 = self.carve(512, F32)
        r1 = self.carve(512, F32)
        stg = r1
        scale = 32 ** -0.5
        nlam = self.lamc[:, 4 * l + 1:4 * l + 2]
        gcl = self.lamc[:, 4 * l + 2:4 * l + 3]
        B.MEMSET(V[:, :, 0, 64:128], 1.0, ["V"])
        B.MEMSET(V[:, :, 1, 0:64], 1.0, ["V"])
        for hp in range(2):
            if Wall is not None:
                W = Wall[:, :, KC0:KC0 + 512]
                kcol, vcol = hp * 128, 256 + hp * 128
            else:
                B.DMA("pool", W, wv[:, :, PK0 + hp * 512:PK0 + (hp + 1) * 512], writes=["wC"])
                kcol, vcol = 0, 128
            if sample:
                for c in range(PAST // 128):
                    B.DMA("sp", stg[:, 0:128], self.cck[l, c * 128:(c + 1) * 128, hp * 128:(hp + 1) * 128], writes=["r1"])
                    pb, pk = B.ps("b")
                    B.TR(pb[:, 0:128], stg[:, 0:128], self.ident[:], ["r1", "ident"], [pk])
                    B.CP(KT[:, c * 128:(c + 1) * 128], pb[:, 0:128], [pk], ["KT"])
                    B.DMA("sp", rc[:, 0:128], self.ccv[l, c * 128:(c + 1) * 128, hp * 128:(hp + 1) * 128], writes=["rc"])
                    B.CP(V[:, c, 0, 0:64], rc[:, 0:64], ["rc"], ["V"])
                    B.CP(V[:, c, 1, 64:128], rc[:, 64:128], ["rc"], ["V"])
            def k_tile(t):
                o = t * TW
                pb, pk = self.proj_fm(W, wCk, kcol, 128, U, o, TW)
                if sample:
                    B.DMA("sp", cs[:, 0:TW], self.cn["cosC"][:, o:o + TW], writes=["cs"])
                    B.DMA("sp", sn[:, 0:TW], self.cn["sinC"][:, o:o + TW], writes=["cs2"])
                    B.ACT(xn[:, 0:TW], pb[:, 0:TW], AF.Copy, [pk], [xnk])
                    self.rope(xn[:, 0:TW], xnk, TW, self.pswC, "pswC_b", cs[:, 0:TW], sn[:, 0:TW], ["cs", "cs2"], KT[:, npast + o:npast + o + TW], ["KT"], ctx)
                else:
                    B.ACT(KT[:, o:o + TW], pb[:, 0:TW], AF.Copy, [pk], ["KT"])

            def v_chunk(c):
                pb, pk = B.ps("a")
                for k in range(8):
                    B.MM(pb[:, 0:128], U[:, k, c * 128:(c + 1) * 128], W[:, k, vcol:vcol + 128], k == 0, k == 7, [wCk, ("u", k, (c * 128) // TW)], [pk])
                cc = npast // 128 + c
                B.CP(V[:, cc, 0, 0:64], pb[:, 0:64], [pk], ["V"])
                B.ACT(V[:, cc, 1, 64:128], pb[:, 64:128], AF.Copy, [pk], ["V"])
                if (not sample) and hp == 0:
                    pb2, pk2 = B.ps("a")
                    for k in range(8):
                        B.MM(pb2[:, 0:512], U[:, k, c * 128:(c + 1) * 128], W[:, k, 0:512], k == 0, k == 7, [wCk, ("u", k, (c * 128) // TW)], [pk2])
                    B.CP(stg[:, 0:512], pb2[:, 0:512], [pk2], ["r1"])
                    B.DMA("sp", self.o_ck[seq - 1, l, c * 128:(c + 1) * 128, :], stg[:, 0:256], reads=["r1"])
                    B.DMA("sp", self.o_cv[seq - 1, l, c * 128:(c + 1) * 128, :], stg[:, 256:512], reads=["r1"])

            vper = nch // ntile
            for t in range(ntile):
                k_tile(t)
                for c in range(t * vper, (t + 1) * vper):
                    v_chunk(c)
            if Wall is not None:
                W = Wall[:, :, QC0 + hp * 256:QC0 + (hp + 1) * 256]
                qcol = 0
            else:
                qcol = 256
            def c_stages(t, n):
                o = t * TW
                qouts = [(QT[0][n][0:64, 0:TW], [("qtc", 0, n)], slice(0, 64)), (QT[1][n][64:128, 0:TW], [("qtc", 1, n)], slice(64, 128))]
                stl = self.prep_stages(W, wCk, qcol + n * 128, U, o, TW, None, self.pswC, "pswC_b", cs, sn, qouts, ctx, False)
                if n == 0:
                    def ld(o=o):
                        B.DMA("sp", cs[:, 0:TW], self.cn["cosC"][:, o:o + TW], writes=["cs"])
                        B.DMA("sp", sn[:, 0:TW], self.cn["sinC"][:, o:o + TW], writes=["cs2"])
                    stl = [ld] + stl
                return stl
            if sample:
                for n in range(2):
                    for f in c_stages(0, n):
                        f()
            deferred = None
            for t in range(ntile):
                o = t * TW
                side = []
                last = []
                if deferred is not None:
                    side += [[1, deferred[0]], [5, deferred[1]], [9, deferred[2]]]
                    deferred = None
                if sample:
                    if t + 1 < ntile:
                        st0 = c_stages(t + 1, 0)
                        st1 = c_stages(t + 1, 1)
                        pos = 2 * nk + 2
                        for f in st0 + st1[:-1]:
                            side.append([pos, f])
                            pos += 3
                        last = [st1[-1]]
                else:
                    for n in range(2):
                        pb, pk = self.proj_fm(W, wCk, qcol + n * 128, 128, U, o, TW)
                        B.ACT(QT[0][n][0:64, 0:TW], pb[0:64, 0:TW], AF.Copy, [pk], [("qtc", 0, n)])
                        B.CP(QT[1][n][64:128, 0:TW], pb[64:128, 0:TW], [pk], [("qtc", 1, n)])
                for hi, (hl, n) in enumerate(((0, 0), (1, 0), (0, 1), (1, 1))):
                    acc, acck = self.attn_scores_pv(QT[hl][n], ("qtc", hl, n), hl, KT, "KT", nk, lambda c, hl=hl: V[:, c, hl, :], "V", TW, scale, PT, ctx, side=side, it0=hi * nk)
                    dst = r0 if n == 0 else r1
                    self.softmax_norm(acc, acck, hl, TW, dst[:, 0:TW], ["r%d" % n], rc, rc2)
                while side:
                    side.pop(0)[1]()
                for f in last:
                    f()
                def mk_fin(o=o):
                    sq0, rs_ = ctx["sq"][0], ctx["rs"]
                    st = {}

                    def f0():
                        B.STT(r0[:, 0:TW], r1[:, 0:TW], nlam, r0[:, 0:TW], ALU.mult, ALU.add, ["r0", "r1", "lamc"], ["r0"])
                        B.ACT(sq0[:, 0:TW], r0[:, 0:TW], AF.Square, ["r0"], [("sq", 0)])

                    def f1():
                        st["pd"], st["pdk"] = B.ps("b")
                        B.MM(st["pd"][:, 0:TW], self.bd[:], sq0[:, 0:TW], True, True, [("sq", 0), "bd"], [st["pdk"]])

                    def f2():
                        B.ACT(rs_[:, 0:TW], st["pd"][:, 0:TW], AF.Ln, [st["pdk"], "epsc"], ["rs"], bias=self.epsc[:, 0:1], scale=1.0 / 64)
                        B.ACT(rs_[:, 0:TW], rs_[:, 0:TW], AF.Exp, ["rs"], ["rs"], scale=-0.5)
                        B.STT(Y[:, 6 + hp, o:o + TW], r0[:, 0:TW], gcl, rs_[:, 0:TW], ALU.mult, ALU.mult, ["r0", "rs", "lamc"], [("y", 6 + hp)])
                    return [f0, f1, f2]
                fin = mk_fin()
                if sample and t + 1 < ntile:
                    deferred = fin
                else:
                    for f in fin:
                        f()
        self.pspools = saved_pools
        self.psidx = {}

    def group_B(self, ctx):
        B = self
        l, seq, off, L, sample, TW, ntile, nch, U, Y, wv = (ctx[k] for k in ("l", "seq", "off", "L", "sample", "TW", "ntile", "nch", "U", "Y", "wv"))
        sm = self.small
        nblk = TW // 128
        saved_pools = self.pspools
        self.pspools = {"a": [0, 1], "b": [2, 3], "c": [4, 5, 6, 7]}
        self.psidx = {}
        thrb = [self.carve(L // 2, BF16) for _ in range(2)]
        wcol = self.carve(nch * 8, F32).rearrange("p (c d h) -> p c d h", d=2, h=4)
        rsm = self.carve(64, F32)
        B.MEMSET(rsm, 0.0, ["rsm"])
        seldb = self.carve(128, BF16)
        B.DMA("pool", seldb[0:4, :], self.cn["selden"], writes=["seldb"])
        Wall = ctx["Wall"]
        wGk = "wall" if Wall is not None else "wG"
        wBk = "wall" if Wall is not None else "wB"
        p2off = self.aoff
        if Wall is not None:
            WG = Wall[:, :, GB0:GB0 + 16]
        else:
            WG = self.carve(64, BF16).rearrange("p (k n) -> p k n", n=16)
            B.DMA("pool", WG, wv[:, :, GB0:GB0 + 16], writes=["wG"])
        order = {0: list(range(ntile)), 1: list(range(ntile - 1, -1, -1))}
        ig = self.carve(L, F32)
        lp = self.carve(L, F32)
        th = self.carve(L, F32)
        B.MEMSET(sm[0:4, 48:49], 1.0, ["sm1"])
        for d in range(2):
            for t in range(ntile):
                o = t * TW
                pb, pk = self.proj_fm(WG, wGk, 8 * d, 4, U, o, TW)
                B.ACT(ig[0:4, o:o + TW], pb[0:4, 0:TW], AF.Identity, [pk, "bgs"], ["ig"], bias=self.bgs[:, 4 * l + 2 * d:4 * l + 2 * d + 1])
                pb, pk = self.proj_fm(WG, wGk, 8 * d + 4, 4, U, o, TW)
                B.ACT(lp[0:4, o:o + TW], pb[0:4, 0:TW], AF.Exp, [pk, "nbg"], ["lp"], bias=self.nbg[:, 4 * l + 2 * d + 1:4 * l + 2 * d + 2], scale=-1.0)
            B.ACT(lp[0:4, :], lp[0:4, :], AF.Ln, ["lp", "sm1"], ["lp"], bias=sm[0:4, 48:49])
            rv = (lambda a: a) if d == 0 else (lambda a: a[:, ::-1])
            onesb = sm[0:4, 48:49].to_broadcast([4, L])
            B.SCAN(rv(th[0:4, :]), onesb, rv(lp[0:4, :]), 0.0, ALU.mult, ALU.add, ["lp", "sm1"], ["th"])
            B.TT(ig[0:4, :], ig[0:4, :], th[0:4, :], ALU.add, ["ig", "th"], ["ig"])
            init = self.sbms[:, 2 * l + d:2 * l + d + 1] if sample else 0.0
            B.SCAN(rv(lp[0:4, :]), rv(ig[0:4, :]), rv(ig[0:4, :]), init, ALU.max, ALU.max, ["ig", "sbms"], ["lp"])
            if not sample:
                Rl = lp[0:4, L - 1:L] if d == 0 else lp[0:4, 0:1]
                nfl = th[0:4, L - 1:L] if d == 0 else th[0:4, 0:1]
                B.TT(sm[0:4, 56 + d:57 + d], Rl, nfl, ALU.subtract, ["lp", "th"], [("mfin", d)])
                col = (seq - 1) * 4 + l * 2 + d
                B.DMA("sp", self.o_bm[col].unsqueeze(1), sm[0:4, 56 + d:57 + d], reads=[("mfin", d)])
            for j in range(ntile):
                o = j * TW
                rb = 4 * (2 * j + d)
                Rcol = lp[0:4, o + TW - 1:o + TW] if d == 0 else lp[0:4, o:o + 1]
                B.TS(rsm[0:4, rb:rb + 1], Rcol, -1.0, None, ALU.mult, None, ["lp"], ["rsm"])
                B.TS(rsm[0:4, rb + 1:rb + 2], Rcol, -1.0, -math.log(8.0), ALU.mult, ALU.add, ["lp"], ["rsm"])
                B.ACT(thrb[d][0:4, o:o + TW], th[0:4, o:o + TW], AF.Exp, ["th", "rsm"], [("thrb", d)], bias=rsm[0:4, rb:rb + 1])
            if sample:
                j0 = order[d][0]
                rb0 = 4 * (2 * j0 + d)
                B.ACT(rsm[0:4, rb0 + 2:rb0 + 3], self.sbms[:, 2 * l + d:2 * l + d + 1], AF.Exp, ["sbms", "rsm"], ["rsm"], bias=rsm[0:4, rb0:rb0 + 1])
                for i in range(ntile - 1):
                    ja, jb = order[d][i], order[d][i + 1]
                    ra, rbb = 4 * (2 * ja + d), 4 * (2 * jb + d)
                    B.TT(rsm[0:4, ra + 3:ra + 4], rsm[0:4, rbb:rbb + 1], rsm[0:4, ra:ra + 1], ALU.subtract, ["rsm"], ["rsm"])
                    B.ACT(rsm[0:4, ra + 3:ra + 4], rsm[0:4, ra + 3:ra + 4], AF.Exp, ["rsm"], ["rsm"])
            for j in range(ntile):
                o = j * TW
                rb = 4 * (2 * j + d)
                B.ACT(th[0:4, o:o + TW], ig[0:4, o:o + TW], AF.Exp, ["ig", "rsm", ("thrb", d)], ["th"], bias=rsm[0:4, rb + 1:rb + 2])
                for c in range(o // 128, (o + TW) // 128):
                    pb, pk = B.ps("b")
                    B.TR(pb[:, 0:4], th[0:4, c * 128:(c + 1) * 128], self.ident[0:4, 0:4], ["th", "ident"], [pk])
                    B.CP(wcol[:, c, d, :], pb[:, 0:4], [pk], ["wcol"])
        self.barrier()
        self.aoff = p2off
        W = None if Wall is not None else self.carve(8 * 512 // 2, BF16).rearrange("p (k n) -> p k n", n=512)
        KT = self.carve(L // 2, BF16)
        V = self.carve(nch * 2 * 128 // 2, BF16).rearrange("p (c g n) -> p c g n", g=2, n=128)
        QT = [self.carve(256, BF16) for _ in range(2)]
        B.MEMSET(QT[0][64:128, :], 0.0, [("qtb", 0)])
        B.MEMSET(QT[1][0:64, :], 0.0, [("qtb", 1)])
        PT = [self.carve(256, BF16) for _ in range(2)]
        ptkeys = [("pt", 0), ("pt", 1)]
        thrt = self.carve(512, F32)
        thrt2 = self.carve(512, F32)
        thrtb = [(thrt, "thrt"), (thrt2, "thrt2")]
        hf, hb = ctx["tmp"][0], ctx["tmp"][1]
        hfk, hbk = ("tmp", 0), ("tmp", 1)
        nkt = nblk if sample else nch
        ktok_raw = self.carve(nkt * 128 // 2, BF16)
        ktok = ktok_raw.rearrange("p (c n) -> p c n", n=128)
        vw = [self.carve(64, BF16)] * 2
        if sample:
            c0s_raw = self.carve(2 * 128, F32)
            c0s = c0s_raw.rearrange("p (d n) -> p d n", d=2)
            PT += [ktok_raw, c0s_raw.bitcast(BF16)]
            ptkeys += ["ktok", "c0s"]
            Sst = self.carve(128, F32)
            Slh = self.carve(2 * ntile * 64, BF16).rearrange("p (d j n) -> p d j n", d=2, j=ntile)
        else:
            PT += [self.carve(256, BF16) for _ in range(2)]
            ptkeys += [("pt", 2), ("pt", 3)]
            vwb = self.carve(nch * 2 * 2 * 66 // 2, BF16).rearrange("p (c d h n) -> p c d h n", d=2, h=2, n=66)
        gb = self.gc[:, 4 * l + 2:4 * l + 3]
        B.MEMSET(V[:, :, 0, 64:128], 1.0, ["V"])
        B.MEMSET(V[:, :, 1, 0:64], 1.0, ["V"])
        for m in range(2):
            if Wall is not None:
                W = Wall[:, :, QB0 + m * 512:QB0 + (m + 1) * 512]
            else:
                B.DMA("pool", W, wv[:, :, QB0 + m * 512:QB0 + (m + 1) * 512], writes=["wB"])
            for t in range(ntile):
                o = t * TW
                pb, pk = self.proj_fm(W, wBk, 128, 128, U, o, TW)
                B.ACT(KT[:, o:o + TW], pb[:, 0:TW], AF.Copy, [pk], ["KTb"])
            for c in range(nch):
                pb, pk = B.ps("a")
                for k in range(8):
                    B.MM(pb[:, 0:256], U[:, k, c * 128:(c + 1) * 128], W[:, k, 128:384], k == 0, k == 7, [wBk, ("u", k, (c * 128) // TW)], [pk])
                B.CP(V[:, c, 0, 0:64], pb[:, 128:192], [pk], ["V"])
                B.ACT(V[:, c, 1, 64:128], pb[:, 192:256], AF.Copy, [pk], ["V"])
                if not sample:
                    B.ACT(ktok[:, c, :], pb[:, 0:128], AF.Copy, [pk], ["ktok"])
                    for d in range(2):
                        for hl in range(2):
                            hh = 2 * m + hl
                            B.TS(vwb[:, c, d, hl, 0:64], pb[:, 128 + hl * 64:192 + hl * 64], wcol[:, c, d, hh:hh + 1], None, ALU.mult, None, [pk, "wcol"], ["vwb"])
                            B.CP(vwb[:, c, d, hl, 64:65], wcol[:, c, d, hh:hh + 1], ["wcol"], ["vwb"])
            if not sample:
                for d in range(2):
                    for hl in range(2):
                        hh = 2 * m + hl
                        pb, pk = B.ps("b")
                        for c in range(nch):
                            B.MM(pb[0:64, 0:65], ktok[:, c, hl * 64:(hl + 1) * 64], vwb[:, c, d, hl, 0:65], c == 0, c == nch - 1, ["ktok", "vwb"], [pk])
                        B.CP(thrt[0:64, 0:65], pb[0:64, 0:65], [pk], ["thrt"])
                        B.DMA("sp", self.o_bC[seq - 1, l, d, hh], thrt[0:64, 0:64], reads=["thrt"])
                        B.DMA("sp", self.o_bn[seq - 1, l, d, hh].unsqueeze(1), thrt[0:64, 64:65], reads=["thrt"])
            else:
                for d in range(2):
                    B.DMA("sp", c0s[:, d, 0:64], self.sbC[l, d, m], writes=["c0s"])
                    B.DMA("sp", c0s[:, d, 64:65], self.sbn[l, d, m], writes=["c0s"])
                    B.TS(c0s[:, d, 65:128], c0s[:, d, 64:65].to_broadcast([128, 63]), 1.0, None, ALU.mult, None, ["c0s"], ["c0s"])
                    j0 = order[d][0]
                    rb0 = 4 * (2 * j0 + d)
                    pb, pk = B.ps("b")
                    B.MM(pb[:, 0:2], self.selnum(m), rsm[0:4, rb0 + 2:rb0 + 4], True, True, ["selnum", "rsm"], [pk])
                    B.CP(sm[:, 60:61], pb[:, 0:1], [pk], ["e0c"])
                    B.TS(Sst[0:64, :], c0s[0:64, d, :], sm[0:64, 60:61], None, ALU.mult, None, ["c0s", "e0c"], ["Sst"])
                    B.TS(Sst[64:128, 64:128], c0s[64:128, d, 0:64], sm[64:128, 60:61], None, ALU.mult, None, ["c0s", "e0c"], ["Sst"])
                    B.TS(Sst[64:128, 0:64], c0s[64:128, d, 64:128], sm[64:128, 60:61], None, ALU.mult, None, ["c0s", "e0c"], ["Sst"])
                    for i, j in enumerate(order[d]):
                        B.CP(Slh[:, d, j, :], Sst[:, :], ["Sst"], [("Slh", d, j)])
                        if i == ntile - 1:
                            break
                        rb = 4 * (2 * j + d)
                        for ci in range(nblk):
                            c = j * nblk + ci
                            pb, pk = B.ps("a")
                            for k in range(8):
                                B.MM(pb[:, 0:128], U[:, k, c * 128:(c + 1) * 128], W[:, k, 128:256], k == 0, k == 7, [wBk, ("u", k, j)], [pk])
                            B.ACT(ktok[:, ci, :], pb[:, 0:128], AF.Copy, [pk], ["ktok"])
                        pst, pstk = B.ps("c")
                        vi = 0
                        for hl in range(2):
                            hh = 2 * m + hl
                            for ci in range(nblk):
                                c = j * nblk + ci
                                v_ = vw[0]
                                vk = ("vw", 0)
                                vi += 1
                                B.TS(v_[:, :], V[:, c, hl, :], wcol[:, c, d, hh:hh + 1], None, ALU.mult, None, ["V", "wcol"], [vk])
                                B.MM(pst[:, hl * 128:(hl + 1) * 128], ktok[:, ci, :], v_[:, :], ci == 0, ci == nblk - 1, ["ktok", vk], [pstk])
                        pf, pfk = B.ps("b")
                        B.MM(pf[:, 0:2], self.selnum(m), rsm[0:4, rb + 2:rb + 4], True, True, ["selnum", "rsm"], [pfk])
                        B.CP(sm[:, 61:62], pf[:, 1:2], [pfk], ["facc"])
                        for hl in range(2):
                            ps_ = slice(hl * 64, hl * 64 + 64)
                            B.TT(Sst[ps_, :], Sst[ps_, :], pst[ps_, hl * 128:(hl + 1) * 128], ALU.add, ["Sst", pstk], ["Sst"])
                        B.TS(Sst[:, :], Sst[:, :], sm[:, 61:62], None, ALU.mult, None, ["Sst", "facc"], ["Sst"])
            for t in range(ntile):
                o = t * TW
                pb, pk = self.proj_fm(W, wBk, 0, 128, U, o, TW)
                B.ACT(QT[0][0:64, 0:TW], pb[0:64, 0:TW], AF.Copy, [pk], [("qtb", 0)])
                B.CP(QT[1][64:128, 0:TW], pb[64:128, 0:TW], [pk], [("qtb", 1)])
                chs = list(range(t * nblk, (t + 1) * nblk)) if sample else list(range(nch))
                accs = {}
                for hl in range(2):
                    hh = 2 * m + hl
                    accf, acfk = B.ps("c")
                    accb, acbk = B.ps("c")
                    accs[hl] = (accf, acfk, accb, acbk)
                    first = {0: True, 1: True}
                    if sample:
                        for d, (acc, ak) in enumerate(((accf, acfk), (accb, acbk))):
                            B.MM(acc[:, 0:TW], Slh[:, d, t, :], QT[hl][:, 0:TW], True, False, [("Slh", d, t), ("qtb", hl)], [ak])
                        first = {0: False, 1: False}
                    pti = 0
                    pend = []
                    for c in chs + [None]:
                        if c is not None:
                            sp_, spk = B.ps("a")
                            B.MM(sp_[:, 0:TW], KT[:, c * 128:(c + 1) * 128], QT[hl][:, 0:TW], True, True, ["KTb", ("qtb", hl)], [spk])
                            rel = c - t * nblk
                            for d, acc, ak, mk, mkk in ((0, accf, acfk, self.mkf, "mkf_b"), (1, accb, acbk, self.mkb, "mkb_b")):
                                pt = PT[pti % 4]
                                ptk = ptkeys[pti % 4]
                                pti += 1
                                wc = wcol[:, c, d, hh:hh + 1]
                                mo = 384 - 128 * rel
                                lo, hi = 0, TW
                                if sample:
                                    lo, hi = (128 * rel, TW) if d == 0 else (0, 128 * (rel + 1))
                                B.STT(pt[:, lo:hi], sp_[:, lo:hi], wc, mk[:, mo + lo:mo + hi], ALU.mult, ALU.mult, [spk, "wcol", mkk], [ptk])
                                pend.append((c, d, acc, ak, pt, ptk, lo, hi))
                        while pend and (c is None or pend[0][0] < c):
                            cc, d, acc, ak, pt, ptk, lo, hi = pend.pop(0)
                            B.MM(acc[:, lo:hi], V[:, cc, hl, :], pt[:, lo:hi], first[d], cc == chs[-1], ["V", ptk], [ak])
                            first[d] = False
                thrp = {}
                for d in range(2):
                    pbt, pbk = B.ps("b")
                    B.MM(pbt[:, 0:TW], seldb[0:4, m * 128:(m + 1) * 128], thrb[d][0:4, o:o + TW], True, True, ["seldb", ("thrb", d)], [pbk])
                    thrp[d] = (pbt, pbk)
                for hl in range(2):
                    hs = slice(hl * 64, hl * 64 + 64)
                    ds = slice((1 - hl) * 64, (1 - hl) * 64 + 64)
                    accf, acfk, accb, acbk = accs[hl]
                    chains = ((accf, acfk, hf, hfk), (accb, acbk, hb, hbk))
                    for d, (acc, ak, dst, dk) in enumerate(chains):
                        tb, tk = thrtb[d]
                        pbt, pbk = thrp[d]
                        B.ACT(tb[ds, 0:TW], pbt[ds, 0:TW], AF.Copy, [pbk], [tk])
                    for d, (acc, ak, dst, dk) in enumerate(chains):
                        tb, tk = thrtb[d]
                        B.TT(tb[ds, 0:TW], acc[ds, 0:TW], tb[ds, 0:TW], ALU.max, [ak, tk], [tk])
                        B.STT(tb[ds, 0:TW], acc[ds, 0:TW], -1.0, tb[ds, 0:TW], ALU.mult, ALU.max, [ak, tk], [tk])
                    for d, (acc, ak, dst, dk) in enumerate(chains):
                        tb, tk = thrtb[d]
                        B.ACT(tb[ds, 0:TW], tb[ds, 0:TW], AF.Ln, [tk], [tk])
                        B.ACT(tb[ds, 0:TW], tb[ds, 0:TW], AF.Exp, [tk], [tk], scale=-1.0)
                    for d, (acc, ak, dst, dk) in enumerate(chains):
                        tb, tk = thrtb[d]
                        B.CP(tb[hs, 0:TW], tb[ds, 0:TW], [tk], [tk])
                        B.TT(dst[hs, 0:TW], acc[hs, 0:TW], tb[hs, 0:TW], ALU.mult, [ak, tk], [dk])
                    B.TT(hf[hs, 0:TW], hf[hs, 0:TW], hb[hs, 0:TW], ALU.add, [hfk, hbk], [hfk])
                self.headnorm(hf[:, 0:TW], hfk, TW, gb, [(hb[:, 0:TW], [hbk])], ctx)
                pb, pk = self.proj_fm(W, wBk, 384, 128, U, o, TW)
                B.ACT(thrt[:, 0:TW], pb[:, 0:TW], AF.Sigmoid, [pk], ["thrt"])
                B.TT(Y[:, 4 + m, o:o + TW], hb[:, 0:TW], thrt[:, 0:TW], ALU.mult, [hbk, "thrt"], [("y", 4 + m)])
        self.pspools = saved_pools
        self.psidx = {}

    def selnum(self, m):
        return self.selnum_t[:, m * 128:(m + 1) * 128]


_CACHE = {}


def _get_nc():
    if "nc" not in _CACHE:
        b = Builder()
        _CACHE["nc"] = b.build()
    return _CACHE["nc"]


def kernel(x_prompt, x_sample, cache_a_k, cache_a_v, cache_c_k, cache_c_v, state_b_C, state_b_n,
           state_b_m, c, c_ctx, w_ada, b_ada, g_norm, w_ff_in, w_ff_out, w_in, w_out, g_qa, g_ka,
           b_gates, g_b, lam_q1, lam_k1, lam_q2, lam_k2, g_c, g_final):
    f = lambda a: np.ascontiguousarray(np.asarray(a, dtype=np.float32))
    x_prompt, x_sample = f(x_prompt), f(x_sample)
    consts = _consts()
    shared = {}
    shared["w_ada"] = f(w_ada)
    shared["b_ada"] = f(np.asarray(b_ada).reshape(2, 72, 128).transpose(0, 2, 1))
    shared["g_norm"] = f(np.asarray(g_norm).reshape(6, 8, 128).transpose(2, 0, 1).reshape(128, 48))
    shared["g_final"] = f(np.asarray(g_final).reshape(8, 128).T)
    shared["w_ff_in"] = f(w_ff_in)
    shared["w_ff_out"] = f(w_ff_out)
    shared["w_in"] = f(np.stack([_perm_w_in(np.asarray(w_in[l])) for l in range(2)]))
    shared["w_out"] = f(np.stack([_perm_w_out(np.asarray(w_out[l])) for l in range(2)]))
    gcols = np.zeros((128, 8), np.float32)
    for l in range(2):
        for i, g in enumerate((g_qa, g_ka, g_b, g_c)):
            gcols[:, 4 * l + i] = np.tile(np.asarray(g[l]), 2)
    shared["gcols"] = gcols
    bgv = np.zeros((4, 8), np.float32)
    for l in range(2):
        bgv[:, 4 * l:4 * l + 4] = np.asarray(b_gates[l]).reshape(4, 4).T
    shared["bg"] = bgv
    lamv = np.zeros((128, 256), np.float32)
    for l in range(2):
        for i, v in enumerate((lam_q1, lam_k1, lam_q2, lam_k2)):
            lamv[:, (l * 4 + i) * 32:(l * 4 + i + 1) * 32] = np.asarray(v[l])[None, :]
    shared["lamv"] = lamv
    for k, v in consts.items():
        shared["c_" + k] = f(v)
    in_maps = []
    for core in range(NCORES):
        b = SAMPLE_OF_CORE[core]
        real = b is not None
        b = 0 if b is None else b
        zl = (lambda a: np.zeros_like(a)) if not real else (lambda a: a)
        m = dict(shared)
        m["xin"] = f(np.concatenate([zl(x_sample[b]), x_prompt[2 * core], x_prompt[2 * core + 1]], axis=0))
        cv = np.stack([np.asarray(c[b]), np.asarray(c_ctx)], axis=-1)
        m["cvec"] = f(cv.reshape(8, 128, 2).transpose(1, 0, 2))
        m["cak"] = f(zl(np.asarray(cache_a_k[b])).reshape(2, PAST, 128))
        m["cav"] = f(zl(np.asarray(cache_a_v[b])).reshape(2, PAST, 128))
        m["cck"] = f(zl(np.asarray(cache_c_k[b])).reshape(2, PAST, 256))
        m["ccv"] = f(zl(np.asarray(cache_c_v[b])).reshape(2, PAST, 256))
        sC = zl(np.asarray(state_b_C[b]))
        m["sbC"] = f(sC.reshape(2, 2, 2, 2, 64, 64).reshape(2, 2, 2, 128, 64))
        sn = zl(np.asarray(state_b_n[b]))
        m["sbn"] = f(sn.reshape(2, 2, 2, 128, 1))
        sm_ = zl(np.asarray(state_b_m[b]))
        m["sbm"] = f(sm_.transpose(2, 0, 1).reshape(4, 4))
        in_maps.append(m)
    nc = _get_nc()
    res = run_bass_kernel_spmd(nc, in_maps, core_ids=list(range(NCORES)))
    R = res.results
    y_prompt = np.zeros((16, LP, D), np.float32)
    y_sample = np.zeros((4, LS, D), np.float32)
    nak = np.zeros((16, 2, LP, 2, 64), np.float32)
    nav = np.zeros((16, 2, LP, 2, 64), np.float32)
    nck = np.zeros((16, 2, LP, 4, 2, 32), np.float32)
    ncv = np.zeros((16, 2, LP, 4, 64), np.float32)
    nbC = np.zeros((16, 2, 2, 4, 64, 64), np.float32)
    nbn = np.zeros((16, 2, 2, 4, 64), np.float32)
    nbm = np.zeros((16, 2, 2, 4), np.float32)
    for core in range(NCORES):
        r = R[core]
        oy = np.asarray(r["o_y"])
        if SAMPLE_OF_CORE[core] is not None:
            y_sample[SAMPLE_OF_CORE[core]] = oy[0:LS]
        for s in range(2):
            bp = 2 * core + s
            y_prompt[bp] = oy[LS + s * LP:LS + (s + 1) * LP]
            nak[bp] = np.asarray(r["o_ak"])[s].reshape(2, LP, 2, 64)
            nav[bp] = np.asarray(r["o_av"])[s].reshape(2, LP, 2, 64)
            nck[bp] = np.asarray(r["o_ck"])[s].reshape(2, LP, 4, 2, 32)
            ncv[bp] = np.asarray(r["o_cv"])[s].reshape(2, LP, 4, 64)
            nbC[bp] = np.asarray(r["o_bC"])[s]
            nbn[bp] = np.asarray(r["o_bn"])[s]
            nbm[bp] = np.asarray(r["o_bm"]).reshape(2, 2, 2, 4)[s]
    return (y_prompt, y_sample, nak, nav, nck, ncv, nbC, nbn, nbm)
```
